# Optimizing a Trainium2 kernel written in Bass

```python
import jax
import jax.numpy as jnp
from jax import lax
import numpy as np

D_MODEL = 1024
BATCH = 4
SEQ = 4096
DEPTH = 2
DEC_BATCH = 128
DEC_SEQ = 1
PAST_LEN = 2048
PAGE_SIZE = 128

N_EVEN = (DEPTH + 1) // 2
N_ODD = DEPTH // 2
HG_DIM = 128
HG_WIDTH = D_MODEL // 2
HG_HEADS = HG_WIDTH // HG_DIM
HG_CHUNK = 64
FOX_DIM = 64
FOX_WIDTH = D_MODEL - HG_WIDTH
FOX_HEADS = FOX_WIDTH // FOX_DIM
Q_BLOCK = 128
IN0_SIZES = (HG_WIDTH,) * 4 + (FOX_WIDTH,) * 3 + (FOX_HEADS,)
IN0_COLS = sum(IN0_SIZES)
IN0_SPLITS = tuple(sum(IN0_SIZES[:i + 1]) for i in range(len(IN0_SIZES) - 1))
CONV_CH = D_MODEL
CONV_W = 31
FFN_DIM = ((8 * D_MODEL // 3 + 127) // 128) * 128
FFN_CONV_W = 3
EPS = 1e-6

kernel_name = 'hybrid_hgrn2_fox_conformer_step'


def _rmsnorm(x, g):
    x32 = x.astype(jnp.float32)
    y = x32 * lax.rsqrt(jnp.mean(x32 * x32, axis=-1, keepdims=True) + EPS)
    return (y * g.astype(jnp.float32)).astype(x.dtype)


def _layernorm(x, g, b):
    x32 = x.astype(jnp.float32)
    xc = x32 - jnp.mean(x32, axis=-1, keepdims=True)
    y = xc * lax.rsqrt(jnp.mean(xc * xc, axis=-1, keepdims=True) + EPS)
    return y * g.astype(jnp.float32) + b.astype(jnp.float32)


def _dwconv(x_ext, w, b):
    c = x_ext.shape[-1]
    y = lax.conv_general_dilated(x_ext, w.astype(x_ext.dtype)[:, None, :], window_strides=(1,),
                                 padding='VALID', dimension_numbers=('NWC', 'WIO', 'NWC'),
                                 feature_group_count=c)
    return y + b.astype(y.dtype)


def _hgrn2_chunked(q, k, v, logf, s0):
    B, T, H, _ = q.shape
    DV = v.shape[-1]
    n = T // HG_CHUNK
    r = lambda a: a.reshape(B, n, HG_CHUNK, H, a.shape[-1])
    q, k, v, logf = r(q), r(k), r(v), r(logf)
    b = jnp.cumsum(logf, axis=2)
    b_last = b[:, :, -1]
    b_ref = b[:, :, HG_CHUNK // 2][:, :, None]
    kv = jnp.einsum('bnchk,bnchv->bnhkv', k * jnp.exp(b_last[:, :, None] - b), v)

    def step(s, inp):
        dec, kvc = inp
        return jnp.exp(dec)[..., None] * s + kvc, s

    s_fin, s_prev = lax.scan(step, s0, (jnp.moveaxis(b_last, 1, 0), jnp.moveaxis(kv, 1, 0)))
    s_prev = jnp.moveaxis(s_prev, 0, 1)
    o_inter = jnp.einsum('bnchk,bnhkv->bnchv', q * jnp.exp(b), s_prev)
    att = jnp.einsum('bnthk,bnshk->bnhts', q * jnp.exp(b - b_ref), k * jnp.exp(b_ref - b))
    causal = jnp.tril(jnp.ones((HG_CHUNK, HG_CHUNK), dtype=bool))
    att = jnp.where(causal, att, 0.0)
    o_intra = jnp.einsum('bnhts,bnshv->bnthv', att, v)
    return (o_inter + o_intra).reshape(B, T, H, DV), s_fin


def _hgrn2_recurrent(q, k, v, logf, s0):
    def step(s, inp):
        qt, kt, vt, lt = inp
        s = jnp.exp(lt)[..., None] * s + kt[..., None] * vt[..., None, :]
        return s, jnp.einsum('bhk,bhkv->bhv', qt, s)

    s_fin, o = lax.scan(step, s0, tuple(jnp.moveaxis(a, 1, 0) for a in (q, k, v, logf)))
    return jnp.moveaxis(o, 0, 1), s_fin


def _fox_prompt(q, k, v, logf):
    B, T, H, D = q.shape
    nb = T // Q_BLOCK
    scale = FOX_DIM ** -0.5
    c = jnp.cumsum(logf, axis=1).transpose(0, 2, 1)
    qb = jnp.moveaxis(q.reshape(B, nb, Q_BLOCK, H, D), 1, 0)
    cb = jnp.moveaxis(c.reshape(B, H, nb, Q_BLOCK), 2, 0)
    kpos = jnp.arange(T)

    def block(args):
        qi, ci, i = args
        s = jnp.einsum('bqhd,bkhd->bhqk', qi, k).astype(jnp.float32) * scale
        s = s + ci[..., :, None] - c[:, :, None, :]
        qpos = i * Q_BLOCK + jnp.arange(Q_BLOCK)
        s = jnp.where(qpos[:, None] >= kpos[None, :], s, -jnp.inf)
        p = jax.nn.softmax(s, axis=-1)
        return jnp.einsum('bhqk,bkhd->bqhd', p.astype(v.dtype), v)

    o = lax.map(block, (qb, cb, jnp.arange(nb)))
    return jnp.moveaxis(o, 0, 1).reshape(B, T, H, D)


def _fox_sample(q, k, v, logf, k_past, v_past, logf_past):
    scale = FOX_DIM ** -0.5
    n = q.shape[1]
    P = k_past.shape[1]
    cp = jnp.cumsum(logf_past.astype(jnp.float32), axis=1)
    suffix = (cp[:, -1:] - cp).transpose(0, 2, 1)
    cn = jnp.cumsum(logf, axis=1).transpose(0, 2, 1)
    s_past = (jnp.einsum('bqhd,bkhd->bhqk', q, k_past).astype(jnp.float32) * scale
              + cn[..., :, None] + suffix[..., None, :])
    s_new = (jnp.einsum('bqhd,bkhd->bhqk', q, k).astype(jnp.float32) * scale
             + cn[..., :, None] - cn[..., None, :])
    s_new = jnp.where(jnp.tril(jnp.ones((n, n), dtype=bool)), s_new, -jnp.inf)
    p = jax.nn.softmax(jnp.concatenate([s_past, s_new], axis=-1), axis=-1)
    o = (jnp.einsum('bhqk,bkhd->bqhd', p[..., :P].astype(v.dtype), v_past.astype(v.dtype))
         + jnp.einsum('bhqk,bkhd->bqhd', p[..., P:].astype(v.dtype), v))
    return o


def _even_mix(h, w_in, fox_fb, lb, gnorm, w_out, s0, past):
    B, T, _ = h.shape
    f32 = jnp.float32
    hq, hf, hi, hg, fq, fk, fv, ff = jnp.split(h @ w_in, IN0_SPLITS, axis=-1)
    fr = lb + (1.0 - lb) * jax.nn.sigmoid(hf.astype(f32))
    hh = lambda a: a.reshape(B, T, HG_HEADS, HG_DIM)
    q, k, v, lf = hh(jax.nn.silu(hq.astype(f32))), hh(1.0 - fr), hh(hi.astype(f32)), hh(jnp.log(fr))
    fh = lambda a: a.reshape(B, T, FOX_HEADS, FOX_DIM)
    fq, fk, fv = fh(fq), fh(fk), fh(fv)
    flf = jax.nn.log_sigmoid(ff.astype(f32) + fox_fb.astype(f32))
    if past is None:
        o_hg, s_fin = _hgrn2_chunked(q, k, v, lf, s0)
        o_fox = _fox_prompt(fq, fk, fv, flf)
    else:
        o_hg, s_fin = _hgrn2_recurrent(q, k, v, lf, s0)
        o_fox = _fox_sample(fq, fk, fv, flf, past[0], past[1], past[2])
    o_hg = o_hg * lax.rsqrt(jnp.mean(o_hg * o_hg, axis=-1, keepdims=True) + EPS)
    o_hg = o_hg.reshape(B, T, HG_WIDTH) * gnorm.astype(f32) * jax.nn.silu(hg.astype(f32))
    cat = jnp.concatenate([o_hg.astype(h.dtype), o_fox.reshape(B, T, FOX_WIDTH).astype(h.dtype)], axis=-1)
    return cat @ w_out, fk, fv, flf, s_fin


def _conformer_conv(h, hist, w_pw1, b_pw1, w_dw, b_dw, ln_g, ln_b, w_pw2, b_pw2):
    u = h @ w_pw1 + b_pw1
    a, g = jnp.split(u, 2, axis=-1)
    u = a * jax.nn.sigmoid(g)
    ext = jnp.concatenate([hist.astype(u.dtype), u], axis=1)
    d = _layernorm(_dwconv(ext, w_dw, b_dw), ln_g, ln_b)
    out = jax.nn.silu(d).astype(h.dtype) @ w_pw2 + b_pw2
    return out, ext[:, -(CONV_W - 1):]


def _conv_ffn(h, hist, w_in, w_dw, b_dw, w_out):
    g, u = jnp.split(h @ w_in, 2, axis=-1)
    ext = jnp.concatenate([hist.astype(g.dtype), g], axis=1)
    gc = _dwconv(ext, w_dw, b_dw)
    out = (jax.nn.gelu(gc.astype(jnp.float32), approximate=False) * u.astype(jnp.float32)).astype(h.dtype) @ w_out
    return out, ext[:, -(FFN_CONV_W - 1):]


def setup_inputs(seed: int = 0) -> dict:
    key = jax.random.key(seed)
    ks = iter(list(jax.random.split(key, 40)))
    f32 = jnp.float32
    nrm = lambda shape, scale: jax.random.normal(next(ks), shape, f32) * scale
    gain = lambda shape: 1.0 + 0.02 * jax.random.normal(next(ks), shape, f32)
    n_pages = PAST_LEN // PAGE_SIZE
    n_pool = (DEC_BATCH * n_pages * 5) // 4
    page_table = jax.random.permutation(next(ks), n_pool)[:DEC_BATCH * n_pages]
    page_table = page_table.reshape(DEC_BATCH, n_pages).astype(jnp.int32)
    logf_cache = jax.nn.log_sigmoid(
        jax.random.uniform(next(ks), (N_EVEN, n_pool, PAGE_SIZE, FOX_HEADS), f32, 1.0, 7.0))
    return {
        'x_prompt': nrm((BATCH, SEQ, D_MODEL), 1.0),
        'x_sample': nrm((DEC_BATCH, DEC_SEQ, D_MODEL), 1.0),
        'cache_fox_k': nrm((N_EVEN, n_pool, PAGE_SIZE, FOX_HEADS, FOX_DIM), 1.0),
        'cache_fox_v': nrm((N_EVEN, n_pool, PAGE_SIZE, FOX_HEADS, FOX_DIM), 1.0),
        'cache_fox_logf': logf_cache,
        'page_table': page_table,
        'state_hgrn': nrm((N_EVEN, DEC_BATCH, HG_HEADS, HG_DIM, HG_DIM), 1.0),
        'state_conv': nrm((N_ODD, DEC_BATCH, CONV_W - 1, CONV_CH), 0.5),
        'state_ffn_conv': nrm((DEPTH, DEC_BATCH, FFN_CONV_W - 1, FFN_DIM), 1.0),
        'norm_mix': gain((DEPTH, D_MODEL)),
        'norm_ffn': gain((DEPTH, D_MODEL)),
        'norm_final': gain((D_MODEL,)),
        'w_in0': nrm((N_EVEN, D_MODEL, IN0_COLS), D_MODEL ** -0.5),
        'fox_fb': jax.random.uniform(next(ks), (N_EVEN, FOX_HEADS), f32, 1.0, 7.0),
        'hg_lb': nrm((N_EVEN + 1, HG_WIDTH), 0.1),
        'hg_gnorm': gain((N_EVEN, HG_WIDTH)),
        'w_out0': nrm((N_EVEN, HG_WIDTH + FOX_WIDTH, D_MODEL), D_MODEL ** -0.5),
        'w_pw1': nrm((N_ODD, D_MODEL, 2 * CONV_CH), D_MODEL ** -0.5),
        'b_pw1': nrm((N_ODD, 2 * CONV_CH), 0.02),
        'w_dw': nrm((N_ODD, CONV_W, CONV_CH), CONV_W ** -0.5),
        'b_dw': nrm((N_ODD, CONV_CH), 0.02),
        'ln_g': gain((N_ODD, CONV_CH)),
        'ln_b': nrm((N_ODD, CONV_CH), 0.02),
        'w_pw2': nrm((N_ODD, CONV_CH, D_MODEL), CONV_CH ** -0.5),
        'b_pw2': nrm((N_ODD, D_MODEL), 0.02),
        'w_ffn_in': nrm((DEPTH, D_MODEL, 2 * FFN_DIM), D_MODEL ** -0.5),
        'w_ffn_dw': nrm((DEPTH, FFN_CONV_W, FFN_DIM), FFN_CONV_W ** -0.5),
        'b_ffn_dw': nrm((DEPTH, FFN_DIM), 0.02),
        'w_ffn_out': nrm((DEPTH, FFN_DIM, D_MODEL), FFN_DIM ** -0.5),
    }


def reference(x_prompt, x_sample, cache_fox_k, cache_fox_v, cache_fox_logf, page_table,
              state_hgrn, state_conv, state_ffn_conv, norm_mix, norm_ffn, norm_final,
              w_in0, fox_fb, hg_lb, hg_gnorm, w_out0, w_pw1, b_pw1, w_dw, b_dw, ln_g, ln_b,
              w_pw2, b_pw2, w_ffn_in, w_ffn_dw, b_ffn_dw, w_ffn_out):
    f32 = jnp.float32
    xp, xs = x_prompt, x_sample
    bp, bs = xp.shape[0], xs.shape[0]
    lb_all = jnp.cumsum(jax.nn.softmax(hg_lb.astype(f32), axis=0), axis=0)
    fk_p, fv_p, fl_p, fk_s, fv_s, fl_s = [], [], [], [], [], []
    hs_p, hs_s, cs_p, cs_s, ffs_p, ffs_s = [], [], [], [], [], []
    for l in range(DEPTH):
        if l % 2 == 0:
            e = l // 2
            hp = _rmsnorm(xp, norm_mix[l])
            out, fk, fv, flf, s_fin = _even_mix(hp, w_in0[e], fox_fb[e], lb_all[e], hg_gnorm[e], w_out0[e],
                                                jnp.zeros((bp, HG_HEADS, HG_DIM, HG_DIM), f32), None)
            xp = xp + out.astype(xp.dtype)
            fk_p.append(fk); fv_p.append(fv); fl_p.append(flf); hs_p.append(s_fin)
            kp = cache_fox_k[e][page_table].reshape(bs, -1, FOX_HEADS, FOX_DIM)
            vp = cache_fox_v[e][page_table].reshape(bs, -1, FOX_HEADS, FOX_DIM)
            lp = cache_fox_logf[e][page_table].reshape(bs, -1, FOX_HEADS)
            hsm = _rmsnorm(xs, norm_mix[l])
            out, fk, fv, flf, s_fin = _even_mix(hsm, w_in0[e], fox_fb[e], lb_all[e], hg_gnorm[e], w_out0[e],
                                                state_hgrn[e].astype(f32), (kp, vp, lp))
            xs = xs + out.astype(xs.dtype)
            fk_s.append(fk); fv_s.append(fv); fl_s.append(flf); hs_s.append(s_fin)
        else:
            o = l // 2
            hp = _rmsnorm(xp, norm_mix[l])
            out, hist = _conformer_conv(hp, jnp.zeros((bp, CONV_W - 1, CONV_CH), hp.dtype), w_pw1[o], b_pw1[o],
                                        w_dw[o], b_dw[o], ln_g[o], ln_b[o], w_pw2[o], b_pw2[o])
            xp = xp + out.astype(xp.dtype)
            cs_p.append(hist)
            hsm = _rmsnorm(xs, norm_mix[l])
            out, hist = _conformer_conv(hsm, state_conv[o], w_pw1[o], b_pw1[o], w_dw[o], b_dw[o],
                                        ln_g[o], ln_b[o], w_pw2[o], b_pw2[o])
            xs = xs + out.astype(xs.dtype)
            cs_s.append(hist)
        hp = _rmsnorm(xp, norm_ffn[l])
        out, hist = _conv_ffn(hp, jnp.zeros((bp, FFN_CONV_W - 1, FFN_DIM), hp.dtype),
                              w_ffn_in[l], w_ffn_dw[l], b_ffn_dw[l], w_ffn_out[l])
        xp = xp + out.astype(xp.dtype)
        ffs_p.append(hist)
        hsm = _rmsnorm(xs, norm_ffn[l])
        out, hist = _conv_ffn(hsm, state_ffn_conv[l], w_ffn_in[l], w_ffn_dw[l], b_ffn_dw[l], w_ffn_out[l])
        xs = xs + out.astype(xs.dtype)
        ffs_s.append(hist)
    y_prompt = _rmsnorm(xp, norm_final)
    y_sample = _rmsnorm(xs, norm_final)
    new_fox_k_prompt = jnp.stack(fk_p)
    new_fox_v_prompt = jnp.stack(fv_p)
    new_fox_logf_prompt = jnp.stack(fl_p)
    new_fox_k_sample = jnp.stack(fk_s)
    new_fox_v_sample = jnp.stack(fv_s)
    new_fox_logf_sample = jnp.stack(fl_s)
    new_hgrn_prompt = jnp.stack(hs_p)
    new_hgrn_sample = jnp.stack(hs_s)
    new_conv_prompt = jnp.stack(cs_p)
    new_conv_sample = jnp.stack(cs_s)
    new_ffn_conv_prompt = jnp.stack(ffs_p)
    new_ffn_conv_sample = jnp.stack(ffs_s)
    return (y_prompt, y_sample, new_fox_k_prompt, new_fox_v_prompt, new_fox_logf_prompt,
            new_fox_k_sample, new_fox_v_sample, new_fox_logf_sample, new_hgrn_prompt, new_hgrn_sample,
            new_conv_prompt, new_conv_sample, new_ffn_conv_prompt, new_ffn_conv_sample)
```

```python
import numpy as np
from contextlib import ExitStack
import concourse.bass as bass
import concourse.mybir as mybir
from concourse.bass_utils import run_bass_kernel_spmd

F32, BF16, I32 = mybir.dt.float32, mybir.dt.bfloat16, mybir.dt.int32
AF = mybir.ActivationFunctionType
ALU = mybir.AluOpType
AX = mybir.AxisListType

D = 1024
SEQ = 4096
NCORE = 8
NB = 16
NPOOL = 2560
FFN = 2816
NJ = 22
IN0 = 3592
EPS = 1e-6
NCHUNK = 512
RING = 4096


def _keys(lst):
    out = []
    for a in lst:
        if a is None:
            continue
        out.append(a if isinstance(a, str) else getattr(a, 'tensor', a).name)
    return out


class Bld:
    def __init__(self, nc, es):
        self.nc = nc
        self.es = es
        self.E = {'pe': nc.tensor, 'act': nc.scalar, 'dve': nc.vector, 'pool': nc.gpsimd, 'sp': nc.sync}
        self.sem = {e: es.enter_context(nc.semaphore('s_' + e)) for e in self.E}
        self.cnt = {e: 0 for e in self.E}
        self.waited = {e: {} for e in self.E}
        self.NDS = 48
        self.dsem = [es.enter_context(nc.semaphore('d%d' % i)) for i in range(self.NDS)]
        self.dval = [0] * self.NDS
        self.dnext = 0
        self.res = {}
        self.out_deps = []
        self.ntens = 0

    def sb(self, shape, dt, name=None):
        self.ntens += 1
        return self.es.enter_context(self.nc.sbuf_tensor(name or ('t%d' % self.ntens), list(shape), dt))

    def psum(self, name):
        return self.es.enter_context(self.nc.psum_tensor(name, [128, 512], F32))

    def _semh(self, key):
        return self.sem[key] if isinstance(key, str) else self.dsem[key]

    def _wait(self, eng, dep):
        key, val = dep
        if key == 'pe' and eng == 'pe':
            return
        if self.waited[eng].get(key, 0) >= val:
            return
        self.E[eng].wait_ge(self._semh(key), val)
        self.waited[eng][key] = val

    def _deps(self, eng, r, w):
        for k in r:
            e = self.res.get(k)
            if e and e[0]:
                self._wait(eng, e[0])
        for k in w:
            e = self.res.get(k)
            if e:
                if e[0]:
                    self._wait(eng, e[0])
                for kk, vv in e[1].items():
                    self._wait(eng, (kk, vv))

    def _commit(self, me, r, w):
        for k in r:
            e = self.res.setdefault(k, [None, {}])
            e[1][me[0]] = max(e[1].get(me[0], 0), me[1])
        for k in w:
            self.res[k] = [me, {}]

    def op(self, eng, fn, r, w):
        r = _keys(r)
        w = _keys(w)
        self._deps(eng, r, w)
        ins = fn(self.E[eng])
        self.cnt[eng] += 1
        ins.then_inc(self.sem[eng], 1)
        self._commit((eng, self.cnt[eng]), r, w)

    def dma(self, q, fn, r, w, is_out=False):
        r = _keys(r)
        w = _keys(w)
        self._deps(q, r, w)
        i = self.dnext
        self.dnext = (i + 1) % self.NDS
        if self.dval[i] > 0:
            self._wait(q, (i, self.dval[i]))
        ins = fn(self.E[q])
        self.dval[i] += 16
        ins.then_inc(self.dsem[i], 16)
        me = (i, self.dval[i])
        self._commit(me, r, w)
        if is_out:
            self.out_deps.append(me)

    def finish(self):
        for dep in self.out_deps:
            self._wait('sp', dep)

    def mm(self, out, lhsT, rhs, start=True, stop=True, extra_r=()):
        self.op('pe', lambda e: e.matmul(out, lhsT=lhsT, rhs=rhs, start=start, stop=stop),
                [lhsT, rhs] + list(extra_r), [out])

    def tr(self, out, in_, ident):
        self.op('pe', lambda e: e.transpose(out, in_, ident), [in_, ident], [out])

    def act(self, out, in_, func, bias=None, scale=None, eng='act'):
        kw = {}
        rr = [in_]
        if bias is not None:
            kw['bias'] = bias
            if not isinstance(bias, (int, float)):
                rr.append(bias)
        if scale is not None:
            kw['scale'] = scale
            if not isinstance(scale, (int, float)):
                rr.append(scale)
        self.op('act', lambda e: e.activation(out=out, in_=in_, func=func, **kw), rr, [out])

    def copy(self, out, in_, eng='act'):
        if eng == 'act':
            self.op('act', lambda e: e.copy(out=out, in_=in_), [in_], [out])
        else:
            self.op(eng, lambda e: e.tensor_copy(out=out, in_=in_), [in_], [out])

    def tt(self, out, in0, in1, op, eng='dve'):
        self.op(eng, lambda e: e.tensor_tensor(out=out, in0=in0, in1=in1, op=op), [in0, in1], [out])

    def ts(self, out, in0, s1, s2, op0, op1=None, eng='dve'):
        rr = [in0] + [s for s in (s1, s2) if s is not None and not isinstance(s, (int, float))]
        if op1 is None:
            self.op(eng, lambda e: e.tensor_scalar(out=out, in0=in0, scalar1=s1, scalar2=None, op0=op0), rr, [out])
        else:
            self.op(eng, lambda e: e.tensor_scalar(out=out, in0=in0, scalar1=s1, scalar2=s2, op0=op0, op1=op1), rr, [out])

    def stt(self, out, in0, scalar, in1, op0, op1, eng='dve'):
        rr = [in0, in1] + ([] if isinstance(scalar, (int, float)) else [scalar])
        self.op(eng, lambda e: e.scalar_tensor_tensor(out=out, in0=in0, scalar=scalar, in1=in1, op0=op0, op1=op1),
                rr, [out])

    def memset(self, ap, val, eng='dve'):
        self.op(eng, lambda e: e.memset(ap, val), [], [ap])


def _seg2(ap2d, off, stride, n=64):
    pst = ap2d.ap[0][0]
    npart = ap2d.ap[0][1]
    return bass.AP(ap2d.tensor, ap2d.offset + off, [[pst, npart], [stride, 2], [1, n]])


def build_program(nch=SEQ // NCHUNK, do_sample=True, dbg_chunk=None, npool=NPOOL):
    nc = bass.Bass("TRN2", target_bir_lowering=False)

    def din(name, shape, dt=F32):
        return nc.dram_tensor(name, list(shape), dt, kind="ExternalInput").ap()

    def dout(name, shape, dt=F32):
        return nc.dram_tensor(name, list(shape), dt, kind="ExternalOutput").ap()

    def dscr(name, shape, dt=BF16):
        return nc.dram_tensor(name, list(shape), dt, kind="Internal").ap()

    xp = din("xp", [SEQ, D])
    xs = din("xs", [NB, D])
    ck = din("ck", [npool * 128, 512])
    cv = din("cv", [npool * 128, 512])
    cl = din("cl", [npool * 128, 8])
    ptab = din("ptab", [128, NB * 16], I32)
    st_hg = din("st_hg", [NB, 4, 128, 128])
    st_cv = din("st_cv", [NB, 30, D])
    st_ff = din("st_ff", [2, NB, 2, FFN])
    pvec_d = din("pvec", [128, 640])
    cst_d = din("cst", [128, 1024])
    cflag_d = din("cflag", [128, 72])
    wdw_rep = din("wdw_rep", [120, D])
    w_in0 = din("w_in0", [D, IN0])
    w_out0 = din("w_out0", [D, D])
    w_pw1 = din("w_pw1", [D, 2 * D])
    w_pw2 = din("w_pw2", [D, D])
    w_ffi = din("w_ffi", [2, D, 2 * FFN])
    w_ffo = din("w_ffo", [2, FFN, D])
    OWN0 = 4 * NCHUNK
    o_y = dout("o_y", [SEQ - OWN0, D])
    o_ys = dout("o_ys", [NB, D])
    o_fk = dout("o_fk", [SEQ - OWN0, 512])
    o_fv = dout("o_fv", [SEQ - OWN0, 512])
    o_fl = dout("o_fl", [SEQ - OWN0, 8])
    o_fks = dout("o_fks", [NB, 512])
    o_fvs = dout("o_fvs", [NB, 512])
    o_fls = dout("o_fls", [NB, 8])
    o_hg = dout("o_hg", [4, 128, 128])
    o_hgs = dout("o_hgs", [NB, 4, 128, 128])
    o_cv = dout("o_cv", [30, D])
    o_cvs = dout("o_cvs", [NB, 30, D])
    o_ff = dout("o_ff", [2, 2, FFN])
    o_ffs = dout("o_ffs", [2, NB, 2, FFN])
    o_dbg = dout("o_dbg", [8, 8, 128, 512]) if dbg_chunk is not None else None
    s_in0 = dscr("s_in0", [D, IN0])
    s_out0 = dscr("s_out0", [D, D])
    s_pw1 = dscr("s_pw1", [D, 2 * D])
    s_pw2 = dscr("s_pw2", [D, D])
    s_ffi = dscr("s_ffi", [2, D, 2 * FFN])
    s_ffo = dscr("s_ffo", [2, FFN, D])
    s_dcv = dscr("s_dcv", [8, 128, 31 * 128])
    s_k = dscr("s_k", [8, 64, SEQ])
    s_v = dscr("s_v", [8, 128, 32, 64])

    es = ExitStack()
    with es:
        B = Bld(nc, es)
        pv = B.sb([128, 640], F32, "pv_sb")
        cst = B.sb([128, 1024], F32, "cst_sb")
        cflag = B.sb([128, 72], F32, "cflag_sb")
        ident = cst[:, 0:128]
        onesf = cst[:, 128:256]
        tri_le = cst[:, 256:384]
        mask2f = cst[:, 384:512]
        tri_gt = cst[:, 512:640]
        bmask = cst[0:8, 640:1152 - 128] if False else None
        iota_p = cst[:, 640:641]
        epsc = cst[:, 641:642]
        sel4 = cst[0:120, 648:652]
        sel65 = cst[0:65, 656:720]
        cb = B.sb([128, 768], BF16, "cb")
        ident_b = cb[:, 0:128]
        onesm_b = cb[:, 128:256]
        ones128_b = cb[:, 256:384]
        tri_le_b = cb[:, 384:512]
        mask2_b = cb[:, 512:640]
        ones_b = cb[:, 640:768]
        wff = B.sb([128, 8, 8], BF16, "wff")
        xT = [B.sb([128, 512], F32, "xT%d" % i) for i in range(8)]
        hT = [B.sb([128, 512], BF16, "hT%d" % i) for i in range(8)]
        aT = [B.sb([128, 512], BF16, "aT%d" % i) for i in range(NJ)]
        sq = aT[0:8]
        rstd = B.sb([128, 512], F32, "rstd")
        ring = [B.sb([128, RING], BF16, "ring%d" % i) for i in range(4)]
        xtm = [B.sb([128, D], F32, "xtm%d" % i) for i in range(2)]
        PS = [B.psum("ps%d" % i) for i in range(8)]
        psrr = [0]

        def ps_mm():
            i = psrr[0] % 4
            psrr[0] += 1
            return PS[i]

        class WS:
            def __init__(self):
                self.n = 0
                self.loaded = 0
                self.plan = []

            def add(self, tag, src, npart, shape):
                nm = src.tensor.name
                if tag.startswith("dcv"):
                    deps = ["s_dcv#%s" % tag[3:]]
                else:
                    deps = CAST_KEYS[nm + ("#%d" % int(tag[3]) if nm in ("s_ffi", "s_ffo") else "")]
                self.plan.append((tag, src, npart, shape, deps))

            def _issue(self, i):
                tag, src, npart, shape, deps = self.plan[i]
                slot = ring[i % 4]
                sz = int(np.prod(shape))
                dst = slot[0:npart, 0:sz]
                if len(shape) == 2:
                    dst = dst.rearrange("p (a b) -> p a b", b=shape[1])
                B.dma('sp', lambda e: e.dma_start(out=dst, in_=src), deps, [slot])

            def get(self, tag):
                i = self.n
                assert self.plan[i][0] == tag, (self.plan[i][0], tag)
                while self.loaded < min(len(self.plan), i + 2):
                    self._issue(self.loaded)
                    self.loaded += 1
                self.n += 1
                _, src, npart, shape, _d = self.plan[i]
                sz = int(np.prod(shape))
                v = ring[i % 4][0:npart, 0:sz]
                if len(shape) == 2:
                    v = v.rearrange("p (a b) -> p a b", b=shape[1])
                return v

        CAST_KEYS = {}

        def _ck(name, rows, rstep=256):
            CAST_KEYS[name] = ["%s#r%d" % (name, r0) for r0 in range(0, rows, rstep)]

        _ck("s_in0", D)
        _ck("s_out0", D)
        _ck("s_pw1", D)
        _ck("s_pw2", D)
        for l_ in range(2):
            _ck("s_ffi#%d" % l_, D)
            _ck("s_ffo#%d" % l_, FFN)
        W = WS()
        in0v = s_in0.rearrange("(c p) n -> p c n", p=128)
        pw1v = s_pw1.rearrange("(c p) n -> p c n", p=128)
        pw2v = s_pw2.rearrange("(c p) n -> p c n", p=128)

        def plan_chunk(sample, mode='own'):
            if mode == 'partial':
                for i in (1, 2, 5, 6):
                    W.add("in%d" % i, in0v[:, :, i * 512:(i + 1) * 512], 128, [8, 512])
                return
            for i in range(7):
                W.add("in%d" % i, in0v[:, :, i * 512:(i + 1) * 512], 128, [8, 512])
            W.add("wo_a", s_out0[0:512, :].rearrange("(c p) n -> p c n", p=128), 128, [4, 1024])
            if sample:
                W.add("wo_c", s_out0[512:1024, :].rearrange("(c p) n -> p c n", p=128), 128, [4, 1024])
            else:
                for hh in range(2):
                    W.add("wo_b%d" % hh, s_out0[512:1024, hh * 512:(hh + 1) * 512].rearrange("(h p) n -> p h n", p=64),
                          64, [8, 512])
            plan_ffn(0)
            for i in range(2):
                W.add("pw1a%d" % i, pw1v[:, :, i * 512:(i + 1) * 512], 128, [8, 512])
                W.add("pw1g%d" % i, pw1v[:, :, 1024 + i * 512:1024 + (i + 1) * 512], 128, [8, 512])
            for c in range(8):
                W.add("dcv%d" % c, s_dcv[c], 128, [31 * 128])
            for i in range(2):
                W.add("pw2_%d" % i, pw2v[:, :, i * 512:(i + 1) * 512], 128, [8, 512])
            plan_ffn(1)

        def plan_ffn(l):
            fi = s_ffi[l].rearrange("(c p) n -> p c n", p=128)
            fo = s_ffo[l].rearrange("(j p) n -> p j n", p=128)
            for g in range(6):
                wd = 512 if g < 5 else 256
                W.add("ffg%d_%d" % (l, g), fi[:, :, g * 512:g * 512 + wd], 128, [8, wd])
                W.add("ffu%d_%d" % (l, g), fi[:, :, FFN + g * 512:FFN + g * 512 + wd], 128, [8, wd])
            for m in range(8):
                W.add("ffo%d_%d" % (l, m), fo[:, :, m * 128:(m + 1) * 128], 128, [NJ, 128])

        NCH = nch
        def chunk_mode(ci):
            if nch < 8:
                return 'own'
            return 'partial' if ci < 3 else ('halo' if ci == 3 else 'own')

        for c in range(NCH):
            plan_chunk(False, chunk_mode(c))
        if do_sample:
            plan_chunk(True)

        B.dma('sp', lambda e: e.dma_start(out=pv[:, :], in_=pvec_d), [], [pv])
        B.dma('sp', lambda e: e.dma_start(out=cst[:, :], in_=cst_d), [], [cst])
        B.dma('sp', lambda e: e.dma_start(out=cflag[:, :], in_=cflag_d), [], [cflag])
        B.ts(cb[:, 0:128], ident, 1.0, None, ALU.mult)
        B.ts(cb[:, 128:256], onesf, 1.0 / 1024.0, None, ALU.mult)
        B.ts(cb[:, 256:384], onesf, 1.0 / 128.0, None, ALU.mult)
        B.ts(cb[:, 384:512], tri_le, 1.0, None, ALU.mult)
        B.ts(cb[:, 512:640], mask2f, 1.0, None, ALU.mult)
        B.ts(cb[:, 640:768], onesf, 1.0, None, ALU.mult)

        B.dma('pool', lambda e: e.dma_start(out=wff[:, :, :], in_=w_in0[:, 3584:3592].rearrange("(c p) n -> p c n", p=128)),
              [], [wff])
        PV_NM, PV_NF, PV_NFIN = 0, 16, 32
        PV_LB, PV_GN = 40, 48
        PV_BPW1, PV_BDW, PV_LNG, PV_LNB, PV_BPW2 = 52, 68, 76, 84, 92
        PV_BFF = 100
        PV_WFF = 144
        PV_WDW = 276
        PV_FB = 524
        lbt = B.sb([128, 8], F32, "lbt")
        B.tt(lbt[:, 0:4], pv[:, PV_LB:PV_LB + 4], pv[:, PV_LB + 4:PV_LB + 8], ALU.subtract)
        B.act(lbt[:, 0:4], lbt[:, 0:4], AF.Sigmoid)
        B.ts(lbt[:, 4:8], lbt[:, 0:4], -1.0, 1.0, ALU.mult, ALU.add)

        for c in range(8):
            t = ring[c % 4]
            for j in range(31):
                B.ts(t[:, j * 128:(j + 1) * 128], ident, pv[:, PV_WDW + c * 31 + j:PV_WDW + c * 31 + j + 1], None, ALU.mult)
            B.dma('pool', lambda e, t=t, c=c: e.dma_start(out=s_dcv[c], in_=t[:, 0:31 * 128]), [t], ["s_dcv#%d" % c])

        def cast_w(dst, src, rows, name, rstep=256):
            for r0 in range(0, rows, rstep):
                r1 = min(rows, r0 + rstep)
                B.dma('pool', lambda e, r0=r0, r1=r1: e.dma_start(out=dst[r0:r1, :], in_=src[r0:r1, :]), [],
                      ["%s#r%d" % (name, r0)])

        cast_w(s_in0, w_in0, D, "s_in0")
        cast_w(s_out0, w_out0, D, "s_out0")
        cast_w(s_ffi[0], w_ffi[0], D, "s_ffi#0")
        cast_w(s_ffo[0], w_ffo[0], FFN, "s_ffo#0")
        cast_w(s_pw1, w_pw1, D, "s_pw1")
        cast_w(s_pw2, w_pw2, D, "s_pw2")
        cast_w(s_ffi[1], w_ffi[1], D, "s_ffi#1")
        cast_w(s_ffo[1], w_ffo[1], FFN, "s_ffo#1")

        def norm_sq(m, N):
            B.act(sq2[m][:, :N], xT[m][:, :N], AF.Square)

        def rmsnorm(gcol, N, outs=None, out_f32=False):
            ps = ps_mm()
            for kc in range(8):
                B.mm(ps[:, :N], onesm_b, sq2[kc][:, :N], start=(kc == 0), stop=(kc == 7))
            B.act(rstd[:, :N], ps[:, :N], AF.Ln, bias=epsc)
            B.act(rstd[:, :N], rstd[:, :N], AF.Exp, scale=-0.5)
            for kc in range(8):
                o = xT[kc] if out_f32 else hT[kc]
                B.stt(o[:, :N], xT[kc][:, :N], pv[:, gcol + kc:gcol + kc + 1], rstd[:, :N], ALU.mult, ALU.mult)

        def fm_mm(ps_ap, slab, c0, m, N, src=None):
            src = src or hT
            for kc in range(8):
                B.mm(ps_ap, slab[:, kc, c0:c0 + m], src[kc][:, :N], start=(kc == 0), stop=(kc == 7))

        def tm_mm(ps_ap, slab, c0, ncol, t0, nt):
            for kc in range(8):
                B.mm(ps_ap, hT[kc][:, t0:t0 + nt], slab[:, kc, c0:c0 + ncol], start=(kc == 0), stop=(kc == 7))

        def resid_add(m, ps, N, bias=None):
            if bias is None:
                B.tt(xT[m][:, :N], ps[:, :N], xT[m][:, :N], ALU.add)
            else:
                B.stt(xT[m][:, :N], ps[:, :N], bias, xT[m][:, :N], ALU.add, ALU.add)
            norm_sq(m, N)

        trc = [0]

        def fm_to_dram(srcs, n, dram_rows, q='pool'):
            nck = len(srcs)
            for b0 in range(0, nck, 8):
                st = xtm[trc[0] % 2]
                trc[0] += 1
                bn = min(8, nck - b0)
                for g0 in range(0, bn, 4):
                    ps = ps_mm()
                    gn = min(4, bn - g0)
                    for i in range(gn):
                        B.tr(ps[0:n, i * 128:(i + 1) * 128], srcs[b0 + g0 + i], ident)
                    B.copy(st[0:n, g0 * 128:(g0 + gn) * 128], ps[0:n, 0:gn * 128])
                B.dma(q, lambda e, st=st, b0=b0, bn=bn: e.dma_start(out=dram_rows[:, b0 * 128:(b0 + bn) * 128],
                                                                 in_=st[0:n, 0:bn * 128]),
                      [st], [dram_rows], is_out=True)

        q32 = [B.sb([128, 512], F32, "q32_%d" % h) for h in range(4)]
        frt = [B.sb([128, 512], F32, "fr_%d" % h) for h in range(4)]
        gate = [B.sb([128, 512], BF16, "gate_%d" % h) for h in range(4)]
        vhg = [B.sb([128, 512], BF16, "vhg_%d" % j) for j in range(4)]
        cat_hg = [B.sb([128, 512], BF16, "cathg_%d" % h) for h in range(4)]
        sq2 = gate + vhg
        cat_fx = [B.sb([64, 512], BF16, "catfx_%d" % h) for h in range(8)]
        Qp = [B.sb([65, 512], BF16, "Qp_%d" % h) for h in range(8)]
        S32 = [B.sb([128, 128], F32, "S32_%d" % h) for h in range(4)]
        Sbf = [B.sb([128, 128], BF16, "Sbf_%d" % h) for h in range(4)]
        for h in range(4):
            B.memset(S32[h][:, :], 0.0)
            B.memset(Sbf[h][:, :], 0.0, eng='pool')
        tA = B.sb([128, 512], F32, "tA")
        tB = B.sb([128, 512], F32, "tB")
        tC = B.sb([128, 512], F32, "tC")
        nQt_s = [B.sb([128, 512], BF16, "nQt%d" % i) for i in range(2)]
        nKt_s = [B.sb([128, 512], BF16, "nKt%d" % i) for i in range(2)]
        Qb_s = [B.sb([128, 512], BF16, "Qb%d" % i) for i in range(2)]
        nKtm_s = [[B.sb([128, 128], BF16, "nKtm%d_%d" % (i, j)) for j in range(4)] for i in range(2)]
        csm_s = [B.sb([128, 64], F32, "csm%d" % i) for i in range(2)]
        attm = [B.sb([128, 128], BF16, "attm%d" % j) for j in range(2)]
        kvt = [B.sb([128, 128], F32, "kvt%d" % j) for j in range(2)]
        carry = B.sb([128, 8], F32, "carry")
        cref = B.sb([128, 8], F32, "cref")
        negc = B.sb([128, 32, 8], F32, "negc")
        biasc = B.sb([128, 32, 8], F32, "biasc")
        flf = [B.sb([128, 8], F32, "flf%d" % j) for j in range(4)]
        ctm = [B.sb([128, 8], F32, "ctm%d" % j) for j in range(2)]
        Zr = [B.sb([128, 8, 65], BF16, "Zr%d" % j) for j in range(4)]
        for j in range(4):
            B.memset(Zr[j][:, :, :], 0.0, eng='pool')
        B.memset(carry[:, :], 0.0)
        kst = [B.sb([64, 512], BF16, "kst%d" % i) for i in range(2)]
        vst = [B.sb([128, 512], BF16, "vst%d" % i) for i in range(2)]
        ost = [B.sb([128, 512], F32, "ost%d" % i) for i in range(2)]
        ostc = [0]
        Kbuf = [B.sb([65, SEQ], BF16, "Kbuf%d" % i) for i in range(1)]
        Vbuf = [B.sb([128, 32 * 65], BF16, "Vbuf%d" % i) for i in range(1)]
        for i in range(1):
            B.memset(Kbuf[i][64:65, :], 1.0, eng='pool')
            B.memset(Vbuf[i][:, :].rearrange("p (k d) -> p k d", d=65)[:, :, 64:65], 1.0, eng='pool')
        PT = [B.sb([128, 512], BF16, "PT%d" % i) for i in range(2)]
        rden = B.sb([64, 512], F32, "rden")
        gbuf = [B.sb([128, 514], F32, "gbuf%d" % i) for i in range(2)]
        gcar = [B.sb([128, NJ, 2], F32, "gcar%d" % l) for l in range(2)]
        for l in range(2):
            B.memset(gcar[l][:, :, :], 0.0, eng='pool')
        ubuf = [B.sb([128, 542], BF16, "ubuf%d" % c) for c in range(8)]
        for c in range(8):
            B.memset(ubuf[c][:, 0:30], 0.0, eng='pool')
        u32 = B.sb([128, 8, 32], F32, "u32")
        mean_t = B.sb([128, 512], F32, "mean_t")

        def hfs_ap(l, j):
            idx = l * NJ + j
            return ubuf[idx // 8][:, 0:512].bitcast(F32)[:, (idx % 8) * 32:(idx % 8) * 32 + 32]

        def ost_next():
            t = ost[ostc[0] % 2]
            ostc[0] += 1
            return t

        cur = {"ci": -1}

        def dump(i, tiles=None, k0=0):
            if dbg_chunk is None or cur["ci"] != dbg_chunk:
                return
            tiles = tiles or xT
            for kc, t in enumerate(tiles):
                tv = t if hasattr(t, "tensor") else t[:, :]
                npart = tv.ap[0][1]
                ncol = tv.ap[-1][1]
                B.dma('pool', lambda e, tv=tv, kc=kc, npart=npart, ncol=ncol: e.dma_start(
                    out=o_dbg[i, k0 + kc, 0:npart, 0:ncol], in_=tv), [tv], [o_dbg], is_out=True)

        def load_x(src_rows, N):
            nt = max(1, N // 128)
            rows = min(N, 128)
            for g in range(0, nt, 2):
                for j in range(g, min(nt, g + 2)):
                    t = xtm[j % 2]
                    B.dma('sp', lambda e, t=t, j=j: e.dma_start(out=t[0:rows, :], in_=src_rows[j * 128:j * 128 + rows, :]),
                          [src_rows], [t])
                for kc in range(8):
                    ps = ps_mm()
                    n2 = min(nt, g + 2) - g
                    for jj in range(n2):
                        B.tr(ps[:, jj * 128:jj * 128 + rows], xtm[(g + jj) % 2][0:rows, kc * 128:(kc + 1) * 128],
                             ident[0:rows, 0:rows])
                    B.copy(xT[kc][:, g * 128:g * 128 + (n2 - 1) * 128 + rows], ps[:, 0:(n2 - 1) * 128 + rows])
            for kc in range(8):
                norm_sq(kc, N)

        def store_y(dst_rows, N):
            nt = max(1, N // 128)
            rows = min(N, 128)
            for j in range(nt):
                t = xtm[j % 2]
                for g in range(2):
                    ps = ps_mm()
                    for i in range(4):
                        kc = g * 4 + i
                        B.tr(ps[0:rows, i * 128:(i + 1) * 128], xT[kc][:, j * 128:j * 128 + rows], ident)
                    B.copy(t[0:rows, g * 512:(g + 1) * 512], ps[0:rows, :])
                B.dma('pool', lambda e, t=t, j=j: e.dma_start(out=dst_rows[j * 128:j * 128 + rows, :], in_=t[0:rows, :]),
                      [t], [dst_rows], is_out=True)

        def ffn(l, N, sample=False, last=False, halo=False):
            rmsnorm(PV_NF + 8 * l, N)
            for g in range(6):
                nj = 4 if g < 5 else 2
                G = W.get("ffg%d_%d" % (l, g))
                U = W.get("ffu%d_%d" % (l, g))
                for jj in range(nj):
                    j = 4 * g + jj
                    gps = ps_mm()
                    ups = ps_mm()
                    fm_mm(gps[:, :N], G, jj * 128, 128, N)
                    fm_mm(ups[:, :N], U, jj * 128, 128, N)
                    gb = gbuf[j % 2]
                    acc = tA if j % 2 == 0 else tB
                    w0 = pv[:, PV_WFF + (l * NJ + j) * 3 + 0:PV_WFF + (l * NJ + j) * 3 + 1]
                    w1 = pv[:, PV_WFF + (l * NJ + j) * 3 + 1:PV_WFF + (l * NJ + j) * 3 + 2]
                    w2 = pv[:, PV_WFF + (l * NJ + j) * 3 + 2:PV_WFF + (l * NJ + j) * 3 + 3]
                    bj = pv[:, PV_BFF + l * NJ + j:PV_BFF + l * NJ + j + 1]
                    if not sample:
                        B.copy(gb[:, 0:2], gcar[l][:, j, :], eng='pool')
                        B.copy(gb[:, 2:N + 2], gps[:, :N])
                        B.copy(gcar[l][:, j, :], gb[:, N:N + 2], eng='pool')
                        B.ts(acc[:, :N], gb[:, 0:N], w0, None, ALU.mult)
                        B.stt(acc[:, :N], gb[:, 1:N + 1], w1, acc[:, :N], ALU.mult, ALU.add)
                        B.stt(acc[:, :N], gb[:, 2:N + 2], w2, acc[:, :N], ALU.mult, ALU.add)
                    else:
                        hv = hfs_ap(l, j).rearrange("p (b r) -> p b r", r=2)
                        B.copy(gb[:, 0:N], gps[:, :N])
                        B.ts(acc[:, :N], hv[:, :, 0], w0, None, ALU.mult)
                        B.stt(acc[:, :N], hv[:, :, 1], w1, acc[:, :N], ALU.mult, ALU.add)
                        B.stt(acc[:, :N], gb[:, 0:N], w2, acc[:, :N], ALU.mult, ALU.add)
                        B.op('dve', lambda e, hv=hv, gb=gb: e.tensor_copy(out=hv[:, :, 0], in_=gb[:, 0:N]), [gb], [hv])
                    ge = tC
                    B.act(ge[:, :N], acc[:, :N], AF.Gelu, bias=bj)
                    B.tt(aT[j][:, :N], ge[:, :N], ups[:, :N], ALU.mult)
            for m in range(8):
                Wo = W.get("ffo%d_%d" % (l, m))
                ps = ps_mm()
                for j in range(NJ):
                    B.mm(ps[:, :N], Wo[:, j, :], aT[j][:, :N], start=(j == 0), stop=(j == NJ - 1))
                resid_add(m, ps, N)
            if halo and l == 1:
                B.ts(gcar[1][:, :, :], gcar[1][:, :, :], cflag[:, 64:65], None, ALU.mult)
            if last and not sample:
                fm_to_dram([gcar[l][:, j, :] for j in range(NJ)], 2, o_ff[l])
            if sample:
                B.dma('pool', lambda e: e.dma_start(out=o_ffs[l, :, 0, :], in_=st_ff[l, :, 1, :]), [], [o_ffs], is_out=True)
                fm_to_dram([hfs_ap(l, j).rearrange("p (b r) -> p b r", r=2)[:, :, 0] for j in range(NJ)], NB,
                           o_ffs[l, :, 1, :])

        def conformer(N, sample=False, last=False, halo=False):
            rmsnorm(PV_NM + 8, N)
            for i in range(2):
                A = W.get("pw1a%d" % i)
                G = W.get("pw1g%d" % i)
                for cc in range(4):
                    c = 4 * i + cc
                    aps = ps_mm()
                    gps = ps_mm()
                    fm_mm(aps[:, :N], A, cc * 128, 128, N)
                    fm_mm(gps[:, :N], G, cc * 128, 128, N)
                    B.act(tA[:, :N], gps[:, :N], AF.Sigmoid, bias=pv[:, PV_BPW1 + 8 + c:PV_BPW1 + 8 + c + 1])
                    if not sample:
                        B.stt(ubuf[c][:, 30:30 + N], aps[:, :N], pv[:, PV_BPW1 + c:PV_BPW1 + c + 1], tA[:, :N], ALU.add, ALU.mult)
                        if last:
                            B.stt(u32[:, c, 0:30], aps[:, N - 30:N], pv[:, PV_BPW1 + c:PV_BPW1 + c + 1], tA[:, N - 30:N],
                                  ALU.add, ALU.mult)
                    else:
                        B.stt(u32[:, c, 0:N], aps[:, :N], pv[:, PV_BPW1 + c:PV_BPW1 + c + 1], tA[:, :N], ALU.add, ALU.mult)
            dps_list = []
            if not sample:
                for c in range(8):
                    Dg = W.get("dcv%d" % c)
                    ps = PS[4 + c % 4]
                    for j in range(31):
                        B.mm(ps[:, :N], Dg[:, j * 128:(j + 1) * 128], ubuf[c][:, j:j + N], start=(j == 0), stop=(j == 30))
                    dsb = dbuf[c]
                    B.ts(dsb[:, :N], ps[:, :N], pv[:, PV_BDW + c:PV_BDW + c + 1], None, ALU.add)
                    if halo:
                        B.ts(ubuf[c][:, 0:30], ubuf[c][:, N:N + 30], cflag[:, 64:65], None, ALU.mult, eng='pool')
                    else:
                        B.copy(ubuf[c][:, 0:30], ubuf[c][:, N:N + 30], eng='pool')
            else:
                for c in range(8):
                    W.get("dcv%d" % c)
                hps = [PS[4], PS[5]]
                for hh_ in range(2):
                    B.dma('sp', lambda e, hh_=hh_: e.dma_start(out=gbuf[hh_][0:120, 0:512], in_=wdw_rep[:, hh_ * 512:(hh_ + 1) * 512]),
                          [], [gbuf[hh_]])
                for tl in range(4):
                    t = xtm[tl % 2]
                    B.dma('sp', lambda e, t=t, tl=tl: e.dma_start(
                        out=t[0:120, :], in_=st_cv[4 * tl:4 * tl + 4].rearrange("b j c -> (b j) c")), [], [t])
                    for hh_ in range(2):
                        B.tt(t[0:120, hh_ * 512:(hh_ + 1) * 512], t[0:120, hh_ * 512:(hh_ + 1) * 512], gbuf[hh_][0:120, 0:512], ALU.mult)
                    for c in range(8):
                        B.mm(hps[c // 4][:, (c % 4) * 16 + 4 * tl:(c % 4) * 16 + 4 * tl + 4], t[0:120, c * 128:(c + 1) * 128],
                             sel4, start=True, stop=True)
                for c in range(8):
                    w30 = pv[:, PV_WDW + c * 31 + 30:PV_WDW + c * 31 + 31]
                    B.stt(dbuf[c][:, :N], u32[:, c, 0:N], w30, hps[c // 4][:, (c % 4) * 16:(c % 4) * 16 + 16], ALU.mult, ALU.add)
                    B.ts(dbuf[c][:, :N], dbuf[c][:, :N], pv[:, PV_BDW + c:PV_BDW + c + 1], None, ALU.add)
            mps = ps_mm()
            qps = ps_mm()
            for c in range(8):
                B.copy(hT[c][:, :N], dbuf[c][:, :N], eng='pool')
                B.act(sq[c][:, :N], dbuf[c][:, :N], AF.Square)
            for c in range(8):
                B.mm(mps[:, :N], onesm_b, hT[c][:, :N], start=(c == 0), stop=(c == 7))
            for c in range(8):
                B.mm(qps[:, :N], onesm_b, sq[c][:, :N], start=(c == 0), stop=(c == 7))
            B.copy(mean_t[:, :N], mps[:, :N])
            B.tt(tA[:, :N], mean_t[:, :N], mean_t[:, :N], ALU.mult)
            B.tt(tA[:, :N], qps[:, :N], tA[:, :N], ALU.subtract)
            B.act(rstd[:, :N], tA[:, :N], AF.Ln, bias=epsc)
            B.act(rstd[:, :N], rstd[:, :N], AF.Exp, scale=-0.5)
            for c in range(8):
                t = tB if c % 2 == 0 else tC
                B.tt(t[:, :N], dbuf[c][:, :N], mean_t[:, :N], ALU.subtract)
                B.tt(t[:, :N], t[:, :N], rstd[:, :N], ALU.mult)
                B.act(hT[c][:, :N], t[:, :N], AF.Silu, bias=pv[:, PV_LNB + c:PV_LNB + c + 1],
                      scale=pv[:, PV_LNG + c:PV_LNG + c + 1])
            for i in range(2):
                W2 = W.get("pw2_%d" % i)
                for mm_ in range(4):
                    m = 4 * i + mm_
                    ps = ps_mm()
                    fm_mm(ps[:, :N], W2, mm_ * 128, 128, N)
                    resid_add(m, ps, N, bias=pv[:, PV_BPW2 + m:PV_BPW2 + m + 1])
            if last and not sample:
                fm_to_dram([u32[:, c, 0:30] for c in range(8)], 30, o_cv)
            if sample:
                B.dma('pool', lambda e: e.dma_start(out=o_cvs[:, 0:29, :], in_=st_cv[:, 1:30, :]), [], [o_cvs], is_out=True)
                fm_to_dram([u32[:, c, 0:NB] for c in range(8)], NB, o_cvs[:, 29, :])

        dbuf = q32 + frt

        def hgrn_common(h, N, o_ps):
            B.act(tA[:, :N], o_ps[:, :N], AF.Square)
            B.copy(PT[0][:, :N], tA[:, :N], eng='pool')
            ms = ps_mm()
            B.mm(ms[:, :N], ones128_b, PT[0][:, :N])
            B.act(tB[:, :N], ms[:, :N], AF.Ln, bias=epsc)
            B.act(tB[:, :N], tB[:, :N], AF.Exp, scale=-0.5)
            B.stt(tA[:, :N], o_ps[:, :N], pv[:, PV_GN + h:PV_GN + h + 1], tB[:, :N], ALU.mult, ALU.mult)
            B.tt(cat_hg[h][:, :N], tA[:, :N], gate[h][:, :N], ALU.mult)

        def mixer_prompt(ci, t0, mode='own'):
            N = NCHUNK
            NT = 4
            part = (mode == 'partial')
            wr_out = (mode == 'own')
            to = t0 - (OWN0 if nch == 8 else 0)
            rmsnorm(PV_NM, N)
            nkb = 4 * ci + 4
            B.copy(cref[:, :], carry[:, :], eng='dve')
            for j in range(NT):
                ps = ps_mm()
                for kc in range(8):
                    B.mm(ps[:, 0:8], hT[kc][:, j * 128:(j + 1) * 128], wff[:, kc, :], start=(kc == 0), stop=(kc == 7))
                B.tt(flf[j][:, :], ps[:, 0:8], pv[:, PV_FB:PV_FB + 8], ALU.add)
                B.act(flf[j][:, :], flf[j][:, :], AF.Sigmoid)
                B.act(flf[j][:, :], flf[j][:, :], AF.Ln)
                if wr_out:
                    B.dma('pool', lambda e, j=j: e.dma_start(out=o_fl[to + j * 128:to + (j + 1) * 128, :], in_=flf[j][:, :]),
                          [flf[j]], [o_fl], is_out=True)
            if not part:
                s0 = W.get("in0")
                for h in range(4):
                    ps = ps_mm()
                    fm_mm(ps[:, :N], s0, h * 128, 128, N)
                    B.act(q32[h][:, :N], ps[:, :N], AF.Silu)
            for j in range(NT):
                ps2 = ps_mm()
                B.mm(ps2[:, 0:8], tri_le, flf[j][:, :])
                B.mm(ps2[:, 8:16], onesf, flf[j][:, :])
                ct = ctm[j % 2]
                B.tt(ct[:, :], ps2[:, 0:8], carry[:, :], ALU.add)
                kb = ci * 4 + j
                B.ts(negc[:, kb, :], ct[:, :], -1.0, None, ALU.mult)
                B.tt(ct[:, :], ct[:, :], cref[:, :], ALU.subtract)
                B.ts(Zr[j][:, :, 64], ct[:, :], 8.0, None, ALU.mult)
                B.tt(carry[:, :], carry[:, :], ps2[:, 8:16], ALU.add)
            if not part:
                B.tt(biasc[:, 0:nkb, :], negc[:, 0:nkb, :], cref[:, :].unsqueeze(1).to_broadcast([128, nkb, 8]), ALU.add)
                if nch == 8:
                    mrow = 0 if mode == 'halo' else 32
                    B.tt(biasc[:, 0:nkb, :], biasc[:, 0:nkb, :],
                         cflag[:, mrow:mrow + nkb].unsqueeze(2).to_broadcast([128, nkb, 8]), ALU.add)
            s1 = W.get("in1")
            for h in range(4):
                ps = ps_mm()
                fm_mm(ps[:, :N], s1, h * 128, 128, N)
                B.act(tA[:, :N], ps[:, :N], AF.Sigmoid)
                B.ts(frt[h][:, :N], tA[:, :N], lbt[:, 4 + h:5 + h], lbt[:, h:h + 1], ALU.mult, ALU.add)

            def prep(h):
                st = h % 2
                nQt, nKt, Qb, csm, nKtm = nQt_s[st], nKt_s[st], Qb_s[st], csm_s[st], nKtm_s[st]
                lf = tA
                B.act(lf[:, :N], frt[h][:, :N], AF.Ln)
                Cs = tB
                B.op('dve', lambda e: e.tensor_tensor_scan(out=Cs[:, :N], data0=onesf[:, 0:1].to_broadcast([128, N]), data1=lf[:, :N],
                                                           initial=0.0, op0=ALU.mult, op1=ALU.add), [cst, lf], [Cs])
                C3 = Cs[:, :N].rearrange("p (n t) -> p n t", t=64)
                D1 = tC
                D3 = D1[:, :N].rearrange("p (n t) -> p n t", t=64)
                B.tt(D3, C3, C3[:, :, 32:33].to_broadcast([128, 8, 64]), ALU.subtract)
                B.memset(csm[:, 0:1], 0.0)
                B.copy(csm[:, 1:8], C3[:, 0:7, 63], eng='dve')
                B.tt(csm[:, 8:16], C3[:, :, 32], csm[:, 0:8], ALU.subtract)
                B.tt(csm[:, 16:24], C3[:, :, 63], csm[:, 0:8], ALU.subtract)
                B.tt(csm[:, 48:56], csm[:, 16:24], csm[:, 8:16], ALU.subtract)
                E1 = tA
                B.act(E1[:, :N], D1[:, :N], AF.Exp)
                B.act(csm[:, 24:48], csm[:, 8:32], AF.Exp) if False else None
                B.act(csm[:, 24:32], csm[:, 8:16], AF.Exp)
                B.act(csm[:, 32:40], csm[:, 16:24], AF.Exp)
                B.act(csm[:, 40:48], csm[:, 48:56], AF.Exp)
                E3 = tB
                B.act(E3[:, :N], D1[:, :N], AF.Exp, scale=-1.0)
                if not part:
                    B.stt(nQt[:, :N], q32[h][:, :N], -1.0, E1[:, :N], ALU.mult, ALU.mult)
                B.ts(csm[:, 24:32], csm[:, 24:32], -1.0, None, ALU.mult)
                B.ts(csm[:, 40:48], csm[:, 40:48], -1.0, None, ALU.mult)
                B.stt(nKt[:, :N], frt[h][:, :N], 1.0, E3[:, :N], ALU.subtract, ALU.mult)
                if not part:
                    B.tt(Qb[:, :N].rearrange("p (n t) -> p n t", t=64), nQt[:, :N].rearrange("p (n t) -> p n t", t=64),
                         csm[:, 24:32].unsqueeze(2).to_broadcast([128, 8, 64]), ALU.mult)
                pst = ps_mm()
                pstb = pst[:, :].bitcast(BF16)
                for j in range(NT):
                    B.tr(pstb[:, j * 128:(j + 1) * 128], nKt[:, j * 128:(j + 1) * 128], ident_b)
                for j in range(NT):
                    B.copy(nKtm[j][:, :], pstb[:, j * 128:(j + 1) * 128])

            prep(0)
            s2 = W.get("in2")
            for j in range(NT):
                ps = ps_mm()
                tm_mm(ps[:, :], s2, 0, 512, j * 128, 128)
                B.copy(vhg[j][:, :], ps[:, :])
            prep(1)
            if not part:
                s3 = W.get("in3")
                for h in range(4):
                    ps = ps_mm()
                    fm_mm(ps[:, :N], s3, h * 128, 128, N)
                    B.act(gate[h][:, :N], ps[:, :N], AF.Silu)
                s4 = W.get("in4")
                for h in range(8):
                    ps = ps_mm()
                    for j in range(NT):
                        B.mm(ps[0:65, j * 128:(j + 1) * 128], Zr[j][:, h, :], ident_b, start=True, stop=False)
                    for kc in range(8):
                        B.mm(ps[0:64, :N], s4[:, kc, h * 64:(h + 1) * 64], hT[kc][:, :N], start=False, stop=(kc == 7))
                    B.copy(Qp[h][:, :N], ps[0:65, :N])
            s5 = W.get("in5")
            for h in range(8):
                ps = ps_mm()
                fm_mm(ps[0:64, :N], s5, h * 64, 64, N)
                ks = kst[h % 2]
                B.copy(ks[:, :N], ps[0:64, :N])
                B.dma('pool', lambda e, ks=ks, h=h: e.dma_start(out=s_k[h, :, t0:t0 + N], in_=ks[:, :N]), [ks], [s_k])
            for j in range(NT if wr_out else 0):
                ps = ps_mm()
                tm_mm(ps[:, :], s5, 0, 512, j * 128, 128)
                o = ost_next()
                B.copy(o[:, :], ps[:, :])
                B.dma('pool', lambda e, o=o, j=j: e.dma_start(out=o_fk[to + j * 128:to + (j + 1) * 128, :], in_=o[:, :]),
                      [o], [o_fk], is_out=True)
            s6 = W.get("in6")
            for j in range(NT):
                ps = ps_mm()
                tm_mm(ps[:, :], s6, 0, 512, j * 128, 128)
                if wr_out:
                    o = ost_next()
                    B.copy(o[:, :], ps[:, :])
                    B.dma('pool', lambda e, o=o, j=j: e.dma_start(out=o_fv[to + j * 128:to + (j + 1) * 128, :], in_=o[:, :]),
                          [o], [o_fv], is_out=True)
                vs = vst[j % 2]
                B.copy(vs[:, :], ps[:, :], eng='dve')
                kb = ci * 4 + j
                B.dma('pool', lambda e, vs=vs, kb=kb: e.dma_start(
                    out=s_v[:, :, kb, :].rearrange("h p d -> p h d"), in_=vs[:, :].rearrange("p (h d) -> p h d", d=64)),
                    [vs], [s_v])
            for h in range(4):
                st = h % 2
                nQt, nKt, Qb, csm, nKtm = nQt_s[st], nKt_s[st], Qb_s[st], csm_s[st], nKtm_s[st]
                o_ps = PS[6 + h % 2]
                for j in range(NT):
                    if not part:
                        aps = ps_mm()
                        B.mm(aps[:, 0:128], nKt[:, j * 128:(j + 1) * 128], nQt[:, j * 128:(j + 1) * 128])
                        am = attm[j % 2]
                        B.tt(am[:, :], aps[:, 0:128], mask2f, ALU.mult)
                        B.mm(o_ps[:, j * 128:(j + 1) * 128], vhg[j][:, h * 128:(h + 1) * 128], am[:, :], start=True, stop=False)
                    for hf in range(2):
                        n = 2 * j + hf
                        if not part:
                            B.mm(o_ps[:, n * 64:(n + 1) * 64], Sbf[h][:, :], Qb[:, n * 64:(n + 1) * 64], start=False, stop=True)
                        kps = ps_mm()
                        B.mm(kps[:, 0:128], nKtm[j][hf * 64:(hf + 1) * 64, :], vhg[j][hf * 64:(hf + 1) * 64, h * 128:(h + 1) * 128])
                        kt = kvt[n % 2]
                        B.ts(kt[:, :], kps[:, 0:128], csm[:, 40 + n:41 + n], None, ALU.mult)
                        B.stt(S32[h][:, :], S32[h][:, :], csm[:, 32 + n:33 + n], kt[:, :], ALU.mult, ALU.add)
                        B.copy(Sbf[h][:, :], S32[h][:, :], eng='pool')
                if not part:
                    hgrn_common(h, N, o_ps)
                if ci == NCH - 1:
                    B.dma('pool', lambda e, h=h: e.dma_start(out=o_hg[h], in_=S32[h][:, :]), [S32[h]], [o_hg], is_out=True)
                if h + 2 < 4:
                    prep(h + 2)
            if part:
                return
            for h in range(8):
                Kb = Kbuf[0]
                Vb = Vbuf[0]
                B.dma('sp', lambda e, Kb=Kb, h=h: e.dma_start(out=Kb[0:64, 0:nkb * 128], in_=s_k[h, :, 0:nkb * 128]), [s_k], [Kb])
                B.dma('sp', lambda e, Vb=Vb, h=h: e.dma_start(out=Vb[:, :].rearrange("p (k d) -> p k d", d=65)[:, 0:nkb, 0:64],
                                                          in_=s_v[h, :, 0:nkb, :]), [s_v], [Vb])
                O_ps = PS[6 + h % 2]
                for kb in range(nkb):
                    jd = kb - 4 * ci
                    c0 = max(0, jd) * 128
                    S_ps = PS[4 + kb % 2]
                    B.mm(S_ps[:, c0:N], Kb[0:65, kb * 128:(kb + 1) * 128], Qp[h][0:65, c0:N])
                    P = PT[kb % 2]
                    B.act(P[:, c0:N], S_ps[:, c0:N], AF.Exp, bias=biasc[:, kb, h:h + 1], scale=0.125)
                    if jd >= 0:
                        B.tt(P[:, c0:c0 + 128], P[:, c0:c0 + 128], tri_le_b, ALU.mult, eng='pool')
                    B.mm(O_ps[0:65, c0:N], Vb[:, kb * 65:(kb + 1) * 65], P[:, c0:N], start=(kb == 0), stop=(kb == nkb - 1))
                B.copy(tA[0:65, :N], O_ps[0:65, :N])
                dps = ps_mm()
                B.mm(dps[0:64, :N], sel65, tA[0:65, :N])
                B.op('dve', lambda e, dps=dps: e.reciprocal(out=rden[0:64, :N], in_=dps[0:64, :N]), [dps], [rden])
                B.tt(cat_fx[h][0:64, :N], tA[0:64, :N], rden[0:64, :N], ALU.mult)
            wa = W.get("wo_a")
            wb = [W.get("wo_b0"), W.get("wo_b1")]
            for m in range(8):
                ps = ps_mm()
                for h in range(4):
                    B.mm(ps[:, :N], wa[:, h, m * 128:(m + 1) * 128], cat_hg[h][:, :N], start=(h == 0), stop=False)
                for h in range(8):
                    B.mm(ps[:, :N], wb[m // 4][0:64, h, (m % 4) * 128:(m % 4 + 1) * 128], cat_fx[h][0:64, :N], start=False,
                         stop=(h == 7))
                resid_add(m, ps, N)

        def mixer_sample():
            N = NB
            rmsnorm(PV_NM, N)
            sl = [W.get("in%d" % i) for i in range(3)]
            flfs = B.sb([NB, 8], F32, "flfs")
            ps = ps_mm()
            for kc in range(8):
                B.mm(ps[0:N, 0:8], hT[kc][:, 0:N], wff[:, kc, :], start=(kc == 0), stop=(kc == 7))
            B.tt(flfs[:, :], ps[0:N, 0:8], pv[0:N, PV_FB:PV_FB + 8], ALU.add)
            B.act(flfs[:, :], flfs[:, :], AF.Sigmoid)
            B.act(flfs[:, :], flfs[:, :], AF.Ln)
            B.dma('pool', lambda e: e.dma_start(out=o_fls, in_=flfs[:, :]), [flfs], [o_fls], is_out=True)
            for h in range(4):
                ps = ps_mm()
                fm_mm(ps[:, :N], sl[0], h * 128, 128, N)
                B.act(q32[h][:, :N], ps[:, :N], AF.Silu)
            for h in range(4):
                ps = ps_mm()
                fm_mm(ps[:, :N], sl[1], h * 128, 128, N)
                B.act(tA[:, :N], ps[:, :N], AF.Sigmoid)
                B.ts(frt[h][:, :N], tA[:, :N], lbt[:, 4 + h:5 + h], lbt[:, h:h + 1], ALU.mult, ALU.add)
            K32 = Kbuf[0][:, :].bitcast(F32)
            vs_tm = K32[0:NB, 0:512]
            ps = ps_mm()
            tm_mm(ps[0:N, :], sl[2], 0, 512, 0, N)
            B.copy(vs_tm, ps[0:N, :])
            s3 = W.get("in3")
            for h in range(4):
                ps = ps_mm()
                fm_mm(ps[:, :N], s3, h * 128, 128, N)
                B.act(gate[h][:, :N], ps[:, :N], AF.Silu)
            qs_tm = K32[0:NB, 512:1024]
            ks_tm = K32[0:NB, 1024:1536]
            vf_tm = K32[0:NB, 1536:2048]
            for i, (dstt, odr) in enumerate(((qs_tm, None), (ks_tm, o_fks), (vf_tm, o_fvs))):
                s = W.get("in%d" % (4 + i))
                ps = ps_mm()
                tm_mm(ps[0:N, :], s, 0, 512, 0, N)
                B.copy(dstt, ps[0:N, :])
                if odr is not None:
                    B.dma('pool', lambda e, dstt=dstt, odr=odr: e.dma_start(out=odr, in_=dstt), [dstt], [odr], is_out=True)
            o_ps = [PS[6], PS[7]]
            for b in range(NB):
                st = ost_next()
                B.dma('sp', lambda e, st=st, b=b: e.dma_start(out=st[:, :].rearrange("p (h v) -> p h v", v=128),
                                                          in_=st_hg[b].rearrange("h k v -> k h v")), [], [st])
                vb = ps_mm()
                B.mm(vb[:, :], ident[0:N, b:b + 1].to_broadcast([N, 128]), vs_tm)
                for h in range(4):
                    B.ts(tC[:, h * 128:(h + 1) * 128], vb[:, h * 128:(h + 1) * 128], frt[h][:, b:b + 1], -1.0, ALU.mult, ALU.mult)
                    B.tt(tC[:, h * 128:(h + 1) * 128], tC[:, h * 128:(h + 1) * 128], vb[:, h * 128:(h + 1) * 128], ALU.add)
                    B.stt(st[:, h * 128:(h + 1) * 128], st[:, h * 128:(h + 1) * 128], frt[h][:, b:b + 1],
                          tC[:, h * 128:(h + 1) * 128], ALU.mult, ALU.add)
                    B.mm(o_ps[h // 2][:, (h % 2) * 16 + b:(h % 2) * 16 + b + 1], st[:, h * 128:(h + 1) * 128], q32[h][:, b:b + 1])
                B.dma('pool', lambda e, st=st, b=b: e.dma_start(out=o_hgs[b].rearrange("h k v -> k h v"),
                                                            in_=st[:, :].rearrange("p (h v) -> p h v", v=128)),
                      [st], [o_hgs], is_out=True)
            for h in range(4):
                hgrn_common(h, N, o_ps[h // 2][:, (h % 2) * 16:(h % 2) * 16 + 16])
            ptb_i = rstd[:, 0:NB * 16].bitcast(I32)
            ptb_f = mean_t[:, 0:NB * 16]
            B.dma('sp', lambda e: e.dma_start(out=ptb_i, in_=ptab), [], [ptb_i])
            B.copy(ptb_f, ptb_i, eng='dve')
            B.ts(ptb_f, ptb_f, 128.0, iota_p, ALU.mult, ALU.add)
            B.copy(ptb_i, ptb_f, eng='dve')
            pn = B.sb([NB, 8], F32, "pn")
            pnb = [B.sb([NB, 8], F32, "pnb%d" % i) for i in range(2)]
            prod16 = ost[0][0:NB, :]
            B.tt(prod16, qs_tm, ks_tm, ALU.mult)
            B.op('dve', lambda e: e.tensor_reduce(out=pn[:, :], in_=prod16.rearrange("p (h d) -> p h d", d=64),
                                                  axis=AX.X, op=ALU.add), [prod16], [pn])
            B.act(pn[:, :], pn[:, :], AF.Exp, scale=0.125)
            lfp = [kvt[0][:, :], kvt[1][:, :]]
            bia = [tA[:, 256:384], tA[:, 384:512]]
            sfxb = [tC[:, 0:128], tC[:, 128:256]]
            sc = [tB[:, 0:128], tB[:, 128:256]]
            pp = [tB[:, 256:384], tB[:, 384:512]]
            kpg = [q32[0], q32[1], q32[2]]
            vpg = [frt[0], frt[1], frt[2]]
            V32 = Vbuf[0][:, 0:2048].bitcast(F32)
            Rn = [V32[0:8, 0:512], V32[0:8, 512:1024]]
            rd = B.sb([8, 2], F32, "rd")
            ofx = PS[5]
            bmask_t = ost[1][0:8, :]
            B.op('dve', lambda e: e.tensor_copy(out=bmask_t.rearrange("p (h d) -> p h d", d=64),
                                                in_=cst[0:8, 0:8].unsqueeze(2).to_broadcast([8, 8, 64])), [cst], [bmask_t])
            for b in range(NB):
                lf = lfp[b % 2]
                for pg in range(16):
                    col = b * 16 + pg
                    B.dma('pool', lambda e, lf=lf, pg=pg, col=col: e.indirect_dma_start(
                        out=lf[:, pg * 8:(pg + 1) * 8], out_offset=None, in_=cl,
                        in_offset=bass.IndirectOffsetOnAxis(ap=ptb_i[:, col:col + 1], axis=0)), [ptb_i], ["lfk%d_%d" % (b % 2, pg)])
                ps = ps_mm()
                lfkeys = ["lfk%d_%d" % (b % 2, pg) for pg in range(16)]
                B.mm(ps[:, 0:128], tri_gt, lf, extra_r=lfkeys)
                B.mm(ps[:, 128:256], onesf, lf, extra_r=lfkeys)
                B.mm(ps[:, 256:264], ident[0:N, b:b + 1].to_broadcast([N, 128]), flfs[:, :])
                prev = sfxb[0]
                B.copy(prev, ps[:, 128:256], eng='dve')
                for li, sh in enumerate((1, 2, 4, 8)):
                    cur = sfxb[(li + 1) % 2]
                    n_ok = (16 - sh) * 8
                    B.tt(cur[:, 0:n_ok], prev[:, 0:n_ok], prev[:, sh * 8:128], ALU.add)
                    B.copy(cur[:, n_ok:128], prev[:, n_ok:128], eng='dve')
                    prev = cur
                bi = bia[b % 2]
                B.tt(bi[:, 0:120], ps[:, 0:120], prev[:, 8:128], ALU.add)
                B.copy(bi[:, 120:128], ps[:, 120:128], eng='dve')
                B.tt(bi.rearrange("p (g h) -> p g h", h=8), bi.rearrange("p (g h) -> p g h", h=8),
                     ps[:, 256:264].unsqueeze(1).to_broadcast([128, 16, 8]), ALU.add)
                qb = ps_mm()
                B.mm(qb[:, :], ident[0:N, b:b + 1].to_broadcast([N, 128]), qs_tm)
                s_t = sc[b % 2]
                for pg in range(16):
                    col = b * 16 + pg
                    kp = kpg[pg % 3]
                    B.dma('pool', lambda e, kp=kp, col=col: e.indirect_dma_start(
                        out=kp[:, :], out_offset=None, in_=ck,
                        in_offset=bass.IndirectOffsetOnAxis(ap=ptb_i[:, col:col + 1], axis=0)), [ptb_i], [kp])
                    B.tt(kp[:, :], kp[:, :], qb[:, :], ALU.mult)
                    B.op('dve', lambda e, kp=kp, pg=pg, s_t=s_t: e.tensor_reduce(
                        out=s_t[:, pg * 8:(pg + 1) * 8], in_=kp[:, :].rearrange("p (h d) -> p h d", d=64), axis=AX.X,
                        op=ALU.add), [kp], [s_t])
                p_t = pp[b % 2]
                B.stt(s_t, s_t, 0.125, bi, ALU.mult, ALU.add)
                B.act(p_t, s_t, AF.Exp)
                pb = pnb[b % 2]
                B.ts(pb[:, :], pn[:, :], ident[0:N, b:b + 1], None, ALU.mult)
                R_ps = PS[6 + b % 2]
                d_ps = PS[4]
                for pg in range(16):
                    col = b * 16 + pg
                    vp = vpg[pg % 3]
                    B.dma('pool', lambda e, vp=vp, col=col: e.indirect_dma_start(
                        out=vp[:, :], out_offset=None, in_=cv,
                        in_offset=bass.IndirectOffsetOnAxis(ap=ptb_i[:, col:col + 1], axis=0)), [ptb_i], [vp])
                    B.mm(R_ps[0:8, :], p_t[:, pg * 8:(pg + 1) * 8], vp[:, :], start=(pg == 0), stop=False)
                B.mm(R_ps[0:8, :], pb[:, :], vf_tm, start=False, stop=True)
                for pg in range(16):
                    B.mm(d_ps[0:8, 0:1], p_t[:, pg * 8:(pg + 1) * 8], onesf[:, 0:1], start=(pg == 0), stop=False)
                B.mm(d_ps[0:8, 0:1], pb[:, :], onesf[0:N, 0:1], start=False, stop=True)
                B.op('dve', lambda e: e.reciprocal(out=rd[:, 0:1], in_=d_ps[0:8, 0:1]), [d_ps], [rd])
                rn = Rn[b % 2]
                B.stt(rn, R_ps[0:8, :], rd[:, 0:1], bmask_t, ALU.mult, ALU.mult)
                for pr in range(4):
                    B.mm(ofx[:, pr * 16 + b:pr * 16 + b + 1], rn[:, pr * 128:(pr + 1) * 128], onesf[0:8, 0:1])
            catfs = B.sb([128, 4, NB], BF16, "catfs")
            B.copy(catfs[:, :, :], ofx[:, 0:64].rearrange("p (a b) -> p a b", b=NB))
            wa = W.get("wo_a")
            wc = W.get("wo_c")
            for m in range(8):
                ps = ps_mm()
                for h in range(4):
                    B.mm(ps[:, :N], wa[:, h, m * 128:(m + 1) * 128], cat_hg[h][:, :N], start=(h == 0), stop=False)
                for h in range(4):
                    B.mm(ps[:, :N], wc[:, h, m * 128:(m + 1) * 128], catfs[:, h, :], start=False, stop=(h == 3))
                resid_add(m, ps, N)

        for ci in range(NCH):
            t0 = ci * NCHUNK
            last = (ci == NCH - 1)
            cur["ci"] = ci
            mode = chunk_mode(ci)
            load_x(xp[t0:t0 + NCHUNK, :], NCHUNK)
            dump(0)
            mixer_prompt(ci, t0, mode)
            if mode == 'partial':
                continue
            dump(1)
            dump(5, cat_hg)
            dump(6, cat_fx)
            ffn(0, NCHUNK, last=last)
            dump(2)
            conformer(NCHUNK, last=last, halo=(mode == 'halo'))
            dump(3)
            ffn(1, NCHUNK, last=last, halo=(mode == 'halo'))
            dump(4)
            if mode == 'halo':
                continue
            rmsnorm(PV_NFIN, NCHUNK, out_f32=True)
            to = t0 - (OWN0 if nch == 8 else 0)
            store_y(o_y[to:to + NCHUNK, :], NCHUNK)
        if do_sample:
            for l in range(2):
                t = xtm[l % 2]
                for half in range(3):
                    c0 = half * 1024
                    c1 = min(FFN, c0 + 1024)
                    B.dma('sp', lambda e, t=t, l=l, c0=c0, c1=c1: e.dma_start(
                        out=t[0:32, 0:c1 - c0], in_=st_ff[l].rearrange("b r f -> (b r) f")[:, c0:c1]), [], [t])
                    for j in range(c0 // 128, c1 // 128):
                        ps = ps_mm()
                        B.tr(ps[:, 0:32], t[0:32, j * 128 - c0:(j + 1) * 128 - c0], ident[0:32, 0:32])
                        B.copy(hfs_ap(l, j), ps[:, 0:32])
            load_x(xs, NB)
            mixer_sample()
            ffn(0, NB, sample=True)
            conformer(NB, sample=True)
            ffn(1, NB, sample=True)
            rmsnorm(PV_NFIN, NB, out_f32=True)
            store_y(o_ys, NB)
        B.finish()
    return nc


_NC_CACHE = {}


def _host_consts():
    cst = np.zeros((128, 1024), np.float32)
    r = np.arange(128)
    cst[:, 0:128] = np.eye(128)
    cst[:, 128:256] = 1.0
    cst[:, 256:384] = (r[:, None] <= r[None, :])
    cst[:, 384:512] = (r[:, None] <= r[None, :]) & ((r[:, None] // 64) == (r[None, :] // 64))
    cst[:, 512:640] = (r[:, None] > r[None, :])
    cst[:, 640] = r
    cst[:, 641] = EPS
    sel = np.zeros((128, 4), np.float32)
    for i in range(120):
        sel[i, i // 30] = 1.0
    cst[:, 648:652] = sel
    cst[64, 656:720] = 1.0
    return cst


def _fm(v, nchunk):
    return np.ascontiguousarray(np.asarray(v, np.float32).reshape(nchunk, 128).T)


def kernel(x_prompt, x_sample, cache_fox_k, cache_fox_v, cache_fox_logf, page_table,
           state_hgrn, state_conv, state_ffn_conv, norm_mix, norm_ffn, norm_final,
           w_in0, fox_fb, hg_lb, hg_gnorm, w_out0, w_pw1, b_pw1, w_dw, b_dw, ln_g, ln_b,
           w_pw2, b_pw2, w_ffn_in, w_ffn_dw, b_ffn_dw, w_ffn_out):
    f = lambda a: np.ascontiguousarray(np.asarray(a, np.float32))
    if 'nc' not in _NC_CACHE:
        _NC_CACHE['nc'] = build_program()
    nc = _NC_CACHE['nc']
    pvec = np.zeros((128, 640), np.float32)
    pvec[:, 0:16] = np.concatenate([_fm(norm_mix[0], 8), _fm(norm_mix[1], 8)], 1)
    pvec[:, 16:32] = np.concatenate([_fm(norm_ffn[0], 8), _fm(norm_ffn[1], 8)], 1)
    pvec[:, 32:40] = _fm(norm_final, 8)
    pvec[:, 40:48] = np.concatenate([_fm(hg_lb[0], 4), _fm(hg_lb[1], 4)], 1)
    pvec[:, 48:52] = _fm(hg_gnorm[0], 4)
    pvec[:, 52:68] = _fm(b_pw1[0], 16)
    pvec[:, 68:76] = _fm(b_dw[0], 8)
    pvec[:, 76:84] = _fm(ln_g[0], 8)
    pvec[:, 84:92] = _fm(ln_b[0], 8)
    pvec[:, 92:100] = _fm(b_pw2[0], 8)
    pvec[:, 100:144] = np.concatenate([_fm(b_ffn_dw[0], 22), _fm(b_ffn_dw[1], 22)], 1)
    wffd = np.asarray(w_ffn_dw, np.float32).reshape(2, 3, 22, 128)
    pvec[:, 144:276] = np.ascontiguousarray(wffd.transpose(3, 0, 2, 1)).reshape(128, 132)
    wd = np.asarray(w_dw, np.float32)[0].reshape(31, 8, 128)
    pvec[:, 276:524] = np.ascontiguousarray(wd.transpose(2, 1, 0)).reshape(128, 248)
    pvec[:, 524:532] = np.broadcast_to(np.asarray(fox_fb, np.float32)[0][None, :], (128, 8))
    cst = _host_consts()
    wdw_rep = np.ascontiguousarray(np.tile(np.asarray(w_dw, np.float32)[0, 0:30], (4, 1)))
    ckf = f(cache_fox_k)[0].reshape(NPOOL * 128, 512)
    cvf = f(cache_fox_v)[0].reshape(NPOOL * 128, 512)
    clf = f(cache_fox_logf)[0].reshape(NPOOL * 128, 8)
    pt = np.asarray(page_table, np.int32)
    shared = {"ck": ckf, "cv": cvf, "cl": clf, "pvec": pvec, "cst": cst, "wdw_rep": wdw_rep,
              "w_in0": f(w_in0)[0], "w_out0": f(w_out0)[0], "w_pw1": f(w_pw1)[0], "w_pw2": f(w_pw2)[0],
              "w_ffi": f(w_ffn_in), "w_ffo": f(w_ffn_out)}
    xpf = f(x_prompt)
    xsf = f(x_sample)[:, 0, :]
    sth = f(state_hgrn)[0]
    stc = f(state_conv)[0]
    stf = f(state_ffn_conv)
    in_maps = []
    for c in range(NCORE):
        sl = slice(NB * c, NB * (c + 1))
        m = dict(shared)
        bb, half = c // 2, c % 2
        cf = np.zeros((128, 72), np.float32)
        if half == 0:
            xin = np.zeros((SEQ, D), np.float32)
            xin[SEQ // 2:] = xpf[bb, :SEQ // 2]
            cf[:, 0:12] = -30000.0
            cf[:, 32:48] = -30000.0
        else:
            xin = xpf[bb]
            cf[:, 64] = 1.0
        m["xp"] = xin
        m["cflag"] = cf
        m["xs"] = np.ascontiguousarray(xsf[sl])
        m["ptab"] = np.ascontiguousarray(np.broadcast_to(pt[sl].reshape(1, NB * 16), (128, NB * 16)))
        m["st_hg"] = np.ascontiguousarray(sth[sl])
        m["st_cv"] = np.ascontiguousarray(stc[sl])
        m["st_ff"] = np.ascontiguousarray(stf[:, sl])
        in_maps.append(m)
    res = run_bass_kernel_spmd(nc, in_maps, core_ids=list(range(NCORE)))
    R = res.results
    cat = lambda k, ax=0: np.concatenate([R[c][k] for c in range(NCORE)], axis=ax)
    stk4 = lambda k: np.stack([np.concatenate([R[2 * b][k], R[2 * b + 1][k]], axis=0) for b in range(4)], axis=0)
    odd4 = lambda k: np.stack([R[2 * b + 1][k] for b in range(4)], axis=0)
    y_prompt = stk4("o_y")
    y_sample = cat("o_ys")[:, None, :]
    fk_p = stk4("o_fk").reshape(1, 4, SEQ, 8, 64)
    fv_p = stk4("o_fv").reshape(1, 4, SEQ, 8, 64)
    fl_p = stk4("o_fl").reshape(1, 4, SEQ, 8)
    fk_s = cat("o_fks").reshape(1, 128, 1, 8, 64)
    fv_s = cat("o_fvs").reshape(1, 128, 1, 8, 64)
    fl_s = cat("o_fls").reshape(1, 128, 1, 8)
    hg_p = odd4("o_hg")[None]
    hg_s = cat("o_hgs")[None]
    cv_p = odd4("o_cv")[None]
    cv_s = cat("o_cvs")[None]
    ff_p = np.stack([R[2 * b + 1]["o_ff"] for b in range(4)], axis=1)
    ff_s = cat("o_ffs", 1)
    outs = (y_prompt, y_sample, fk_p, fv_p, fl_p, fk_s, fv_s, fl_s, hg_p, hg_s, cv_p, cv_s, ff_p, ff_s)
    return tuple(np.ascontiguousarray(o, dtype=np.float32) for o in outs)
```

```python
import numpy as np
from contextlib import ExitStack
import concourse.bass as bass
import concourse.mybir as mybir
from concourse.bass_utils import run_bass_kernel_spmd

F32, BF16, I32 = mybir.dt.float32, mybir.dt.bfloat16, mybir.dt.int32
AF = mybir.ActivationFunctionType
ALU = mybir.AluOpType
AX = mybir.AxisListType

D = 1024
SEQ = 4096
NCORE = 8
NB = 16
NPOOL = 2560
FFN = 2816
NJ = 22
IN0 = 3592
EPS = 1e-6
NCHUNK = 512
RING = 4096


def _keys(lst):
    out = []
    for a in lst:
        if a is None:
            continue
        out.append(a if isinstance(a, str) else getattr(a, 'tensor', a).name)
    return out


class Bld:
    def __init__(self, nc, es):
        self.nc = nc
        self.es = es
        self.E = {'pe': nc.tensor, 'act': nc.scalar, 'dve': nc.vector, 'pool': nc.gpsimd, 'sp': nc.sync}
        self.sem = {e: es.enter_context(nc.semaphore('s_' + e)) for e in self.E}
        self.cnt = {e: 0 for e in self.E}
        self.waited = {e: {} for e in self.E}
        self.NDS = 48
        self.dsem = [es.enter_context(nc.semaphore('d%d' % i)) for i in range(self.NDS)]
        self.dval = [0] * self.NDS
        self.dnext = 0
        self.res = {}
        self.out_deps = []
        self.ntens = 0

    def sb(self, shape, dt, name=None):
        self.ntens += 1
        return self.es.enter_context(self.nc.sbuf_tensor(name or ('t%d' % self.ntens), list(shape), dt))

    def psum(self, name):
        return self.es.enter_context(self.nc.psum_tensor(name, [128, 512], F32))

    def _semh(self, key):
        return self.sem[key] if isinstance(key, str) else self.dsem[key]

    def _wait(self, eng, dep):
        key, val = dep
        if key == 'pe' and eng == 'pe':
            return
        if self.waited[eng].get(key, 0) >= val:
            return
        self.E[eng].wait_ge(self._semh(key), val)
        self.waited[eng][key] = val

    def _deps(self, eng, r, w):
        for k in r:
            e = self.res.get(k)
            if e and e[0]:
                self._wait(eng, e[0])
        for k in w:
            e = self.res.get(k)
            if e:
                if e[0]:
                    self._wait(eng, e[0])
                for kk, vv in e[1].items():
                    self._wait(eng, (kk, vv))

    def _commit(self, me, r, w):
        for k in r:
            e = self.res.setdefault(k, [None, {}])
            e[1][me[0]] = max(e[1].get(me[0], 0), me[1])
        for k in w:
            self.res[k] = [me, {}]

    def op(self, eng, fn, r, w):
        r = _keys(r)
        w = _keys(w)
        self._deps(eng, r, w)
        ins = fn(self.E[eng])
        self.cnt[eng] += 1
        ins.then_inc(self.sem[eng], 1)
        self._commit((eng, self.cnt[eng]), r, w)

    def dma(self, q, fn, r, w, is_out=False):
        r = _keys(r)
        w = _keys(w)
        self._deps(q, r, w)
        i = self.dnext
        self.dnext = (i + 1) % self.NDS
        if self.dval[i] > 0:
            self._wait(q, (i, self.dval[i]))
        ins = fn(self.E[q])
        self.dval[i] += 16
        ins.then_inc(self.dsem[i], 16)
        me = (i, self.dval[i])
        self._commit(me, r, w)
        if is_out:
            self.out_deps.append(me)

    def finish(self):
        for dep in self.out_deps:
            self._wait('sp', dep)

    def mm(self, out, lhsT, rhs, start=True, stop=True, extra_r=()):
        self.op('pe', lambda e: e.matmul(out, lhsT=lhsT, rhs=rhs, start=start, stop=stop),
                [lhsT, rhs] + list(extra_r), [out])

    def tr(self, out, in_, ident):
        self.op('pe', lambda e: e.transpose(out, in_, ident), [in_, ident], [out])

    def act(self, out, in_, func, bias=None, scale=None, eng='act'):
        kw = {}
        rr = [in_]
        if bias is not None:
            kw['bias'] = bias
            if not isinstance(bias, (int, float)):
                rr.append(bias)
        if scale is not None:
            kw['scale'] = scale
            if not isinstance(scale, (int, float)):
                rr.append(scale)
        self.op('act', lambda e: e.activation(out=out, in_=in_, func=func, **kw), rr, [out])

    def copy(self, out, in_, eng='act'):
        if eng == 'act':
            self.op('act', lambda e: e.copy(out=out, in_=in_), [in_], [out])
        else:
            self.op(eng, lambda e: e.tensor_copy(out=out, in_=in_), [in_], [out])

    def tt(self, out, in0, in1, op, eng='dve'):
        self.op(eng, lambda e: e.tensor_tensor(out=out, in0=in0, in1=in1, op=op), [in0, in1], [out])

    def ts(self, out, in0, s1, s2, op0, op1=None, eng='dve'):
        rr = [in0] + [s for s in (s1, s2) if s is not None and not isinstance(s, (int, float))]
        if op1 is None:
            self.op(eng, lambda e: e.tensor_scalar(out=out, in0=in0, scalar1=s1, scalar2=None, op0=op0), rr, [out])
        else:
            self.op(eng, lambda e: e.tensor_scalar(out=out, in0=in0, scalar1=s1, scalar2=s2, op0=op0, op1=op1), rr, [out])

    def stt(self, out, in0, scalar, in1, op0, op1, eng='dve'):
        rr = [in0, in1] + ([] if isinstance(scalar, (int, float)) else [scalar])
        self.op(eng, lambda e: e.scalar_tensor_tensor(out=out, in0=in0, scalar=scalar, in1=in1, op0=op0, op1=op1),
                rr, [out])

    def memset(self, ap, val, eng='dve'):
        self.op(eng, lambda e: e.memset(ap, val), [], [ap])


def _seg2(ap2d, off, stride, n=64):
    pst = ap2d.ap[0][0]
    npart = ap2d.ap[0][1]
    return bass.AP(ap2d.tensor, ap2d.offset + off, [[pst, npart], [stride, 2], [1, n]])


def build_program(nch=SEQ // NCHUNK, do_sample=True, dbg_chunk=None, npool=NPOOL):
    nc = bass.Bass("TRN2", target_bir_lowering=False)

    def din(name, shape, dt=F32):
        return nc.dram_tensor(name, list(shape), dt, kind="ExternalInput").ap()

    def dout(name, shape, dt=F32):
        return nc.dram_tensor(name, list(shape), dt, kind="ExternalOutput").ap()

    def dscr(name, shape, dt=BF16):
        return nc.dram_tensor(name, list(shape), dt, kind="Internal").ap()

    xp = din("xp", [SEQ, D])
    xs = din("xs", [NB, D])
    ck = din("ck", [npool * 128, 512])
    cv = din("cv", [npool * 128, 512])
    cl = din("cl", [npool * 128, 8])
    ptab = din("ptab", [128, NB * 16], I32)
    st_hg = din("st_hg", [NB, 4, 128, 128])
    st_cv = din("st_cv", [NB, 30, D])
    st_ff = din("st_ff", [2, NB, 2, FFN])
    pvec_d = din("pvec", [128, 640])
    cst_d = din("cst", [128, 1024])
    cflag_d = din("cflag", [128, 72])
    wdw_rep = din("wdw_rep", [120, D])
    w_in0 = din("w_in0", [D, IN0])
    w_out0 = din("w_out0", [D, D])
    w_pw1 = din("w_pw1", [D, 2 * D])
    w_pw2 = din("w_pw2", [D, D])
    w_ffi = din("w_ffi", [2, D, 2 * FFN])
    w_ffo = din("w_ffo", [2, FFN, D])
    OWN0 = 4 * NCHUNK
    o_y = dout("o_y", [SEQ - OWN0, D])
    o_ys = dout("o_ys", [NB, D])
    o_fk = dout("o_fk", [SEQ - OWN0, 512])
    o_fv = dout("o_fv", [SEQ - OWN0, 512])
    o_fl = dout("o_fl", [SEQ - OWN0, 8])
    o_fks = dout("o_fks", [NB, 512])
    o_fvs = dout("o_fvs", [NB, 512])
    o_fls = dout("o_fls", [NB, 8])
    o_hg = dout("o_hg", [4, 128, 128])
    o_hgs = dout("o_hgs", [NB, 4, 128, 128])
    o_cv = dout("o_cv", [30, D])
    o_cvs = dout("o_cvs", [NB, 30, D])
    o_ff = dout("o_ff", [2, 2, FFN])
    o_ffs = dout("o_ffs", [2, NB, 2, FFN])
    o_dbg = dout("o_dbg", [8, 8, 128, 512]) if dbg_chunk is not None else None
    s_in0 = dscr("s_in0", [D, IN0])
    s_out0 = dscr("s_out0", [D, D])
    s_pw1 = dscr("s_pw1", [D, 2 * D])
    s_pw2 = dscr("s_pw2", [D, D])
    s_ffi = dscr("s_ffi", [2, D, 2 * FFN])
    s_ffo = dscr("s_ffo", [2, FFN, D])
    s_dcv = dscr("s_dcv", [8, 128, 31 * 128])
    s_k = dscr("s_k", [8, 64, SEQ])
    s_v = dscr("s_v", [8, 128, 32, 64])

    es = ExitStack()
    with es:
        B = Bld(nc, es)
        pv = B.sb([128, 640], F32, "pv_sb")
        cst = B.sb([128, 1024], F32, "cst_sb")
        cflag = B.sb([128, 72], F32, "cflag_sb")
        ident = cst[:, 0:128]
        onesf = cst[:, 128:256]
        tri_le = cst[:, 256:384]
        mask2f = cst[:, 384:512]
        tri_gt = cst[:, 512:640]
        bmask = cst[0:8, 640:1152 - 128] if False else None
        iota_p = cst[:, 640:641]
        epsc = cst[:, 641:642]
        sel4 = cst[0:120, 648:652]
        sel65 = cst[0:65, 656:720]
        cb = B.sb([128, 768], BF16, "cb")
        ident_b = cb[:, 0:128]
        onesm_b = cb[:, 128:256]
        ones128_b = cb[:, 256:384]
        tri_le_b = cb[:, 384:512]
        mask2_b = cb[:, 512:640]
        ones_b = cb[:, 640:768]
        wff = B.sb([128, 8, 8], BF16, "wff")
        xT = [B.sb([128, 512], F32, "xT%d" % i) for i in range(8)]
        hT = [B.sb([128, 512], BF16, "hT%d" % i) for i in range(8)]
        aT = [B.sb([128, 512], BF16, "aT%d" % i) for i in range(NJ)]
        sq = aT[0:8]
        rstd = B.sb([128, 512], F32, "rstd")
        ring = [B.sb([128, RING], BF16, "ring%d" % i) for i in range(4)]
        xtm = [B.sb([128, D], F32, "xtm%d" % i) for i in range(2)]
        PS = [B.psum("ps%d" % i) for i in range(8)]
        psrr = [0]

        def ps_mm():
            i = psrr[0] % 4
            psrr[0] += 1
            return PS[i]

        class WS:
            def __init__(self):
                self.n = 0
                self.loaded = 0
                self.plan = []

            def add(self, tag, src, npart, shape):
                nm = src.tensor.name
                if tag.startswith("dcv"):
                    deps = ["s_dcv#%s" % tag[3:]]
                else:
                    deps = CAST_KEYS[nm + ("#%d" % int(tag[3]) if nm in ("s_ffi", "s_ffo") else "")]
                self.plan.append((tag, src, npart, shape, deps))

            def _issue(self, i):
                tag, src, npart, shape, deps = self.plan[i]
                slot = ring[i % 4]
                sz = int(np.prod(shape))
                dst = slot[0:npart, 0:sz]
                if len(shape) == 2:
                    dst = dst.rearrange("p (a b) -> p a b", b=shape[1])
                B.dma('sp', lambda e: e.dma_start(out=dst, in_=src), deps, [slot])

            def get(self, tag):
                i = self.n
                assert self.plan[i][0] == tag, (self.plan[i][0], tag)
                while self.loaded < min(len(self.plan), i + 2):
                    self._issue(self.loaded)
                    self.loaded += 1
                self.n += 1
                _, src, npart, shape, _d = self.plan[i]
                sz = int(np.prod(shape))
                v = ring[i % 4][0:npart, 0:sz]
                if len(shape) == 2:
                    v = v.rearrange("p (a b) -> p a b", b=shape[1])
                return v

        CAST_KEYS = {}

        def _ck(name, rows, rstep=256):
            CAST_KEYS[name] = ["%s#r%d" % (name, r0) for r0 in range(0, rows, rstep)]

        _ck("s_in0", D)
        _ck("s_out0", D)
        _ck("s_pw1", D)
        _ck("s_pw2", D)
        for l_ in range(2):
            _ck("s_ffi#%d" % l_, D)
            _ck("s_ffo#%d" % l_, FFN)
        W = WS()
        in0v = s_in0.rearrange("(c p) n -> p c n", p=128)
        pw1v = s_pw1.rearrange("(c p) n -> p c n", p=128)
        pw2v = s_pw2.rearrange("(c p) n -> p c n", p=128)

        def plan_chunk(sample, mode='own'):
            if mode == 'partial':
                for i in (1, 2, 5, 6):
                    W.add("in%d" % i, in0v[:, :, i * 512:(i + 1) * 512], 128, [8, 512])
                return
            for i in range(7):
                W.add("in%d" % i, in0v[:, :, i * 512:(i + 1) * 512], 128, [8, 512])
            W.add("wo_a", s_out0[0:512, :].rearrange("(c p) n -> p c n", p=128), 128, [4, 1024])
            if sample:
                W.add("wo_c", s_out0[512:1024, :].rearrange("(c p) n -> p c n", p=128), 128, [4, 1024])
            else:
                for hh in range(2):
                    W.add("wo_b%d" % hh, s_out0[512:1024, hh * 512:(hh + 1) * 512].rearrange("(h p) n -> p h n", p=64),
                          64, [8, 512])
            plan_ffn(0)
            for i in range(2):
                W.add("pw1a%d" % i, pw1v[:, :, i * 512:(i + 1) * 512], 128, [8, 512])
                W.add("pw1g%d" % i, pw1v[:, :, 1024 + i * 512:1024 + (i + 1) * 512], 128, [8, 512])
            for c in range(8):
                W.add("dcv%d" % c, s_dcv[c], 128, [31 * 128])
            for i in range(2):
                W.add("pw2_%d" % i, pw2v[:, :, i * 512:(i + 1) * 512], 128, [8, 512])
            plan_ffn(1)

        def plan_ffn(l):
            fi = s_ffi[l].rearrange("(c p) n -> p c n", p=128)
            fo = s_ffo[l].rearrange("(j p) n -> p j n", p=128)
            for g in range(6):
                wd = 512 if g < 5 else 256
                W.add("ffg%d_%d" % (l, g), fi[:, :, g * 512:g * 512 + wd], 128, [8, wd])
                W.add("ffu%d_%d" % (l, g), fi[:, :, FFN + g * 512:FFN + g * 512 + wd], 128, [8, wd])
            for m in range(8):
                W.add("ffo%d_%d" % (l, m), fo[:, :, m * 128:(m + 1) * 128], 128, [NJ, 128])

        NCH = nch
        if nch == 8:
            SCHED = [(0, 512, 'partial'), (512, 512, 'partial'), (1024, 512, 'partial'), (1536, 384, 'partial'),
                     (1920, 128, 'halo')] + [(2048 + 512 * i, 512, 'own') for i in range(4)]
        else:
            SCHED = [(512 * i, 512, 'own') for i in range(nch)]
        for (_t, _n, _m) in SCHED:
            plan_chunk(False, _m)
        if do_sample:
            plan_chunk(True)

        B.dma('sp', lambda e: e.dma_start(out=pv[:, :], in_=pvec_d), [], [pv])
        B.dma('sp', lambda e: e.dma_start(out=cst[:, :], in_=cst_d), [], [cst])
        B.dma('sp', lambda e: e.dma_start(out=cflag[:, :], in_=cflag_d), [], [cflag])
        B.ts(cb[:, 0:128], ident, 1.0, None, ALU.mult)
        B.ts(cb[:, 128:256], onesf, 1.0 / 1024.0, None, ALU.mult)
        B.ts(cb[:, 256:384], onesf, 1.0 / 128.0, None, ALU.mult)
        B.ts(cb[:, 384:512], tri_le, 1.0, None, ALU.mult)
        B.ts(cb[:, 512:640], mask2f, 1.0, None, ALU.mult)
        B.ts(cb[:, 640:768], onesf, 1.0, None, ALU.mult)

        B.dma('pool', lambda e: e.dma_start(out=wff[:, :, :], in_=w_in0[:, 3584:3592].rearrange("(c p) n -> p c n", p=128)),
              [], [wff])
        PV_NM, PV_NF, PV_NFIN = 0, 16, 32
        PV_LB, PV_GN = 40, 48
        PV_BPW1, PV_BDW, PV_LNG, PV_LNB, PV_BPW2 = 52, 68, 76, 84, 92
        PV_BFF = 100
        PV_WFF = 144
        PV_WDW = 276
        PV_FB = 524
        lbt = B.sb([128, 8], F32, "lbt")
        B.tt(lbt[:, 0:4], pv[:, PV_LB:PV_LB + 4], pv[:, PV_LB + 4:PV_LB + 8], ALU.subtract)
        B.act(lbt[:, 0:4], lbt[:, 0:4], AF.Sigmoid)
        B.ts(lbt[:, 4:8], lbt[:, 0:4], -1.0, 1.0, ALU.mult, ALU.add)

        for c in range(8):
            t = ring[c % 4]
            for j in range(31):
                B.ts(t[:, j * 128:(j + 1) * 128], ident, pv[:, PV_WDW + c * 31 + j:PV_WDW + c * 31 + j + 1], None, ALU.mult)
            B.dma('pool', lambda e, t=t, c=c: e.dma_start(out=s_dcv[c], in_=t[:, 0:31 * 128]), [t], ["s_dcv#%d" % c])

        def cast_w(dst, src, rows, name, rstep=256):
            for r0 in range(0, rows, rstep):
                r1 = min(rows, r0 + rstep)
                B.dma('pool', lambda e, r0=r0, r1=r1: e.dma_start(out=dst[r0:r1, :], in_=src[r0:r1, :]), [],
                      ["%s#r%d" % (name, r0)])

        cast_w(s_in0, w_in0, D, "s_in0")
        cast_w(s_out0, w_out0, D, "s_out0")
        cast_w(s_ffi[0], w_ffi[0], D, "s_ffi#0")
        cast_w(s_ffo[0], w_ffo[0], FFN, "s_ffo#0")
        cast_w(s_pw1, w_pw1, D, "s_pw1")
        cast_w(s_pw2, w_pw2, D, "s_pw2")
        cast_w(s_ffi[1], w_ffi[1], D, "s_ffi#1")
        cast_w(s_ffo[1], w_ffo[1], FFN, "s_ffo#1")

        def norm_sq(m, N):
            B.act(sq2[m][:, :N], xT[m][:, :N], AF.Square)

        def rmsnorm(gcol, N, outs=None, out_f32=False):
            ps = ps_mm()
            for kc in range(8):
                B.mm(ps[:, :N], onesm_b, sq2[kc][:, :N], start=(kc == 0), stop=(kc == 7))
            B.act(rstd[:, :N], ps[:, :N], AF.Ln, bias=epsc)
            B.act(rstd[:, :N], rstd[:, :N], AF.Exp, scale=-0.5)
            for kc in range(8):
                o = xT[kc] if out_f32 else hT[kc]
                B.stt(o[:, :N], xT[kc][:, :N], pv[:, gcol + kc:gcol + kc + 1], rstd[:, :N], ALU.mult, ALU.mult)

        def fm_mm(ps_ap, slab, c0, m, N, src=None):
            src = src or hT
            for kc in range(8):
                B.mm(ps_ap, slab[:, kc, c0:c0 + m], src[kc][:, :N], start=(kc == 0), stop=(kc == 7))

        def tm_mm(ps_ap, slab, c0, ncol, t0, nt):
            for kc in range(8):
                B.mm(ps_ap, hT[kc][:, t0:t0 + nt], slab[:, kc, c0:c0 + ncol], start=(kc == 0), stop=(kc == 7))

        def resid_add(m, ps, N, bias=None):
            if bias is None:
                B.tt(xT[m][:, :N], ps[:, :N], xT[m][:, :N], ALU.add)
            else:
                B.stt(xT[m][:, :N], ps[:, :N], bias, xT[m][:, :N], ALU.add, ALU.add)
            norm_sq(m, N)

        trc = [0]

        def fm_to_dram(srcs, n, dram_rows, q='pool'):
            nck = len(srcs)
            for b0 in range(0, nck, 8):
                st = xtm[trc[0] % 2]
                trc[0] += 1
                bn = min(8, nck - b0)
                for g0 in range(0, bn, 4):
                    ps = ps_mm()
                    gn = min(4, bn - g0)
                    for i in range(gn):
                        B.tr(ps[0:n, i * 128:(i + 1) * 128], srcs[b0 + g0 + i], ident)
                    B.copy(st[0:n, g0 * 128:(g0 + gn) * 128], ps[0:n, 0:gn * 128])
                B.dma(q, lambda e, st=st, b0=b0, bn=bn: e.dma_start(out=dram_rows[:, b0 * 128:(b0 + bn) * 128],
                                                                 in_=st[0:n, 0:bn * 128]),
                      [st], [dram_rows], is_out=True)

        q32 = [B.sb([128, 512], F32, "q32_%d" % h) for h in range(4)]
        frt = [B.sb([128, 512], F32, "fr_%d" % h) for h in range(4)]
        gate = [B.sb([128, 512], BF16, "gate_%d" % h) for h in range(4)]
        vhg = [B.sb([128, 512], BF16, "vhg_%d" % j) for j in range(4)]
        cat_hg = [B.sb([128, 512], BF16, "cathg_%d" % h) for h in range(4)]
        sq2 = gate + vhg
        cat_fx = [B.sb([64, 512], BF16, "catfx_%d" % h) for h in range(8)]
        Qp = [B.sb([65, 512], BF16, "Qp_%d" % h) for h in range(8)]
        S32 = [B.sb([128, 128], F32, "S32_%d" % h) for h in range(4)]
        Sbf = [B.sb([128, 128], BF16, "Sbf_%d" % h) for h in range(4)]
        for h in range(4):
            B.memset(S32[h][:, :], 0.0)
            B.memset(Sbf[h][:, :], 0.0, eng='pool')
        tA = B.sb([128, 512], F32, "tA")
        tB = B.sb([128, 512], F32, "tB")
        tC = B.sb([128, 512], F32, "tC")
        nQt_s = [B.sb([128, 512], BF16, "nQt%d" % i) for i in range(2)]
        nKt_s = [B.sb([128, 512], BF16, "nKt%d" % i) for i in range(2)]
        Qb_s = [B.sb([128, 512], BF16, "Qb%d" % i) for i in range(2)]
        nKtm_s = [[B.sb([128, 128], BF16, "nKtm%d_%d" % (i, j)) for j in range(4)] for i in range(2)]
        csm_s = [B.sb([128, 64], F32, "csm%d" % i) for i in range(2)]
        attm = [B.sb([128, 128], BF16, "attm%d" % j) for j in range(2)]
        kvt = [B.sb([128, 128], F32, "kvt%d" % j) for j in range(2)]
        carry = B.sb([128, 8], F32, "carry")
        cref = B.sb([128, 8], F32, "cref")
        negc = B.sb([128, 32, 8], F32, "negc")
        biasc = B.sb([128, 32, 8], F32, "biasc")
        flf = [B.sb([128, 8], F32, "flf%d" % j) for j in range(4)]
        ctm = [B.sb([128, 8], F32, "ctm%d" % j) for j in range(2)]
        Zr = [B.sb([128, 8, 65], BF16, "Zr%d" % j) for j in range(4)]
        for j in range(4):
            B.memset(Zr[j][:, :, :], 0.0, eng='pool')
        B.memset(carry[:, :], 0.0)
        kst = [B.sb([64, 512], BF16, "kst%d" % i) for i in range(2)]
        vst = [B.sb([128, 512], BF16, "vst%d" % i) for i in range(2)]
        ost = [B.sb([128, 512], F32, "ost%d" % i) for i in range(2)]
        ostc = [0]
        Kbuf = [B.sb([65, SEQ], BF16, "Kbuf%d" % i) for i in range(1)]
        Vbuf = [B.sb([128, 32 * 65], BF16, "Vbuf%d" % i) for i in range(1)]
        for i in range(1):
            B.memset(Kbuf[i][64:65, :], 1.0, eng='pool')
            B.memset(Vbuf[i][:, :].rearrange("p (k d) -> p k d", d=65)[:, :, 64:65], 1.0, eng='pool')
        PT = [B.sb([128, 512], BF16, "PT%d" % i) for i in range(2)]
        rden = B.sb([64, 512], F32, "rden")
        gbuf = [B.sb([128, 514], F32, "gbuf%d" % i) for i in range(2)]
        gcar = [B.sb([128, NJ, 2], F32, "gcar%d" % l) for l in range(2)]
        for l in range(2):
            B.memset(gcar[l][:, :, :], 0.0, eng='pool')
        ubuf = [B.sb([128, 542], BF16, "ubuf%d" % c) for c in range(8)]
        for c in range(8):
            B.memset(ubuf[c][:, 0:30], 0.0, eng='pool')
        u32 = B.sb([128, 8, 32], F32, "u32")
        mean_t = B.sb([128, 512], F32, "mean_t")

        def hfs_ap(l, j):
            idx = l * NJ + j
            return ubuf[idx // 8][:, 0:512].bitcast(F32)[:, (idx % 8) * 32:(idx % 8) * 32 + 32]

        def ost_next():
            t = ost[ostc[0] % 2]
            ostc[0] += 1
            return t

        cur = {"ci": -1}

        def dump(i, tiles=None, k0=0):
            if dbg_chunk is None or cur["ci"] != dbg_chunk:
                return
            tiles = tiles or xT
            for kc, t in enumerate(tiles):
                tv = t if hasattr(t, "tensor") else t[:, :]
                npart = tv.ap[0][1]
                ncol = tv.ap[-1][1]
                B.dma('pool', lambda e, tv=tv, kc=kc, npart=npart, ncol=ncol: e.dma_start(
                    out=o_dbg[i, k0 + kc, 0:npart, 0:ncol], in_=tv), [tv], [o_dbg], is_out=True)

        def load_x(src_rows, N):
            nt = max(1, N // 128)
            rows = min(N, 128)
            for g in range(0, nt, 2):
                for j in range(g, min(nt, g + 2)):
                    t = xtm[j % 2]
                    B.dma('sp', lambda e, t=t, j=j: e.dma_start(out=t[0:rows, :], in_=src_rows[j * 128:j * 128 + rows, :]),
                          [src_rows], [t])
                for kc in range(8):
                    ps = ps_mm()
                    n2 = min(nt, g + 2) - g
                    for jj in range(n2):
                        B.tr(ps[:, jj * 128:jj * 128 + rows], xtm[(g + jj) % 2][0:rows, kc * 128:(kc + 1) * 128],
                             ident[0:rows, 0:rows])
                    B.copy(xT[kc][:, g * 128:g * 128 + (n2 - 1) * 128 + rows], ps[:, 0:(n2 - 1) * 128 + rows])
            for kc in range(8):
                norm_sq(kc, N)

        def store_y(dst_rows, N):
            nt = max(1, N // 128)
            rows = min(N, 128)
            for j in range(nt):
                t = xtm[j % 2]
                for g in range(2):
                    ps = ps_mm()
                    for i in range(4):
                        kc = g * 4 + i
                        B.tr(ps[0:rows, i * 128:(i + 1) * 128], xT[kc][:, j * 128:j * 128 + rows], ident)
                    B.copy(t[0:rows, g * 512:(g + 1) * 512], ps[0:rows, :])
                B.dma('pool', lambda e, t=t, j=j: e.dma_start(out=dst_rows[j * 128:j * 128 + rows, :], in_=t[0:rows, :]),
                      [t], [dst_rows], is_out=True)

        def ffn(l, N, sample=False, last=False, halo=False):
            rmsnorm(PV_NF + 8 * l, N)
            for g in range(6):
                nj = 4 if g < 5 else 2
                G = W.get("ffg%d_%d" % (l, g))
                U = W.get("ffu%d_%d" % (l, g))
                for jj in range(nj):
                    j = 4 * g + jj
                    gps = ps_mm()
                    ups = ps_mm()
                    fm_mm(gps[:, :N], G, jj * 128, 128, N)
                    fm_mm(ups[:, :N], U, jj * 128, 128, N)
                    gb = gbuf[j % 2]
                    acc = tA if j % 2 == 0 else tB
                    w0 = pv[:, PV_WFF + (l * NJ + j) * 3 + 0:PV_WFF + (l * NJ + j) * 3 + 1]
                    w1 = pv[:, PV_WFF + (l * NJ + j) * 3 + 1:PV_WFF + (l * NJ + j) * 3 + 2]
                    w2 = pv[:, PV_WFF + (l * NJ + j) * 3 + 2:PV_WFF + (l * NJ + j) * 3 + 3]
                    bj = pv[:, PV_BFF + l * NJ + j:PV_BFF + l * NJ + j + 1]
                    if not sample:
                        B.copy(gb[:, 0:2], gcar[l][:, j, :], eng='pool')
                        B.copy(gb[:, 2:N + 2], gps[:, :N])
                        B.copy(gcar[l][:, j, :], gb[:, N:N + 2], eng='pool')
                        B.ts(acc[:, :N], gb[:, 0:N], w0, None, ALU.mult)
                        B.stt(acc[:, :N], gb[:, 1:N + 1], w1, acc[:, :N], ALU.mult, ALU.add)
                        B.stt(acc[:, :N], gb[:, 2:N + 2], w2, acc[:, :N], ALU.mult, ALU.add)
                    else:
                        hv = hfs_ap(l, j).rearrange("p (b r) -> p b r", r=2)
                        B.copy(gb[:, 0:N], gps[:, :N])
                        B.ts(acc[:, :N], hv[:, :, 0], w0, None, ALU.mult)
                        B.stt(acc[:, :N], hv[:, :, 1], w1, acc[:, :N], ALU.mult, ALU.add)
                        B.stt(acc[:, :N], gb[:, 0:N], w2, acc[:, :N], ALU.mult, ALU.add)
                        B.op('dve', lambda e, hv=hv, gb=gb: e.tensor_copy(out=hv[:, :, 0], in_=gb[:, 0:N]), [gb], [hv])
                    ge = tC
                    B.act(ge[:, :N], acc[:, :N], AF.Gelu, bias=bj)
                    B.tt(aT[j][:, :N], ge[:, :N], ups[:, :N], ALU.mult)
            for m in range(8):
                Wo = W.get("ffo%d_%d" % (l, m))
                ps = ps_mm()
                for j in range(NJ):
                    B.mm(ps[:, :N], Wo[:, j, :], aT[j][:, :N], start=(j == 0), stop=(j == NJ - 1))
                resid_add(m, ps, N)
            if halo and l == 1:
                B.ts(gcar[1][:, :, :], gcar[1][:, :, :], cflag[:, 64:65], None, ALU.mult)
            if last and not sample:
                fm_to_dram([gcar[l][:, j, :] for j in range(NJ)], 2, o_ff[l])
            if sample:
                B.dma('pool', lambda e: e.dma_start(out=o_ffs[l, :, 0, :], in_=st_ff[l, :, 1, :]), [], [o_ffs], is_out=True)
                fm_to_dram([hfs_ap(l, j).rearrange("p (b r) -> p b r", r=2)[:, :, 0] for j in range(NJ)], NB,
                           o_ffs[l, :, 1, :])

        def conformer(N, sample=False, last=False, halo=False):
            rmsnorm(PV_NM + 8, N)
            for i in range(2):
                A = W.get("pw1a%d" % i)
                G = W.get("pw1g%d" % i)
                for cc in range(4):
                    c = 4 * i + cc
                    aps = ps_mm()
                    gps = ps_mm()
                    fm_mm(aps[:, :N], A, cc * 128, 128, N)
                    fm_mm(gps[:, :N], G, cc * 128, 128, N)
                    B.act(tA[:, :N], gps[:, :N], AF.Sigmoid, bias=pv[:, PV_BPW1 + 8 + c:PV_BPW1 + 8 + c + 1])
                    if not sample:
                        B.stt(ubuf[c][:, 30:30 + N], aps[:, :N], pv[:, PV_BPW1 + c:PV_BPW1 + c + 1], tA[:, :N], ALU.add, ALU.mult)
                        if last:
                            B.stt(u32[:, c, 0:30], aps[:, N - 30:N], pv[:, PV_BPW1 + c:PV_BPW1 + c + 1], tA[:, N - 30:N],
                                  ALU.add, ALU.mult)
                    else:
                        B.stt(u32[:, c, 0:N], aps[:, :N], pv[:, PV_BPW1 + c:PV_BPW1 + c + 1], tA[:, :N], ALU.add, ALU.mult)
            dps_list = []
            if not sample:
                for c in range(8):
                    Dg = W.get("dcv%d" % c)
                    ps = PS[4 + c % 4]
                    for j in range(31):
                        B.mm(ps[:, :N], Dg[:, j * 128:(j + 1) * 128], ubuf[c][:, j:j + N], start=(j == 0), stop=(j == 30))
                    dsb = dbuf[c]
                    B.ts(dsb[:, :N], ps[:, :N], pv[:, PV_BDW + c:PV_BDW + c + 1], None, ALU.add)
                    if halo:
                        B.ts(ubuf[c][:, 0:30], ubuf[c][:, N:N + 30], cflag[:, 64:65], None, ALU.mult, eng='pool')
                    else:
                        B.copy(ubuf[c][:, 0:30], ubuf[c][:, N:N + 30], eng='pool')
            else:
                for c in range(8):
                    W.get("dcv%d" % c)
                hps = [PS[4], PS[5]]
                for hh_ in range(2):
                    B.dma('sp', lambda e, hh_=hh_: e.dma_start(out=gbuf[hh_][0:120, 0:512], in_=wdw_rep[:, hh_ * 512:(hh_ + 1) * 512]),
                          [], [gbuf[hh_]])
                for tl in range(4):
                    t = xtm[tl % 2]
                    B.dma('sp', lambda e, t=t, tl=tl: e.dma_start(
                        out=t[0:120, :], in_=st_cv[4 * tl:4 * tl + 4].rearrange("b j c -> (b j) c")), [], [t])
                    for hh_ in range(2):
                        B.tt(t[0:120, hh_ * 512:(hh_ + 1) * 512], t[0:120, hh_ * 512:(hh_ + 1) * 512], gbuf[hh_][0:120, 0:512], ALU.mult)
                    for c in range(8):
                        B.mm(hps[c // 4][:, (c % 4) * 16 + 4 * tl:(c % 4) * 16 + 4 * tl + 4], t[0:120, c * 128:(c + 1) * 128],
                             sel4, start=True, stop=True)
                for c in range(8):
                    w30 = pv[:, PV_WDW + c * 31 + 30:PV_WDW + c * 31 + 31]
                    B.stt(dbuf[c][:, :N], u32[:, c, 0:N], w30, hps[c // 4][:, (c % 4) * 16:(c % 4) * 16 + 16], ALU.mult, ALU.add)
                    B.ts(dbuf[c][:, :N], dbuf[c][:, :N], pv[:, PV_BDW + c:PV_BDW + c + 1], None, ALU.add)
            mps = ps_mm()
            qps = ps_mm()
            for c in range(8):
                B.copy(hT[c][:, :N], dbuf[c][:, :N], eng='pool')
                B.act(sq[c][:, :N], dbuf[c][:, :N], AF.Square)
            for c in range(8):
                B.mm(mps[:, :N], onesm_b, hT[c][:, :N], start=(c == 0), stop=(c == 7))
            for c in range(8):
                B.mm(qps[:, :N], onesm_b, sq[c][:, :N], start=(c == 0), stop=(c == 7))
            B.copy(mean_t[:, :N], mps[:, :N])
            B.tt(tA[:, :N], mean_t[:, :N], mean_t[:, :N], ALU.mult)
            B.tt(tA[:, :N], qps[:, :N], tA[:, :N], ALU.subtract)
            B.act(rstd[:, :N], tA[:, :N], AF.Ln, bias=epsc)
            B.act(rstd[:, :N], rstd[:, :N], AF.Exp, scale=-0.5)
            for c in range(8):
                t = tB if c % 2 == 0 else tC
                B.tt(t[:, :N], dbuf[c][:, :N], mean_t[:, :N], ALU.subtract)
                B.tt(t[:, :N], t[:, :N], rstd[:, :N], ALU.mult)
                B.act(hT[c][:, :N], t[:, :N], AF.Silu, bias=pv[:, PV_LNB + c:PV_LNB + c + 1],
                      scale=pv[:, PV_LNG + c:PV_LNG + c + 1])
            for i in range(2):
                W2 = W.get("pw2_%d" % i)
                for mm_ in range(4):
                    m = 4 * i + mm_
                    ps = ps_mm()
                    fm_mm(ps[:, :N], W2, mm_ * 128, 128, N)
                    resid_add(m, ps, N, bias=pv[:, PV_BPW2 + m:PV_BPW2 + m + 1])
            if last and not sample:
                fm_to_dram([u32[:, c, 0:30] for c in range(8)], 30, o_cv)
            if sample:
                B.dma('pool', lambda e: e.dma_start(out=o_cvs[:, 0:29, :], in_=st_cv[:, 1:30, :]), [], [o_cvs], is_out=True)
                fm_to_dram([u32[:, c, 0:NB] for c in range(8)], NB, o_cvs[:, 29, :])

        dbuf = q32 + frt

        def hgrn_common(h, N, o_ps):
            B.act(tA[:, :N], o_ps[:, :N], AF.Square)
            B.copy(PT[0][:, :N], tA[:, :N], eng='pool')
            ms = ps_mm()
            B.mm(ms[:, :N], ones128_b, PT[0][:, :N])
            B.act(tB[:, :N], ms[:, :N], AF.Ln, bias=epsc)
            B.act(tB[:, :N], tB[:, :N], AF.Exp, scale=-0.5)
            B.stt(tA[:, :N], o_ps[:, :N], pv[:, PV_GN + h:PV_GN + h + 1], tB[:, :N], ALU.mult, ALU.mult)
            B.tt(cat_hg[h][:, :N], tA[:, :N], gate[h][:, :N], ALU.mult)

        def mixer_prompt(t0, N, mode='own', last=False):
            NT = N // 128
            NC = N // 64
            kb0 = t0 // 128
            part = (mode == 'partial')
            wr_out = (mode == 'own')
            to = t0 - (OWN0 if nch == 8 else 0)
            rmsnorm(PV_NM, N)
            nkb = kb0 + NT
            B.copy(cref[:, :], carry[:, :], eng='dve')
            for j in range(NT):
                ps = ps_mm()
                for kc in range(8):
                    B.mm(ps[:, 0:8], hT[kc][:, j * 128:(j + 1) * 128], wff[:, kc, :], start=(kc == 0), stop=(kc == 7))
                B.tt(flf[j][:, :], ps[:, 0:8], pv[:, PV_FB:PV_FB + 8], ALU.add)
                B.act(flf[j][:, :], flf[j][:, :], AF.Sigmoid)
                B.act(flf[j][:, :], flf[j][:, :], AF.Ln)
                if wr_out:
                    B.dma('pool', lambda e, j=j: e.dma_start(out=o_fl[to + j * 128:to + (j + 1) * 128, :], in_=flf[j][:, :]),
                          [flf[j]], [o_fl], is_out=True)
            if not part:
                s0 = W.get("in0")
                for h in range(4):
                    ps = ps_mm()
                    fm_mm(ps[:, :N], s0, h * 128, 128, N)
                    B.act(q32[h][:, :N], ps[:, :N], AF.Silu)
            for j in range(NT):
                ps2 = ps_mm()
                B.mm(ps2[:, 0:8], tri_le, flf[j][:, :])
                B.mm(ps2[:, 8:16], onesf, flf[j][:, :])
                ct = ctm[j % 2]
                B.tt(ct[:, :], ps2[:, 0:8], carry[:, :], ALU.add)
                kb = kb0 + j
                B.ts(negc[:, kb, :], ct[:, :], -1.0, None, ALU.mult)
                B.tt(ct[:, :], ct[:, :], cref[:, :], ALU.subtract)
                B.ts(Zr[j][:, :, 64], ct[:, :], 8.0, None, ALU.mult)
                B.tt(carry[:, :], carry[:, :], ps2[:, 8:16], ALU.add)
            if not part:
                B.tt(biasc[:, 0:nkb, :], negc[:, 0:nkb, :], cref[:, :].unsqueeze(1).to_broadcast([128, nkb, 8]), ALU.add)
                if nch == 8:
                    mrow = 0 if mode == 'halo' else 32
                    B.tt(biasc[:, 0:nkb, :], biasc[:, 0:nkb, :],
                         cflag[:, mrow:mrow + nkb].unsqueeze(2).to_broadcast([128, nkb, 8]), ALU.add)
            s1 = W.get("in1")
            for h in range(4):
                ps = ps_mm()
                fm_mm(ps[:, :N], s1, h * 128, 128, N)
                B.act(tA[:, :N], ps[:, :N], AF.Sigmoid)
                B.ts(frt[h][:, :N], tA[:, :N], lbt[:, 4 + h:5 + h], lbt[:, h:h + 1], ALU.mult, ALU.add)

            def prep(h):
                st = h % 2
                nQt, nKt, Qb, csm, nKtm = nQt_s[st], nKt_s[st], Qb_s[st], csm_s[st], nKtm_s[st]
                lf = tA
                B.act(lf[:, :N], frt[h][:, :N], AF.Ln)
                Cs = tB
                B.op('dve', lambda e: e.tensor_tensor_scan(out=Cs[:, :N], data0=onesf[:, 0:1].to_broadcast([128, N]), data1=lf[:, :N],
                                                           initial=0.0, op0=ALU.mult, op1=ALU.add), [cst, lf], [Cs])
                C3 = Cs[:, :N].rearrange("p (n t) -> p n t", t=64)
                D1 = tC
                D3 = D1[:, :N].rearrange("p (n t) -> p n t", t=64)
                B.tt(D3, C3, C3[:, :, 32:33].to_broadcast([128, NC, 64]), ALU.subtract)
                B.memset(csm[:, 0:1], 0.0)
                if NC > 1:
                    B.copy(csm[:, 1:NC], C3[:, 0:NC - 1, 63], eng='dve')
                B.tt(csm[:, 8:8 + NC], C3[:, :, 32], csm[:, 0:NC], ALU.subtract)
                B.tt(csm[:, 16:16 + NC], C3[:, :, 63], csm[:, 0:NC], ALU.subtract)
                B.tt(csm[:, 48:48 + NC], csm[:, 16:16 + NC], csm[:, 8:8 + NC], ALU.subtract)
                E1 = tA
                B.act(E1[:, :N], D1[:, :N], AF.Exp)
                B.act(csm[:, 24:24 + NC], csm[:, 8:8 + NC], AF.Exp)
                B.act(csm[:, 32:32 + NC], csm[:, 16:16 + NC], AF.Exp)
                B.act(csm[:, 40:40 + NC], csm[:, 48:48 + NC], AF.Exp)
                E3 = tB
                B.act(E3[:, :N], D1[:, :N], AF.Exp, scale=-1.0)
                if not part:
                    B.stt(nQt[:, :N], q32[h][:, :N], -1.0, E1[:, :N], ALU.mult, ALU.mult)
                B.ts(csm[:, 24:24 + NC], csm[:, 24:24 + NC], -1.0, None, ALU.mult)
                B.ts(csm[:, 40:40 + NC], csm[:, 40:40 + NC], -1.0, None, ALU.mult)
                B.stt(nKt[:, :N], frt[h][:, :N], 1.0, E3[:, :N], ALU.subtract, ALU.mult)
                if not part:
                    B.tt(Qb[:, :N].rearrange("p (n t) -> p n t", t=64), nQt[:, :N].rearrange("p (n t) -> p n t", t=64),
                         csm[:, 24:24 + NC].unsqueeze(2).to_broadcast([128, NC, 64]), ALU.mult)
                pst = ps_mm()
                pstb = pst[:, :].bitcast(BF16)
                for j in range(NT):
                    B.tr(pstb[:, j * 128:(j + 1) * 128], nKt[:, j * 128:(j + 1) * 128], ident_b)
                for j in range(NT):
                    B.copy(nKtm[j][:, :], pstb[:, j * 128:(j + 1) * 128])

            prep(0)
            s2 = W.get("in2")
            for j in range(NT):
                ps = ps_mm()
                tm_mm(ps[:, :], s2, 0, 512, j * 128, 128)
                B.copy(vhg[j][:, :], ps[:, :])
            prep(1)
            if not part:
                s3 = W.get("in3")
                for h in range(4):
                    ps = ps_mm()
                    fm_mm(ps[:, :N], s3, h * 128, 128, N)
                    B.act(gate[h][:, :N], ps[:, :N], AF.Silu)
                s4 = W.get("in4")
                for h in range(8):
                    ps = ps_mm()
                    for j in range(NT):
                        B.mm(ps[0:65, j * 128:(j + 1) * 128], Zr[j][:, h, :], ident_b, start=True, stop=False)
                    for kc in range(8):
                        B.mm(ps[0:64, :N], s4[:, kc, h * 64:(h + 1) * 64], hT[kc][:, :N], start=False, stop=(kc == 7))
                    B.copy(Qp[h][:, :N], ps[0:65, :N])
            s5 = W.get("in5")
            for h in range(8):
                ps = ps_mm()
                fm_mm(ps[0:64, :N], s5, h * 64, 64, N)
                ks = kst[h % 2]
                B.copy(ks[:, :N], ps[0:64, :N])
                B.dma('pool', lambda e, ks=ks, h=h: e.dma_start(out=s_k[h, :, t0:t0 + N], in_=ks[:, :N]), [ks], [s_k])
            for j in range(NT if wr_out else 0):
                ps = ps_mm()
                tm_mm(ps[:, :], s5, 0, 512, j * 128, 128)
                o = ost_next()
                B.copy(o[:, :], ps[:, :])
                B.dma('pool', lambda e, o=o, j=j: e.dma_start(out=o_fk[to + j * 128:to + (j + 1) * 128, :], in_=o[:, :]),
                      [o], [o_fk], is_out=True)
            s6 = W.get("in6")
            for j in range(NT):
                ps = ps_mm()
                tm_mm(ps[:, :], s6, 0, 512, j * 128, 128)
                if wr_out:
                    o = ost_next()
                    B.copy(o[:, :], ps[:, :])
                    B.dma('pool', lambda e, o=o, j=j: e.dma_start(out=o_fv[to + j * 128:to + (j + 1) * 128, :], in_=o[:, :]),
                          [o], [o_fv], is_out=True)
                vs = vst[j % 2]
                B.copy(vs[:, :], ps[:, :], eng='dve')
                kb = kb0 + j
                B.dma('pool', lambda e, vs=vs, kb=kb: e.dma_start(
                    out=s_v[:, :, kb, :].rearrange("h p d -> p h d"), in_=vs[:, :].rearrange("p (h d) -> p h d", d=64)),
                    [vs], [s_v])
            for h in range(4):
                st = h % 2
                nQt, nKt, Qb, csm, nKtm = nQt_s[st], nKt_s[st], Qb_s[st], csm_s[st], nKtm_s[st]
                o_ps = PS[6 + h % 2]
                for j in range(NT):
                    if not part:
                        aps = ps_mm()
                        B.mm(aps[:, 0:128], nKt[:, j * 128:(j + 1) * 128], nQt[:, j * 128:(j + 1) * 128])
                        am = attm[j % 2]
                        B.tt(am[:, :], aps[:, 0:128], mask2f, ALU.mult)
                        B.mm(o_ps[:, j * 128:(j + 1) * 128], vhg[j][:, h * 128:(h + 1) * 128], am[:, :], start=True, stop=False)
                    for hf in range(2):
                        n = 2 * j + hf
                        if not part:
                            B.mm(o_ps[:, n * 64:(n + 1) * 64], Sbf[h][:, :], Qb[:, n * 64:(n + 1) * 64], start=False, stop=True)
                        kps = ps_mm()
                        B.mm(kps[:, 0:128], nKtm[j][hf * 64:(hf + 1) * 64, :], vhg[j][hf * 64:(hf + 1) * 64, h * 128:(h + 1) * 128])
                        kt = kvt[n % 2]
                        B.ts(kt[:, :], kps[:, 0:128], csm[:, 40 + n:41 + n], None, ALU.mult)
                        B.stt(S32[h][:, :], S32[h][:, :], csm[:, 32 + n:33 + n], kt[:, :], ALU.mult, ALU.add)
                        B.copy(Sbf[h][:, :], S32[h][:, :], eng='pool')
                if not part:
                    hgrn_common(h, N, o_ps)
                if last:
                    B.dma('pool', lambda e, h=h: e.dma_start(out=o_hg[h], in_=S32[h][:, :]), [S32[h]], [o_hg], is_out=True)
                if h + 2 < 4:
                    prep(h + 2)
            if part:
                return
            for h in range(8):
                Kb = Kbuf[0]
                Vb = Vbuf[0]
                B.dma('sp', lambda e, Kb=Kb, h=h: e.dma_start(out=Kb[0:64, 0:nkb * 128], in_=s_k[h, :, 0:nkb * 128]), [s_k], [Kb])
                B.dma('sp', lambda e, Vb=Vb, h=h: e.dma_start(out=Vb[:, :].rearrange("p (k d) -> p k d", d=65)[:, 0:nkb, 0:64],
                                                          in_=s_v[h, :, 0:nkb, :]), [s_v], [Vb])
                O_ps = PS[6 + h % 2]
                for kb in range(nkb):
                    jd = kb - kb0
                    c0 = max(0, jd) * 128
                    S_ps = PS[4 + kb % 2]
                    B.mm(S_ps[:, c0:N], Kb[0:65, kb * 128:(kb + 1) * 128], Qp[h][0:65, c0:N])
                    P = PT[kb % 2]
                    B.act(P[:, c0:N], S_ps[:, c0:N], AF.Exp, bias=biasc[:, kb, h:h + 1], scale=0.125)
                    if jd >= 0:
                        B.tt(P[:, c0:c0 + 128], P[:, c0:c0 + 128], tri_le_b, ALU.mult, eng='pool')
                    B.mm(O_ps[0:65, c0:N], Vb[:, kb * 65:(kb + 1) * 65], P[:, c0:N], start=(kb == 0), stop=(kb == nkb - 1))
                B.copy(tA[0:65, :N], O_ps[0:65, :N])
                dps = ps_mm()
                B.mm(dps[0:64, :N], sel65, tA[0:65, :N])
                B.op('dve', lambda e, dps=dps: e.reciprocal(out=rden[0:64, :N], in_=dps[0:64, :N]), [dps], [rden])
                B.tt(cat_fx[h][0:64, :N], tA[0:64, :N], rden[0:64, :N], ALU.mult)
            wa = W.get("wo_a")
            wb = [W.get("wo_b0"), W.get("wo_b1")]
            for m in range(8):
                ps = ps_mm()
                for h in range(4):
                    B.mm(ps[:, :N], wa[:, h, m * 128:(m + 1) * 128], cat_hg[h][:, :N], start=(h == 0), stop=False)
                for h in range(8):
                    B.mm(ps[:, :N], wb[m // 4][0:64, h, (m % 4) * 128:(m % 4 + 1) * 128], cat_fx[h][0:64, :N], start=False,
                         stop=(h == 7))
                resid_add(m, ps, N)

        def mixer_sample():
            N = NB
            rmsnorm(PV_NM, N)
            sl = [W.get("in%d" % i) for i in range(3)]
            flfs = B.sb([NB, 8], F32, "flfs")
            ps = ps_mm()
            for kc in range(8):
                B.mm(ps[0:N, 0:8], hT[kc][:, 0:N], wff[:, kc, :], start=(kc == 0), stop=(kc == 7))
            B.tt(flfs[:, :], ps[0:N, 0:8], pv[0:N, PV_FB:PV_FB + 8], ALU.add)
            B.act(flfs[:, :], flfs[:, :], AF.Sigmoid)
            B.act(flfs[:, :], flfs[:, :], AF.Ln)
            B.dma('pool', lambda e: e.dma_start(out=o_fls, in_=flfs[:, :]), [flfs], [o_fls], is_out=True)
            for h in range(4):
                ps = ps_mm()
                fm_mm(ps[:, :N], sl[0], h * 128, 128, N)
                B.act(q32[h][:, :N], ps[:, :N], AF.Silu)
            for h in range(4):
                ps = ps_mm()
                fm_mm(ps[:, :N], sl[1], h * 128, 128, N)
                B.act(tA[:, :N], ps[:, :N], AF.Sigmoid)
                B.ts(frt[h][:, :N], tA[:, :N], lbt[:, 4 + h:5 + h], lbt[:, h:h + 1], ALU.mult, ALU.add)
            K32 = Kbuf[0][:, :].bitcast(F32)
            vs_tm = K32[0:NB, 0:512]
            ps = ps_mm()
            tm_mm(ps[0:N, :], sl[2], 0, 512, 0, N)
            B.copy(vs_tm, ps[0:N, :])
            s3 = W.get("in3")
            for h in range(4):
                ps = ps_mm()
                fm_mm(ps[:, :N], s3, h * 128, 128, N)
                B.act(gate[h][:, :N], ps[:, :N], AF.Silu)
            qs_tm = K32[0:NB, 512:1024]
            ks_tm = K32[0:NB, 1024:1536]
            vf_tm = K32[0:NB, 1536:2048]
            for i, (dstt, odr) in enumerate(((qs_tm, None), (ks_tm, o_fks), (vf_tm, o_fvs))):
                s = W.get("in%d" % (4 + i))
                ps = ps_mm()
                tm_mm(ps[0:N, :], s, 0, 512, 0, N)
                B.copy(dstt, ps[0:N, :])
                if odr is not None:
                    B.dma('pool', lambda e, dstt=dstt, odr=odr: e.dma_start(out=odr, in_=dstt), [dstt], [odr], is_out=True)
            o_ps = [PS[6], PS[7]]
            for b in range(NB):
                st = ost_next()
                B.dma('sp', lambda e, st=st, b=b: e.dma_start(out=st[:, :].rearrange("p (h v) -> p h v", v=128),
                                                          in_=st_hg[b].rearrange("h k v -> k h v")), [], [st])
                vb = ps_mm()
                B.mm(vb[:, :], ident[0:N, b:b + 1].to_broadcast([N, 128]), vs_tm)
                for h in range(4):
                    B.ts(tC[:, h * 128:(h + 1) * 128], vb[:, h * 128:(h + 1) * 128], frt[h][:, b:b + 1], -1.0, ALU.mult, ALU.mult)
                    B.tt(tC[:, h * 128:(h + 1) * 128], tC[:, h * 128:(h + 1) * 128], vb[:, h * 128:(h + 1) * 128], ALU.add)
                    B.stt(st[:, h * 128:(h + 1) * 128], st[:, h * 128:(h + 1) * 128], frt[h][:, b:b + 1],
                          tC[:, h * 128:(h + 1) * 128], ALU.mult, ALU.add)
                    B.mm(o_ps[h // 2][:, (h % 2) * 16 + b:(h % 2) * 16 + b + 1], st[:, h * 128:(h + 1) * 128], q32[h][:, b:b + 1])
                B.dma('pool', lambda e, st=st, b=b: e.dma_start(out=o_hgs[b].rearrange("h k v -> k h v"),
                                                            in_=st[:, :].rearrange("p (h v) -> p h v", v=128)),
                      [st], [o_hgs], is_out=True)
            for h in range(4):
                hgrn_common(h, N, o_ps[h // 2][:, (h % 2) * 16:(h % 2) * 16 + 16])
            ptb_i = rstd[:, 0:NB * 16].bitcast(I32)
            ptb_f = mean_t[:, 0:NB * 16]
            B.dma('sp', lambda e: e.dma_start(out=ptb_i, in_=ptab), [], [ptb_i])
            B.copy(ptb_f, ptb_i, eng='dve')
            B.ts(ptb_f, ptb_f, 128.0, iota_p, ALU.mult, ALU.add)
            B.copy(ptb_i, ptb_f, eng='dve')
            pn = B.sb([NB, 8], F32, "pn")
            pnb = [B.sb([NB, 8], F32, "pnb%d" % i) for i in range(2)]
            prod16 = ost[0][0:NB, :]
            B.tt(prod16, qs_tm, ks_tm, ALU.mult)
            B.op('dve', lambda e: e.tensor_reduce(out=pn[:, :], in_=prod16.rearrange("p (h d) -> p h d", d=64),
                                                  axis=AX.X, op=ALU.add), [prod16], [pn])
            B.act(pn[:, :], pn[:, :], AF.Exp, scale=0.125)
            lfp = [kvt[0][:, :], kvt[1][:, :]]
            bia = [tA[:, 256:384], tA[:, 384:512]]
            sfxb = [tC[:, 0:128], tC[:, 128:256]]
            sc = [tB[:, 0:128], tB[:, 128:256]]
            pp = [tB[:, 256:384], tB[:, 384:512]]
            kpg = [q32[0], q32[1], q32[2]]
            vpg = [frt[0], frt[1], frt[2]]
            V32 = Vbuf[0][:, 0:2048].bitcast(F32)
            Rn = [V32[0:8, 0:512], V32[0:8, 512:1024]]
            rd = B.sb([8, 2], F32, "rd")
            ofx = PS[5]
            bmask_t = ost[1][0:8, :]
            B.op('dve', lambda e: e.tensor_copy(out=bmask_t.rearrange("p (h d) -> p h d", d=64),
                                                in_=cst[0:8, 0:8].unsqueeze(2).to_broadcast([8, 8, 64])), [cst], [bmask_t])
            for b in range(NB):
                lf = lfp[b % 2]
                for pg in range(16):
                    col = b * 16 + pg
                    B.dma('pool', lambda e, lf=lf, pg=pg, col=col: e.indirect_dma_start(
                        out=lf[:, pg * 8:(pg + 1) * 8], out_offset=None, in_=cl,
                        in_offset=bass.IndirectOffsetOnAxis(ap=ptb_i[:, col:col + 1], axis=0)), [ptb_i], ["lfk%d_%d" % (b % 2, pg)])
                ps = ps_mm()
                lfkeys = ["lfk%d_%d" % (b % 2, pg) for pg in range(16)]
                B.mm(ps[:, 0:128], tri_gt, lf, extra_r=lfkeys)
                B.mm(ps[:, 128:256], onesf, lf, extra_r=lfkeys)
                B.mm(ps[:, 256:264], ident[0:N, b:b + 1].to_broadcast([N, 128]), flfs[:, :])
                prev = sfxb[0]
                B.copy(prev, ps[:, 128:256], eng='dve')
                for li, sh in enumerate((1, 2, 4, 8)):
                    cur = sfxb[(li + 1) % 2]
                    n_ok = (16 - sh) * 8
                    B.tt(cur[:, 0:n_ok], prev[:, 0:n_ok], prev[:, sh * 8:128], ALU.add)
                    B.copy(cur[:, n_ok:128], prev[:, n_ok:128], eng='dve')
                    prev = cur
                bi = bia[b % 2]
                B.tt(bi[:, 0:120], ps[:, 0:120], prev[:, 8:128], ALU.add)
                B.copy(bi[:, 120:128], ps[:, 120:128], eng='dve')
                B.tt(bi.rearrange("p (g h) -> p g h", h=8), bi.rearrange("p (g h) -> p g h", h=8),
                     ps[:, 256:264].unsqueeze(1).to_broadcast([128, 16, 8]), ALU.add)
                qb = ps_mm()
                B.mm(qb[:, :], ident[0:N, b:b + 1].to_broadcast([N, 128]), qs_tm)
                s_t = sc[b % 2]
                for pg in range(16):
                    col = b * 16 + pg
                    kp = kpg[pg % 3]
                    B.dma('pool', lambda e, kp=kp, col=col: e.indirect_dma_start(
                        out=kp[:, :], out_offset=None, in_=ck,
                        in_offset=bass.IndirectOffsetOnAxis(ap=ptb_i[:, col:col + 1], axis=0)), [ptb_i], [kp])
                    B.tt(kp[:, :], kp[:, :], qb[:, :], ALU.mult)
                    B.op('dve', lambda e, kp=kp, pg=pg, s_t=s_t: e.tensor_reduce(
                        out=s_t[:, pg * 8:(pg + 1) * 8], in_=kp[:, :].rearrange("p (h d) -> p h d", d=64), axis=AX.X,
                        op=ALU.add), [kp], [s_t])
                p_t = pp[b % 2]
                B.stt(s_t, s_t, 0.125, bi, ALU.mult, ALU.add)
                B.act(p_t, s_t, AF.Exp)
                pb = pnb[b % 2]
                B.ts(pb[:, :], pn[:, :], ident[0:N, b:b + 1], None, ALU.mult)
                R_ps = PS[6 + b % 2]
                d_ps = PS[4]
                for pg in range(16):
                    col = b * 16 + pg
                    vp = vpg[pg % 3]
                    B.dma('pool', lambda e, vp=vp, col=col: e.indirect_dma_start(
                        out=vp[:, :], out_offset=None, in_=cv,
                        in_offset=bass.IndirectOffsetOnAxis(ap=ptb_i[:, col:col + 1], axis=0)), [ptb_i], [vp])
                    B.mm(R_ps[0:8, :], p_t[:, pg * 8:(pg + 1) * 8], vp[:, :], start=(pg == 0), stop=False)
                B.mm(R_ps[0:8, :], pb[:, :], vf_tm, start=False, stop=True)
                for pg in range(16):
                    B.mm(d_ps[0:8, 0:1], p_t[:, pg * 8:(pg + 1) * 8], onesf[:, 0:1], start=(pg == 0), stop=False)
                B.mm(d_ps[0:8, 0:1], pb[:, :], onesf[0:N, 0:1], start=False, stop=True)
                B.op('dve', lambda e: e.reciprocal(out=rd[:, 0:1], in_=d_ps[0:8, 0:1]), [d_ps], [rd])
                rn = Rn[b % 2]
                B.stt(rn, R_ps[0:8, :], rd[:, 0:1], bmask_t, ALU.mult, ALU.mult)
                for pr in range(4):
                    B.mm(ofx[:, pr * 16 + b:pr * 16 + b + 1], rn[:, pr * 128:(pr + 1) * 128], onesf[0:8, 0:1])
            catfs = B.sb([128, 4, NB], BF16, "catfs")
            B.copy(catfs[:, :, :], ofx[:, 0:64].rearrange("p (a b) -> p a b", b=NB))
            wa = W.get("wo_a")
            wc = W.get("wo_c")
            for m in range(8):
                ps = ps_mm()
                for h in range(4):
                    B.mm(ps[:, :N], wa[:, h, m * 128:(m + 1) * 128], cat_hg[h][:, :N], start=(h == 0), stop=False)
                for h in range(4):
                    B.mm(ps[:, :N], wc[:, h, m * 128:(m + 1) * 128], catfs[:, h, :], start=False, stop=(h == 3))
                resid_add(m, ps, N)

        for ci, (t0, N_, mode) in enumerate(SCHED):
            last = (ci == len(SCHED) - 1)
            cur["ci"] = ci
            load_x(xp[t0:t0 + N_, :], N_)
            dump(0)
            mixer_prompt(t0, N_, mode, last=last)
            if mode == 'partial':
                continue
            dump(1)
            dump(5, cat_hg)
            dump(6, cat_fx)
            ffn(0, N_, last=last)
            dump(2)
            conformer(N_, last=last, halo=(mode == 'halo'))
            dump(3)
            ffn(1, N_, last=last, halo=(mode == 'halo'))
            dump(4)
            if mode == 'halo':
                continue
            rmsnorm(PV_NFIN, N_, out_f32=True)
            to = t0 - (OWN0 if nch == 8 else 0)
            store_y(o_y[to:to + N_, :], N_)
        if do_sample:
            for l in range(2):
                t = xtm[l % 2]
                for half in range(3):
                    c0 = half * 1024
                    c1 = min(FFN, c0 + 1024)
                    B.dma('sp', lambda e, t=t, l=l, c0=c0, c1=c1: e.dma_start(
                        out=t[0:32, 0:c1 - c0], in_=st_ff[l].rearrange("b r f -> (b r) f")[:, c0:c1]), [], [t])
                    for j in range(c0 // 128, c1 // 128):
                        ps = ps_mm()
                        B.tr(ps[:, 0:32], t[0:32, j * 128 - c0:(j + 1) * 128 - c0], ident[0:32, 0:32])
                        B.copy(hfs_ap(l, j), ps[:, 0:32])
            load_x(xs, NB)
            mixer_sample()
            ffn(0, NB, sample=True)
            conformer(NB, sample=True)
            ffn(1, NB, sample=True)
            rmsnorm(PV_NFIN, NB, out_f32=True)
            store_y(o_ys, NB)
        B.finish()
    return nc


_NC_CACHE = {}


def _host_consts():
    cst = np.zeros((128, 1024), np.float32)
    r = np.arange(128)
    cst[:, 0:128] = np.eye(128)
    cst[:, 128:256] = 1.0
    cst[:, 256:384] = (r[:, None] <= r[None, :])
    cst[:, 384:512] = (r[:, None] <= r[None, :]) & ((r[:, None] // 64) == (r[None, :] // 64))
    cst[:, 512:640] = (r[:, None] > r[None, :])
    cst[:, 640] = r
    cst[:, 641] = EPS
    sel = np.zeros((128, 4), np.float32)
    for i in range(120):
        sel[i, i // 30] = 1.0
    cst[:, 648:652] = sel
    cst[64, 656:720] = 1.0
    return cst


def _fm(v, nchunk):
    return np.ascontiguousarray(np.asarray(v, np.float32).reshape(nchunk, 128).T)


def kernel(x_prompt, x_sample, cache_fox_k, cache_fox_v, cache_fox_logf, page_table,
           state_hgrn, state_conv, state_ffn_conv, norm_mix, norm_ffn, norm_final,
           w_in0, fox_fb, hg_lb, hg_gnorm, w_out0, w_pw1, b_pw1, w_dw, b_dw, ln_g, ln_b,
           w_pw2, b_pw2, w_ffn_in, w_ffn_dw, b_ffn_dw, w_ffn_out):
    f = lambda a: np.ascontiguousarray(np.asarray(a, np.float32))
    if 'nc' not in _NC_CACHE:
        _NC_CACHE['nc'] = build_program()
    nc = _NC_CACHE['nc']
    pvec = np.zeros((128, 640), np.float32)
    pvec[:, 0:16] = np.concatenate([_fm(norm_mix[0], 8), _fm(norm_mix[1], 8)], 1)
    pvec[:, 16:32] = np.concatenate([_fm(norm_ffn[0], 8), _fm(norm_ffn[1], 8)], 1)
    pvec[:, 32:40] = _fm(norm_final, 8)
    pvec[:, 40:48] = np.concatenate([_fm(hg_lb[0], 4), _fm(hg_lb[1], 4)], 1)
    pvec[:, 48:52] = _fm(hg_gnorm[0], 4)
    pvec[:, 52:68] = _fm(b_pw1[0], 16)
    pvec[:, 68:76] = _fm(b_dw[0], 8)
    pvec[:, 76:84] = _fm(ln_g[0], 8)
    pvec[:, 84:92] = _fm(ln_b[0], 8)
    pvec[:, 92:100] = _fm(b_pw2[0], 8)
    pvec[:, 100:144] = np.concatenate([_fm(b_ffn_dw[0], 22), _fm(b_ffn_dw[1], 22)], 1)
    wffd = np.asarray(w_ffn_dw, np.float32).reshape(2, 3, 22, 128)
    pvec[:, 144:276] = np.ascontiguousarray(wffd.transpose(3, 0, 2, 1)).reshape(128, 132)
    wd = np.asarray(w_dw, np.float32)[0].reshape(31, 8, 128)
    pvec[:, 276:524] = np.ascontiguousarray(wd.transpose(2, 1, 0)).reshape(128, 248)
    pvec[:, 524:532] = np.broadcast_to(np.asarray(fox_fb, np.float32)[0][None, :], (128, 8))
    cst = _host_consts()
    wdw_rep = np.ascontiguousarray(np.tile(np.asarray(w_dw, np.float32)[0, 0:30], (4, 1)))
    ckf = f(cache_fox_k)[0].reshape(NPOOL * 128, 512)
    cvf = f(cache_fox_v)[0].reshape(NPOOL * 128, 512)
    clf = f(cache_fox_logf)[0].reshape(NPOOL * 128, 8)
    pt = np.asarray(page_table, np.int32)
    shared = {"ck": ckf, "cv": cvf, "cl": clf, "pvec": pvec, "cst": cst, "wdw_rep": wdw_rep,
              "w_in0": f(w_in0)[0], "w_out0": f(w_out0)[0], "w_pw1": f(w_pw1)[0], "w_pw2": f(w_pw2)[0],
              "w_ffi": f(w_ffn_in), "w_ffo": f(w_ffn_out)}
    xpf = f(x_prompt)
    xsf = f(x_sample)[:, 0, :]
    sth = f(state_hgrn)[0]
    stc = f(state_conv)[0]
    stf = f(state_ffn_conv)
    in_maps = []
    for c in range(NCORE):
        sl = slice(NB * c, NB * (c + 1))
        m = dict(shared)
        bb, half = c // 2, c % 2
        cf = np.zeros((128, 72), np.float32)
        if half == 0:
            xin = np.zeros((SEQ, D), np.float32)
            xin[SEQ // 2:] = xpf[bb, :SEQ // 2]
            cf[:, 0:15] = -30000.0
            cf[:, 32:48] = -30000.0
        else:
            xin = xpf[bb]
            cf[:, 64] = 1.0
        m["xp"] = xin
        m["cflag"] = cf
        m["xs"] = np.ascontiguousarray(xsf[sl])
        m["ptab"] = np.ascontiguousarray(np.broadcast_to(pt[sl].reshape(1, NB * 16), (128, NB * 16)))
        m["st_hg"] = np.ascontiguousarray(sth[sl])
        m["st_cv"] = np.ascontiguousarray(stc[sl])
        m["st_ff"] = np.ascontiguousarray(stf[:, sl])
        in_maps.append(m)
    res = run_bass_kernel_spmd(nc, in_maps, core_ids=list(range(NCORE)))
    R = res.results
    cat = lambda k, ax=0: np.concatenate([R[c][k] for c in range(NCORE)], axis=ax)
    stk4 = lambda k: np.stack([np.concatenate([R[2 * b][k], R[2 * b + 1][k]], axis=0) for b in range(4)], axis=0)
    odd4 = lambda k: np.stack([R[2 * b + 1][k] for b in range(4)], axis=0)
    y_prompt = stk4("o_y")
    y_sample = cat("o_ys")[:, None, :]
    fk_p = stk4("o_fk").reshape(1, 4, SEQ, 8, 64)
    fv_p = stk4("o_fv").reshape(1, 4, SEQ, 8, 64)
    fl_p = stk4("o_fl").reshape(1, 4, SEQ, 8)
    fk_s = cat("o_fks").reshape(1, 128, 1, 8, 64)
    fv_s = cat("o_fvs").reshape(1, 128, 1, 8, 64)
    fl_s = cat("o_fls").reshape(1, 128, 1, 8)
    hg_p = odd4("o_hg")[None]
    hg_s = cat("o_hgs")[None]
    cv_p = odd4("o_cv")[None]
    cv_s = cat("o_cvs")[None]
    ff_p = np.stack([R[2 * b + 1]["o_ff"] for b in range(4)], axis=1)
    ff_s = cat("o_ffs", 1)
    outs = (y_prompt, y_sample, fk_p, fv_p, fl_p, fk_s, fv_s, fl_s, hg_p, hg_s, cv_p, cv_s, ff_p, ff_s)
    return tuple(np.ascontiguousarray(o, dtype=np.float32) for o in outs)
```

```python
import numpy as np
from contextlib import ExitStack
import concourse.bass as bass
import concourse.mybir as mybir
from concourse.bass_utils import run_bass_kernel_spmd

F32, BF16, I32 = mybir.dt.float32, mybir.dt.bfloat16, mybir.dt.int32
AF = mybir.ActivationFunctionType
ALU = mybir.AluOpType
AX = mybir.AxisListType

D = 1024
SEQ = 4096
NCORE = 8
NB = 16
NPOOL = 2560
FFN = 2816
NJ = 22
IN0 = 3592
EPS = 1e-6
NCHUNK = 512
RING = 4096


def _keys(lst):
    out = []
    for a in lst:
        if a is None:
            continue
        out.append(a if isinstance(a, str) else getattr(a, 'tensor', a).name)
    return out


class Bld:
    def __init__(self, nc, es):
        self.nc = nc
        self.es = es
        self.E = {'pe': nc.tensor, 'act': nc.scalar, 'dve': nc.vector, 'pool': nc.gpsimd, 'sp': nc.sync}
        self.sem = {e: es.enter_context(nc.semaphore('s_' + e)) for e in self.E}
        self.cnt = {e: 0 for e in self.E}
        self.waited = {e: {} for e in self.E}
        self.NDS = 48
        self.dsem = [es.enter_context(nc.semaphore('d%d' % i)) for i in range(self.NDS)]
        self.dval = [0] * self.NDS
        self.dnext = 0
        self.res = {}
        self.out_deps = []
        self.ntens = 0

    def sb(self, shape, dt, name=None):
        self.ntens += 1
        return self.es.enter_context(self.nc.sbuf_tensor(name or ('t%d' % self.ntens), list(shape), dt))

    def psum(self, name):
        return self.es.enter_context(self.nc.psum_tensor(name, [128, 512], F32))

    def _semh(self, key):
        return self.sem[key] if isinstance(key, str) else self.dsem[key]

    def _wait(self, eng, dep):
        key, val = dep
        if key == 'pe' and eng == 'pe':
            return
        if self.waited[eng].get(key, 0) >= val:
            return
        self.E[eng].wait_ge(self._semh(key), val)
        self.waited[eng][key] = val

    def _deps(self, eng, r, w):
        for k in r:
            e = self.res.get(k)
            if e and e[0]:
                self._wait(eng, e[0])
        for k in w:
            e = self.res.get(k)
            if e:
                if e[0]:
                    self._wait(eng, e[0])
                for kk, vv in e[1].items():
                    self._wait(eng, (kk, vv))

    def _commit(self, me, r, w):
        for k in r:
            e = self.res.setdefault(k, [None, {}])
            e[1][me[0]] = max(e[1].get(me[0], 0), me[1])
        for k in w:
            self.res[k] = [me, {}]

    def op(self, eng, fn, r, w):
        r = _keys(r)
        w = _keys(w)
        self._deps(eng, r, w)
        ins = fn(self.E[eng])
        self.cnt[eng] += 1
        ins.then_inc(self.sem[eng], 1)
        self._commit((eng, self.cnt[eng]), r, w)

    def dma(self, q, fn, r, w, is_out=False):
        r = _keys(r)
        w = _keys(w)
        self._deps(q, r, w)
        i = self.dnext
        self.dnext = (i + 1) % self.NDS
        if self.dval[i] > 0:
            self._wait(q, (i, self.dval[i]))
        ins = fn(self.E[q])
        self.dval[i] += 16
        ins.then_inc(self.dsem[i], 16)
        me = (i, self.dval[i])
        self._commit(me, r, w)
        if is_out:
            self.out_deps.append(me)

    def finish(self):
        for dep in self.out_deps:
            self._wait('sp', dep)

    def mm(self, out, lhsT, rhs, start=True, stop=True, extra_r=()):
        self.op('pe', lambda e: e.matmul(out, lhsT=lhsT, rhs=rhs, start=start, stop=stop),
                [lhsT, rhs] + list(extra_r), [out])

    def tr(self, out, in_, ident):
        self.op('pe', lambda e: e.transpose(out, in_, ident), [in_, ident], [out])

    def act(self, out, in_, func, bias=None, scale=None, eng='act'):
        kw = {}
        rr = [in_]
        if bias is not None:
            kw['bias'] = bias
            if not isinstance(bias, (int, float)):
                rr.append(bias)
        if scale is not None:
            kw['scale'] = scale
            if not isinstance(scale, (int, float)):
                rr.append(scale)
        self.op('act', lambda e: e.activation(out=out, in_=in_, func=func, **kw), rr, [out])

    def copy(self, out, in_, eng='act'):
        if eng == 'act':
            self.op('act', lambda e: e.copy(out=out, in_=in_), [in_], [out])
        else:
            self.op(eng, lambda e: e.tensor_copy(out=out, in_=in_), [in_], [out])

    def tt(self, out, in0, in1, op, eng='dve'):
        self.op(eng, lambda e: e.tensor_tensor(out=out, in0=in0, in1=in1, op=op), [in0, in1], [out])

    def ts(self, out, in0, s1, s2, op0, op1=None, eng='dve'):
        rr = [in0] + [s for s in (s1, s2) if s is not None and not isinstance(s, (int, float))]
        if op1 is None:
            self.op(eng, lambda e: e.tensor_scalar(out=out, in0=in0, scalar1=s1, scalar2=None, op0=op0), rr, [out])
        else:
            self.op(eng, lambda e: e.tensor_scalar(out=out, in0=in0, scalar1=s1, scalar2=s2, op0=op0, op1=op1), rr, [out])

    def stt(self, out, in0, scalar, in1, op0, op1, eng='dve'):
        rr = [in0, in1] + ([] if isinstance(scalar, (int, float)) else [scalar])
        self.op(eng, lambda e: e.scalar_tensor_tensor(out=out, in0=in0, scalar=scalar, in1=in1, op0=op0, op1=op1),
                rr, [out])

    def memset(self, ap, val, eng='dve'):
        self.op(eng, lambda e: e.memset(ap, val), [], [ap])


def _seg2(ap2d, off, stride, n=64):
    pst = ap2d.ap[0][0]
    npart = ap2d.ap[0][1]
    return bass.AP(ap2d.tensor, ap2d.offset + off, [[pst, npart], [stride, 2], [1, n]])


def build_program(nch=SEQ // NCHUNK, do_sample=True, dbg_chunk=None, npool=NPOOL):
    nc = bass.Bass("TRN2", target_bir_lowering=False)

    def din(name, shape, dt=F32):
        return nc.dram_tensor(name, list(shape), dt, kind="ExternalInput").ap()

    def dout(name, shape, dt=F32):
        return nc.dram_tensor(name, list(shape), dt, kind="ExternalOutput").ap()

    def dscr(name, shape, dt=BF16):
        return nc.dram_tensor(name, list(shape), dt, kind="Internal").ap()

    xp = din("xp", [SEQ, D])
    xs = din("xs", [NB, D])
    ck = din("ck", [npool * 128, 512])
    cv = din("cv", [npool * 128, 512])
    cl = din("cl", [npool * 128, 8])
    ptab = din("ptab", [128, NB * 16], I32)
    st_hg = din("st_hg", [NB, 4, 128, 128])
    st_cv = din("st_cv", [NB, 30, D])
    st_ff = din("st_ff", [2, NB, 2, FFN])
    pvec_d = din("pvec", [128, 640])
    cst_d = din("cst", [128, 1024])
    cflag_d = din("cflag", [128, 72])
    wdw_rep = din("wdw_rep", [120, D])
    w_in0 = din("w_in0", [D, IN0])
    w_out0 = din("w_out0", [D, D])
    w_pw1 = din("w_pw1", [D, 2 * D])
    w_pw2 = din("w_pw2", [D, D])
    w_ffi = din("w_ffi", [2, D, 2 * FFN])
    w_ffo = din("w_ffo", [2, FFN, D])
    OWN0 = 4 * NCHUNK
    o_y = dout("o_y", [SEQ - OWN0, D])
    o_ys = dout("o_ys", [NB, D])
    o_fk = dout("o_fk", [SEQ - OWN0, 512])
    o_fv = dout("o_fv", [SEQ - OWN0, 512])
    o_fl = dout("o_fl", [SEQ - OWN0, 8])
    o_fks = dout("o_fks", [NB, 512])
    o_fvs = dout("o_fvs", [NB, 512])
    o_fls = dout("o_fls", [NB, 8])
    o_hg = dout("o_hg", [4, 128, 128])
    o_hgs = dout("o_hgs", [NB, 4, 128, 128])
    o_cv = dout("o_cv", [30, D])
    o_cvs = dout("o_cvs", [NB, 30, D])
    o_ff = dout("o_ff", [2, 2, FFN])
    o_ffs = dout("o_ffs", [2, NB, 2, FFN])
    o_dbg = dout("o_dbg", [8, 8, 128, 512]) if dbg_chunk is not None else None
    s_in0 = dscr("s_in0", [D, IN0])
    s_out0 = dscr("s_out0", [D, D])
    s_pw1 = dscr("s_pw1", [D, 2 * D])
    s_pw2 = dscr("s_pw2", [D, D])
    s_ffi = dscr("s_ffi", [2, D, 2 * FFN])
    s_ffo = dscr("s_ffo", [2, FFN, D])
    s_dcv = dscr("s_dcv", [8, 128, 31 * 128])
    s_k = dscr("s_k", [8, 64, SEQ])
    s_v = dscr("s_v", [8, 128, 32, 64])

    es = ExitStack()
    with es:
        B = Bld(nc, es)
        pv = B.sb([128, 640], F32, "pv_sb")
        cst = B.sb([128, 1024], F32, "cst_sb")
        cflag = B.sb([128, 72], F32, "cflag_sb")
        ident = cst[:, 0:128]
        onesf = cst[:, 128:256]
        tri_le = cst[:, 256:384]
        mask2f = cst[:, 384:512]
        tri_gt = cst[:, 512:640]
        bmask = cst[0:8, 640:1152 - 128] if False else None
        iota_p = cst[:, 640:641]
        epsc = cst[:, 641:642]
        sel4 = cst[0:120, 648:652]
        sel65 = cst[0:65, 656:720]
        cb = B.sb([128, 768], BF16, "cb")
        ident_b = cb[:, 0:128]
        onesm_b = cb[:, 128:256]
        ones128_b = cb[:, 256:384]
        tri_le_b = cb[:, 384:512]
        mask2_b = cb[:, 512:640]
        ones_b = cb[:, 640:768]
        wff = B.sb([128, 8, 8], BF16, "wff")
        xT = [B.sb([128, 512], F32, "xT%d" % i) for i in range(8)]
        hT = [B.sb([128, 512], BF16, "hT%d" % i) for i in range(8)]
        aT = [B.sb([128, 512], BF16, "aT%d" % i) for i in range(NJ)]
        sq = aT[0:8]
        rstd = B.sb([128, 512], F32, "rstd")
        ring = [B.sb([128, RING], BF16, "ring%d" % i) for i in range(4)]
        xtm = [B.sb([128, D], F32, "xtm%d" % i) for i in range(2)]
        PS = [B.psum("ps%d" % i) for i in range(8)]
        psrr = [0]

        def ps_mm():
            i = psrr[0] % 4
            psrr[0] += 1
            return PS[i]

        ps6rr = [0]

        def ps_mm6():
            i = ps6rr[0] % 6
            ps6rr[0] += 1
            return PS[i]

        class WS:
            def __init__(self):
                self.n = 0
                self.loaded = 0
                self.plan = []

            def add(self, tag, src, npart, shape):
                nm = src.tensor.name
                if tag.startswith("dcv"):
                    deps = ["s_dcv#%s" % tag[3:]]
                else:
                    deps = CAST_KEYS[nm + ("#%d" % int(tag[3]) if nm in ("s_ffi", "s_ffo") else "")]
                self.plan.append((tag, src, npart, shape, deps))

            def _issue(self, i):
                tag, src, npart, shape, deps = self.plan[i]
                slot = ring[i % 4]
                sz = int(np.prod(shape))
                dst = slot[0:npart, 0:sz]
                if len(shape) == 2:
                    dst = dst.rearrange("p (a b) -> p a b", b=shape[1])
                B.dma('sp', lambda e: e.dma_start(out=dst, in_=src), deps, [slot])

            def get(self, tag):
                i = self.n
                assert self.plan[i][0] == tag, (self.plan[i][0], tag)
                while self.loaded < min(len(self.plan), i + 2):
                    self._issue(self.loaded)
                    self.loaded += 1
                self.n += 1
                _, src, npart, shape, _d = self.plan[i]
                sz = int(np.prod(shape))
                v = ring[i % 4][0:npart, 0:sz]
                if len(shape) == 2:
                    v = v.rearrange("p (a b) -> p a b", b=shape[1])
                return v

        CAST_KEYS = {}

        def _ck(name, rows, rstep=256):
            CAST_KEYS[name] = ["%s#r%d" % (name, r0) for r0 in range(0, rows, rstep)]

        _ck("s_in0", D)
        _ck("s_out0", D)
        _ck("s_pw1", D)
        _ck("s_pw2", D)
        for l_ in range(2):
            _ck("s_ffi#%d" % l_, D)
            _ck("s_ffo#%d" % l_, FFN)
        W = WS()
        in0v = s_in0.rearrange("(c p) n -> p c n", p=128)
        pw1v = s_pw1.rearrange("(c p) n -> p c n", p=128)
        pw2v = s_pw2.rearrange("(c p) n -> p c n", p=128)

        def plan_chunk(sample, mode='own'):
            if mode == 'partial':
                for i in (1, 2, 5, 6):
                    W.add("in%d" % i, in0v[:, :, i * 512:(i + 1) * 512], 128, [8, 512])
                return
            for i in range(7):
                W.add("in%d" % i, in0v[:, :, i * 512:(i + 1) * 512], 128, [8, 512])
            W.add("wo_a", s_out0[0:512, :].rearrange("(c p) n -> p c n", p=128), 128, [4, 1024])
            if sample:
                W.add("wo_c", s_out0[512:1024, :].rearrange("(c p) n -> p c n", p=128), 128, [4, 1024])
            else:
                for hh in range(2):
                    W.add("wo_b%d" % hh, s_out0[512:1024, hh * 512:(hh + 1) * 512].rearrange("(h p) n -> p h n", p=64),
                          64, [8, 512])
            plan_ffn(0)
            for i in range(2):
                W.add("pw1a%d" % i, pw1v[:, :, i * 512:(i + 1) * 512], 128, [8, 512])
                W.add("pw1g%d" % i, pw1v[:, :, 1024 + i * 512:1024 + (i + 1) * 512], 128, [8, 512])
            for c in range(8):
                W.add("dcv%d" % c, s_dcv[c], 128, [31 * 128])
            for i in range(2):
                W.add("pw2_%d" % i, pw2v[:, :, i * 512:(i + 1) * 512], 128, [8, 512])
            plan_ffn(1)

        def plan_ffn(l):
            fi = s_ffi[l].rearrange("(c p) n -> p c n", p=128)
            fo = s_ffo[l].rearrange("(j p) n -> p j n", p=128)
            for g in range(6):
                wd = 512 if g < 5 else 256
                W.add("ffg%d_%d" % (l, g), fi[:, :, g * 512:g * 512 + wd], 128, [8, wd])
                W.add("ffu%d_%d" % (l, g), fi[:, :, FFN + g * 512:FFN + g * 512 + wd], 128, [8, wd])
            for m in range(8):
                W.add("ffo%d_%d" % (l, m), fo[:, :, m * 128:(m + 1) * 128], 128, [NJ, 128])

        NCH = nch
        if nch == 8:
            SCHED = [(0, 512, 'partial'), (512, 512, 'partial'), (1024, 512, 'partial'), (1536, 384, 'partial'),
                     (1920, 128, 'halo')] + [(2048 + 512 * i, 512, 'own') for i in range(4)]
        else:
            SCHED = [(512 * i, 512, 'own') for i in range(nch)]
        for (_t, _n, _m) in SCHED:
            plan_chunk(False, _m)
        if do_sample:
            plan_chunk(True)

        B.dma('sp', lambda e: e.dma_start(out=pv[:, :], in_=pvec_d), [], [pv])
        B.dma('sp', lambda e: e.dma_start(out=cst[:, :], in_=cst_d), [], [cst])
        B.dma('sp', lambda e: e.dma_start(out=cflag[:, :], in_=cflag_d), [], [cflag])
        B.ts(cb[:, 0:128], ident, 1.0, None, ALU.mult)
        B.ts(cb[:, 128:256], onesf, 1.0 / 1024.0, None, ALU.mult)
        B.ts(cb[:, 256:384], onesf, 1.0 / 128.0, None, ALU.mult)
        B.ts(cb[:, 384:512], tri_le, 1.0, None, ALU.mult)
        B.ts(cb[:, 512:640], mask2f, 1.0, None, ALU.mult)
        B.ts(cb[:, 640:768], onesf, 1.0, None, ALU.mult)

        B.dma('pool', lambda e: e.dma_start(out=wff[:, :, :], in_=w_in0[:, 3584:3592].rearrange("(c p) n -> p c n", p=128)),
              [], [wff])
        PV_NM, PV_NF, PV_NFIN = 0, 16, 32
        PV_LB, PV_GN = 40, 48
        PV_BPW1, PV_BDW, PV_LNG, PV_LNB, PV_BPW2 = 52, 68, 76, 84, 92
        PV_BFF = 100
        PV_WFF = 144
        PV_WDW = 276
        PV_FB = 524
        lbt = B.sb([128, 8], F32, "lbt")
        B.tt(lbt[:, 0:4], pv[:, PV_LB:PV_LB + 4], pv[:, PV_LB + 4:PV_LB + 8], ALU.subtract)
        B.act(lbt[:, 0:4], lbt[:, 0:4], AF.Sigmoid)
        B.ts(lbt[:, 4:8], lbt[:, 0:4], -1.0, 1.0, ALU.mult, ALU.add)

        for c in range(8):
            t = ring[c % 4]
            for j in range(31):
                B.ts(t[:, j * 128:(j + 1) * 128], ident, pv[:, PV_WDW + c * 31 + j:PV_WDW + c * 31 + j + 1], None, ALU.mult)
            B.dma('pool', lambda e, t=t, c=c: e.dma_start(out=s_dcv[c], in_=t[:, 0:31 * 128]), [t], ["s_dcv#%d" % c])

        def cast_w(dst, src, rows, name, rstep=256):
            for r0 in range(0, rows, rstep):
                r1 = min(rows, r0 + rstep)
                B.dma('pool', lambda e, r0=r0, r1=r1: e.dma_start(out=dst[r0:r1, :], in_=src[r0:r1, :]), [],
                      ["%s#r%d" % (name, r0)])

        cast_w(s_in0, w_in0, D, "s_in0")
        cast_w(s_out0, w_out0, D, "s_out0")
        cast_w(s_ffi[0], w_ffi[0], D, "s_ffi#0")
        cast_w(s_ffo[0], w_ffo[0], FFN, "s_ffo#0")
        cast_w(s_pw1, w_pw1, D, "s_pw1")
        cast_w(s_pw2, w_pw2, D, "s_pw2")
        cast_w(s_ffi[1], w_ffi[1], D, "s_ffi#1")
        cast_w(s_ffo[1], w_ffo[1], FFN, "s_ffo#1")

        def norm_sq(m, N):
            B.act(sq2[m][:, :N], xT[m][:, :N], AF.Square)

        def rmsnorm(gcol, N, outs=None, out_f32=False):
            ps = ps_mm()
            for kc in range(8):
                B.mm(ps[:, :N], onesm_b, sq2[kc][:, :N], start=(kc == 0), stop=(kc == 7))
            B.act(rstd[:, :N], ps[:, :N], AF.Ln, bias=epsc)
            B.act(rstd[:, :N], rstd[:, :N], AF.Exp, scale=-0.5)
            for kc in range(8):
                o = xT[kc] if out_f32 else hT[kc]
                B.stt(o[:, :N], xT[kc][:, :N], pv[:, gcol + kc:gcol + kc + 1], rstd[:, :N], ALU.mult, ALU.mult)

        def fm_mm(ps_ap, slab, c0, m, N, src=None):
            src = src or hT
            for kc in range(8):
                B.mm(ps_ap, slab[:, kc, c0:c0 + m], src[kc][:, :N], start=(kc == 0), stop=(kc == 7))

        def tm_mm(ps_ap, slab, c0, ncol, t0, nt):
            for kc in range(8):
                B.mm(ps_ap, hT[kc][:, t0:t0 + nt], slab[:, kc, c0:c0 + ncol], start=(kc == 0), stop=(kc == 7))

        def resid_add(m, ps, N, bias=None):
            if bias is None:
                B.tt(xT[m][:, :N], ps[:, :N], xT[m][:, :N], ALU.add)
            else:
                B.stt(xT[m][:, :N], ps[:, :N], bias, xT[m][:, :N], ALU.add, ALU.add)
            norm_sq(m, N)

        trc = [0]

        def fm_to_dram(srcs, n, dram_rows, q='pool'):
            nck = len(srcs)
            for b0 in range(0, nck, 8):
                st = xtm[trc[0] % 2]
                trc[0] += 1
                bn = min(8, nck - b0)
                for g0 in range(0, bn, 4):
                    ps = ps_mm()
                    gn = min(4, bn - g0)
                    for i in range(gn):
                        B.tr(ps[0:n, i * 128:(i + 1) * 128], srcs[b0 + g0 + i], ident)
                    B.copy(st[0:n, g0 * 128:(g0 + gn) * 128], ps[0:n, 0:gn * 128])
                B.dma(q, lambda e, st=st, b0=b0, bn=bn: e.dma_start(out=dram_rows[:, b0 * 128:(b0 + bn) * 128],
                                                                 in_=st[0:n, 0:bn * 128]),
                      [st], [dram_rows], is_out=True)

        q32 = [B.sb([128, 512], F32, "q32_%d" % h) for h in range(4)]
        frt = [B.sb([128, 512], F32, "fr_%d" % h) for h in range(4)]
        gate = [B.sb([128, 512], BF16, "gate_%d" % h) for h in range(4)]
        vhg = [B.sb([128, 512], BF16, "vhg_%d" % j) for j in range(4)]
        cat_hg = [B.sb([128, 512], BF16, "cathg_%d" % h) for h in range(4)]
        sq2 = gate + vhg
        cat_fx = [B.sb([64, 512], BF16, "catfx_%d" % h) for h in range(8)]
        Qp = [B.sb([65, 512], BF16, "Qp_%d" % h) for h in range(8)]
        S32 = [B.sb([128, 128], F32, "S32_%d" % h) for h in range(4)]
        Sbf = [B.sb([128, 128], BF16, "Sbf_%d" % h) for h in range(4)]
        for h in range(4):
            B.memset(S32[h][:, :], 0.0)
            B.memset(Sbf[h][:, :], 0.0, eng='pool')
        tA = B.sb([128, 512], F32, "tA")
        tB = B.sb([128, 512], F32, "tB")
        tC = B.sb([128, 512], F32, "tC")
        nQt_s = [B.sb([128, 512], BF16, "nQt%d" % i) for i in range(2)]
        nKt_s = [B.sb([128, 512], BF16, "nKt%d" % i) for i in range(2)]
        Qb_s = [B.sb([128, 512], BF16, "Qb%d" % i) for i in range(2)]
        nKtm_s = [[B.sb([128, 128], BF16, "nKtm%d_%d" % (i, j)) for j in range(4)] for i in range(2)]
        csm_s = [B.sb([128, 64], F32, "csm%d" % i) for i in range(2)]
        attm = [B.sb([128, 128], BF16, "attm%d" % j) for j in range(2)]
        kvt = [B.sb([128, 128], F32, "kvt%d" % j) for j in range(2)]
        carry = B.sb([128, 8], F32, "carry")
        cref = B.sb([128, 8], F32, "cref")
        negc = B.sb([128, 32, 8], F32, "negc")
        biasc = B.sb([128, 32, 8], F32, "biasc")
        flf = [B.sb([128, 8], F32, "flf%d" % j) for j in range(4)]
        ctm = [B.sb([128, 8], F32, "ctm%d" % j) for j in range(2)]
        Zr = [B.sb([128, 8, 65], BF16, "Zr%d" % j) for j in range(4)]
        for j in range(4):
            B.memset(Zr[j][:, :, :], 0.0, eng='pool')
        B.memset(carry[:, :], 0.0)
        kst = [B.sb([64, 512], BF16, "kst%d" % i) for i in range(2)]
        vst = [B.sb([128, 512], BF16, "vst%d" % i) for i in range(2)]
        ost = [B.sb([128, 512], F32, "ost%d" % i) for i in range(2)]
        ostc = [0]
        Kbuf = [B.sb([65, SEQ], BF16, "Kbuf%d" % i) for i in range(1)]
        Vbuf = [B.sb([128, 32 * 65], BF16, "Vbuf%d" % i) for i in range(1)]
        for i in range(1):
            B.memset(Kbuf[i][64:65, :], 1.0, eng='pool')
            B.memset(Vbuf[i][:, :].rearrange("p (k d) -> p k d", d=65)[:, :, 64:65], 1.0, eng='pool')
        PT = [B.sb([128, 512], BF16, "PT%d" % i) for i in range(2)]
        rden = B.sb([64, 512], F32, "rden")
        gbuf = [B.sb([128, 514], F32, "gbuf%d" % i) for i in range(2)]
        gcar = [B.sb([128, NJ, 2], F32, "gcar%d" % l) for l in range(2)]
        for l in range(2):
            B.memset(gcar[l][:, :, :], 0.0, eng='pool')
        ubuf = [B.sb([128, 542], BF16, "ubuf%d" % c) for c in range(8)]
        for c in range(8):
            B.memset(ubuf[c][:, 0:30], 0.0, eng='pool')
        u32 = B.sb([128, 8, 32], F32, "u32")
        mean_t = B.sb([128, 512], F32, "mean_t")

        def hfs_ap(l, j):
            idx = l * NJ + j
            return ubuf[idx // 8][:, 0:512].bitcast(F32)[:, (idx % 8) * 32:(idx % 8) * 32 + 32]

        def ost_next():
            t = ost[ostc[0] % 2]
            ostc[0] += 1
            return t

        cur = {"ci": -1}

        def dump(i, tiles=None, k0=0):
            if dbg_chunk is None or cur["ci"] != dbg_chunk:
                return
            tiles = tiles or xT
            for kc, t in enumerate(tiles):
                tv = t if hasattr(t, "tensor") else t[:, :]
                npart = tv.ap[0][1]
                ncol = tv.ap[-1][1]
                B.dma('pool', lambda e, tv=tv, kc=kc, npart=npart, ncol=ncol: e.dma_start(
                    out=o_dbg[i, k0 + kc, 0:npart, 0:ncol], in_=tv), [tv], [o_dbg], is_out=True)

        def load_x(src_rows, N):
            nt = max(1, N // 128)
            rows = min(N, 128)
            for g in range(0, nt, 2):
                for j in range(g, min(nt, g + 2)):
                    t = xtm[j % 2]
                    B.dma('sp', lambda e, t=t, j=j: e.dma_start(out=t[0:rows, :], in_=src_rows[j * 128:j * 128 + rows, :]),
                          [src_rows], [t])
                for kc in range(8):
                    ps = ps_mm()
                    n2 = min(nt, g + 2) - g
                    for jj in range(n2):
                        B.tr(ps[:, jj * 128:jj * 128 + rows], xtm[(g + jj) % 2][0:rows, kc * 128:(kc + 1) * 128],
                             ident[0:rows, 0:rows])
                    B.copy(xT[kc][:, g * 128:g * 128 + (n2 - 1) * 128 + rows], ps[:, 0:(n2 - 1) * 128 + rows])
            for kc in range(8):
                norm_sq(kc, N)

        def store_y(dst_rows, N):
            nt = max(1, N // 128)
            rows = min(N, 128)
            for j in range(nt):
                t = xtm[j % 2]
                for g in range(2):
                    ps = ps_mm()
                    for i in range(4):
                        kc = g * 4 + i
                        B.tr(ps[0:rows, i * 128:(i + 1) * 128], xT[kc][:, j * 128:j * 128 + rows], ident)
                    B.copy(t[0:rows, g * 512:(g + 1) * 512], ps[0:rows, :])
                B.dma('pool', lambda e, t=t, j=j: e.dma_start(out=dst_rows[j * 128:j * 128 + rows, :], in_=t[0:rows, :]),
                      [t], [dst_rows], is_out=True)

        def ffn(l, N, sample=False, last=False, halo=False):
            rmsnorm(PV_NF + 8 * l, N)
            for g in range(6):
                nj = 4 if g < 5 else 2
                G = W.get("ffg%d_%d" % (l, g))
                U = W.get("ffu%d_%d" % (l, g))
                for jj in range(nj):
                    j = 4 * g + jj
                    gps = ps_mm6()
                    ups = ps_mm6()
                    fm_mm(gps[:, :N], G, jj * 128, 128, N)
                    fm_mm(ups[:, :N], U, jj * 128, 128, N)
                    gb = gbuf[j % 2]
                    acc = tA if j % 2 == 0 else tB
                    w0 = pv[:, PV_WFF + (l * NJ + j) * 3 + 0:PV_WFF + (l * NJ + j) * 3 + 1]
                    w1 = pv[:, PV_WFF + (l * NJ + j) * 3 + 1:PV_WFF + (l * NJ + j) * 3 + 2]
                    w2 = pv[:, PV_WFF + (l * NJ + j) * 3 + 2:PV_WFF + (l * NJ + j) * 3 + 3]
                    bj = pv[:, PV_BFF + l * NJ + j:PV_BFF + l * NJ + j + 1]
                    if not sample:
                        B.copy(gb[:, 0:2], gcar[l][:, j, :], eng='pool')
                        B.copy(gb[:, 2:N + 2], gps[:, :N])
                        B.copy(gcar[l][:, j, :], gb[:, N:N + 2], eng='pool')
                        B.ts(acc[:, :N], gb[:, 0:N], w0, None, ALU.mult)
                        B.stt(acc[:, :N], gb[:, 1:N + 1], w1, acc[:, :N], ALU.mult, ALU.add)
                        B.stt(acc[:, :N], gb[:, 2:N + 2], w2, acc[:, :N], ALU.mult, ALU.add)
                    else:
                        hv = hfs_ap(l, j).rearrange("p (b r) -> p b r", r=2)
                        B.copy(gb[:, 0:N], gps[:, :N])
                        B.ts(acc[:, :N], hv[:, :, 0], w0, None, ALU.mult)
                        B.stt(acc[:, :N], hv[:, :, 1], w1, acc[:, :N], ALU.mult, ALU.add)
                        B.stt(acc[:, :N], gb[:, 0:N], w2, acc[:, :N], ALU.mult, ALU.add)
                        B.op('dve', lambda e, hv=hv, gb=gb: e.tensor_copy(out=hv[:, :, 0], in_=gb[:, 0:N]), [gb], [hv])
                    ge = tC
                    B.act(ge[:, :N], acc[:, :N], AF.Gelu, bias=bj)
                    B.tt(aT[j][:, :N], ge[:, :N], ups[:, :N], ALU.mult)
            for m in range(8):
                Wo = W.get("ffo%d_%d" % (l, m))
                ps = ps_mm()
                for j in range(NJ):
                    B.mm(ps[:, :N], Wo[:, j, :], aT[j][:, :N], start=(j == 0), stop=(j == NJ - 1))
                resid_add(m, ps, N)
            if halo and l == 1:
                B.ts(gcar[1][:, :, :], gcar[1][:, :, :], cflag[:, 64:65], None, ALU.mult)
            if last and not sample:
                fm_to_dram([gcar[l][:, j, :] for j in range(NJ)], 2, o_ff[l])
            if sample:
                B.dma('pool', lambda e: e.dma_start(out=o_ffs[l, :, 0, :], in_=st_ff[l, :, 1, :]), [], [o_ffs], is_out=True)
                fm_to_dram([hfs_ap(l, j).rearrange("p (b r) -> p b r", r=2)[:, :, 0] for j in range(NJ)], NB,
                           o_ffs[l, :, 1, :])

        def conformer(N, sample=False, last=False, halo=False):
            rmsnorm(PV_NM + 8, N)
            for i in range(2):
                A = W.get("pw1a%d" % i)
                G = W.get("pw1g%d" % i)
                for cc in range(4):
                    c = 4 * i + cc
                    aps = ps_mm()
                    gps = ps_mm()
                    fm_mm(aps[:, :N], A, cc * 128, 128, N)
                    fm_mm(gps[:, :N], G, cc * 128, 128, N)
                    B.act(tA[:, :N], gps[:, :N], AF.Sigmoid, bias=pv[:, PV_BPW1 + 8 + c:PV_BPW1 + 8 + c + 1])
                    if not sample:
                        B.stt(ubuf[c][:, 30:30 + N], aps[:, :N], pv[:, PV_BPW1 + c:PV_BPW1 + c + 1], tA[:, :N], ALU.add, ALU.mult)
                        if last:
                            B.stt(u32[:, c, 0:30], aps[:, N - 30:N], pv[:, PV_BPW1 + c:PV_BPW1 + c + 1], tA[:, N - 30:N],
                                  ALU.add, ALU.mult)
                    else:
                        B.stt(u32[:, c, 0:N], aps[:, :N], pv[:, PV_BPW1 + c:PV_BPW1 + c + 1], tA[:, :N], ALU.add, ALU.mult)
            dps_list = []
            if not sample:
                for c in range(8):
                    Dg = W.get("dcv%d" % c)
                    ps = PS[4 + c % 4]
                    for j in range(31):
                        B.mm(ps[:, :N], Dg[:, j * 128:(j + 1) * 128], ubuf[c][:, j:j + N], start=(j == 0), stop=(j == 30))
                    dsb = dbuf[c]
                    B.ts(dsb[:, :N], ps[:, :N], pv[:, PV_BDW + c:PV_BDW + c + 1], None, ALU.add)
                    if halo:
                        B.ts(ubuf[c][:, 0:30], ubuf[c][:, N:N + 30], cflag[:, 64:65], None, ALU.mult, eng='pool')
                    else:
                        B.copy(ubuf[c][:, 0:30], ubuf[c][:, N:N + 30], eng='pool')
            else:
                for c in range(8):
                    W.get("dcv%d" % c)
                hps = [PS[4], PS[5]]
                for hh_ in range(2):
                    B.dma('sp', lambda e, hh_=hh_: e.dma_start(out=gbuf[hh_][0:120, 0:512], in_=wdw_rep[:, hh_ * 512:(hh_ + 1) * 512]),
                          [], [gbuf[hh_]])
                for tl in range(4):
                    t = xtm[tl % 2]
                    B.dma('sp', lambda e, t=t, tl=tl: e.dma_start(
                        out=t[0:120, :], in_=st_cv[4 * tl:4 * tl + 4].rearrange("b j c -> (b j) c")), [], [t])
                    for hh_ in range(2):
                        B.tt(t[0:120, hh_ * 512:(hh_ + 1) * 512], t[0:120, hh_ * 512:(hh_ + 1) * 512], gbuf[hh_][0:120, 0:512], ALU.mult)
                    for c in range(8):
                        B.mm(hps[c // 4][:, (c % 4) * 16 + 4 * tl:(c % 4) * 16 + 4 * tl + 4], t[0:120, c * 128:(c + 1) * 128],
                             sel4, start=True, stop=True)
                for c in range(8):
                    w30 = pv[:, PV_WDW + c * 31 + 30:PV_WDW + c * 31 + 31]
                    B.stt(dbuf[c][:, :N], u32[:, c, 0:N], w30, hps[c // 4][:, (c % 4) * 16:(c % 4) * 16 + 16], ALU.mult, ALU.add)
                    B.ts(dbuf[c][:, :N], dbuf[c][:, :N], pv[:, PV_BDW + c:PV_BDW + c + 1], None, ALU.add)
            mps = ps_mm()
            qps = ps_mm()
            for c in range(8):
                B.copy(hT[c][:, :N], dbuf[c][:, :N], eng='pool')
                B.act(sq[c][:, :N], dbuf[c][:, :N], AF.Square)
            for c in range(8):
                B.mm(mps[:, :N], onesm_b, hT[c][:, :N], start=(c == 0), stop=(c == 7))
            for c in range(8):
                B.mm(qps[:, :N], onesm_b, sq[c][:, :N], start=(c == 0), stop=(c == 7))
            B.copy(mean_t[:, :N], mps[:, :N])
            B.tt(tA[:, :N], mean_t[:, :N], mean_t[:, :N], ALU.mult)
            B.tt(tA[:, :N], qps[:, :N], tA[:, :N], ALU.subtract)
            B.act(rstd[:, :N], tA[:, :N], AF.Ln, bias=epsc)
            B.act(rstd[:, :N], rstd[:, :N], AF.Exp, scale=-0.5)
            for c in range(8):
                t = tB if c % 2 == 0 else tC
                B.tt(t[:, :N], dbuf[c][:, :N], mean_t[:, :N], ALU.subtract)
                B.tt(t[:, :N], t[:, :N], rstd[:, :N], ALU.mult)
                B.act(hT[c][:, :N], t[:, :N], AF.Silu, bias=pv[:, PV_LNB + c:PV_LNB + c + 1],
                      scale=pv[:, PV_LNG + c:PV_LNG + c + 1])
            for i in range(2):
                W2 = W.get("pw2_%d" % i)
                for mm_ in range(4):
                    m = 4 * i + mm_
                    ps = ps_mm()
                    fm_mm(ps[:, :N], W2, mm_ * 128, 128, N)
                    resid_add(m, ps, N, bias=pv[:, PV_BPW2 + m:PV_BPW2 + m + 1])
            if last and not sample:
                fm_to_dram([u32[:, c, 0:30] for c in range(8)], 30, o_cv)
            if sample:
                B.dma('pool', lambda e: e.dma_start(out=o_cvs[:, 0:29, :], in_=st_cv[:, 1:30, :]), [], [o_cvs], is_out=True)
                fm_to_dram([u32[:, c, 0:NB] for c in range(8)], NB, o_cvs[:, 29, :])

        dbuf = q32 + frt

        def hgrn_common(h, N, o_ps):
            B.act(tA[:, :N], o_ps[:, :N], AF.Square)
            B.copy(PT[0][:, :N], tA[:, :N], eng='pool')
            ms = ps_mm()
            B.mm(ms[:, :N], ones128_b, PT[0][:, :N])
            B.act(tB[:, :N], ms[:, :N], AF.Ln, bias=epsc)
            B.act(tB[:, :N], tB[:, :N], AF.Exp, scale=-0.5)
            B.stt(tA[:, :N], o_ps[:, :N], pv[:, PV_GN + h:PV_GN + h + 1], tB[:, :N], ALU.mult, ALU.mult)
            B.tt(cat_hg[h][:, :N], tA[:, :N], gate[h][:, :N], ALU.mult)

        def mixer_prompt(t0, N, mode='own', last=False):
            NT = N // 128
            NC = N // 64
            kb0 = t0 // 128
            part = (mode == 'partial')
            wr_out = (mode == 'own')
            to = t0 - (OWN0 if nch == 8 else 0)
            rmsnorm(PV_NM, N)
            nkb = kb0 + NT
            B.copy(cref[:, :], carry[:, :], eng='dve')
            for j in range(NT):
                ps = ps_mm()
                for kc in range(8):
                    B.mm(ps[:, 0:8], hT[kc][:, j * 128:(j + 1) * 128], wff[:, kc, :], start=(kc == 0), stop=(kc == 7))
                B.tt(flf[j][:, :], ps[:, 0:8], pv[:, PV_FB:PV_FB + 8], ALU.add)
                B.act(flf[j][:, :], flf[j][:, :], AF.Sigmoid)
                B.act(flf[j][:, :], flf[j][:, :], AF.Ln)
                if wr_out:
                    B.dma('pool', lambda e, j=j: e.dma_start(out=o_fl[to + j * 128:to + (j + 1) * 128, :], in_=flf[j][:, :]),
                          [flf[j]], [o_fl], is_out=True)
            if not part:
                s0 = W.get("in0")
                for h in range(4):
                    ps = ps_mm()
                    fm_mm(ps[:, :N], s0, h * 128, 128, N)
                    B.act(q32[h][:, :N], ps[:, :N], AF.Silu)
            for j in range(NT):
                ps2 = ps_mm()
                B.mm(ps2[:, 0:8], tri_le, flf[j][:, :])
                B.mm(ps2[:, 8:16], onesf, flf[j][:, :])
                ct = ctm[j % 2]
                B.tt(ct[:, :], ps2[:, 0:8], carry[:, :], ALU.add)
                kb = kb0 + j
                B.ts(negc[:, kb, :], ct[:, :], -1.0, None, ALU.mult)
                B.tt(ct[:, :], ct[:, :], cref[:, :], ALU.subtract)
                B.ts(Zr[j][:, :, 64], ct[:, :], 8.0, None, ALU.mult)
                B.tt(carry[:, :], carry[:, :], ps2[:, 8:16], ALU.add)
            if not part:
                B.tt(biasc[:, 0:nkb, :], negc[:, 0:nkb, :], cref[:, :].unsqueeze(1).to_broadcast([128, nkb, 8]), ALU.add)
                if nch == 8:
                    mrow = 0 if mode == 'halo' else 32
                    B.tt(biasc[:, 0:nkb, :], biasc[:, 0:nkb, :],
                         cflag[:, mrow:mrow + nkb].unsqueeze(2).to_broadcast([128, nkb, 8]), ALU.add)
            s1 = W.get("in1")
            for h in range(4):
                ps = ps_mm()
                fm_mm(ps[:, :N], s1, h * 128, 128, N)
                B.act(tA[:, :N], ps[:, :N], AF.Sigmoid)
                B.ts(frt[h][:, :N], tA[:, :N], lbt[:, 4 + h:5 + h], lbt[:, h:h + 1], ALU.mult, ALU.add)

            def prep(h):
                st = h % 2
                nQt, nKt, Qb, csm, nKtm = nQt_s[st], nKt_s[st], Qb_s[st], csm_s[st], nKtm_s[st]
                lf = tA
                B.act(lf[:, :N], frt[h][:, :N], AF.Ln)
                Cs = tB
                B.op('dve', lambda e: e.tensor_tensor_scan(out=Cs[:, :N], data0=onesf[:, 0:1].to_broadcast([128, N]), data1=lf[:, :N],
                                                           initial=0.0, op0=ALU.mult, op1=ALU.add), [cst, lf], [Cs])
                C3 = Cs[:, :N].rearrange("p (n t) -> p n t", t=64)
                D1 = tC
                D3 = D1[:, :N].rearrange("p (n t) -> p n t", t=64)
                B.tt(D3, C3, C3[:, :, 32:33].to_broadcast([128, NC, 64]), ALU.subtract)
                B.memset(csm[:, 0:1], 0.0)
                if NC > 1:
                    B.copy(csm[:, 1:NC], C3[:, 0:NC - 1, 63], eng='dve')
                B.tt(csm[:, 8:8 + NC], C3[:, :, 32], csm[:, 0:NC], ALU.subtract)
                B.tt(csm[:, 16:16 + NC], C3[:, :, 63], csm[:, 0:NC], ALU.subtract)
                B.tt(csm[:, 48:48 + NC], csm[:, 16:16 + NC], csm[:, 8:8 + NC], ALU.subtract)
                E1 = tA
                B.act(E1[:, :N], D1[:, :N], AF.Exp)
                B.act(csm[:, 24:24 + NC], csm[:, 8:8 + NC], AF.Exp)
                B.act(csm[:, 32:32 + NC], csm[:, 16:16 + NC], AF.Exp)
                B.act(csm[:, 40:40 + NC], csm[:, 48:48 + NC], AF.Exp)
                E3 = tB
                B.act(E3[:, :N], D1[:, :N], AF.Exp, scale=-1.0)
                if not part:
                    B.stt(nQt[:, :N], q32[h][:, :N], -1.0, E1[:, :N], ALU.mult, ALU.mult)
                B.ts(csm[:, 24:24 + NC], csm[:, 24:24 + NC], -1.0, None, ALU.mult)
                B.ts(csm[:, 40:40 + NC], csm[:, 40:40 + NC], -1.0, None, ALU.mult)
                B.stt(nKt[:, :N], frt[h][:, :N], 1.0, E3[:, :N], ALU.subtract, ALU.mult)
                if not part:
                    B.tt(Qb[:, :N].rearrange("p (n t) -> p n t", t=64), nQt[:, :N].rearrange("p (n t) -> p n t", t=64),
                         csm[:, 24:24 + NC].unsqueeze(2).to_broadcast([128, NC, 64]), ALU.mult)
                pst = ps_mm()
                pstb = pst[:, :].bitcast(BF16)
                for j in range(NT):
                    B.tr(pstb[:, j * 128:(j + 1) * 128], nKt[:, j * 128:(j + 1) * 128], ident_b)
                for j in range(NT):
                    B.copy(nKtm[j][:, :], pstb[:, j * 128:(j + 1) * 128])

            prep(0)
            s2 = W.get("in2")
            for j in range(NT):
                ps = ps_mm()
                tm_mm(ps[:, :], s2, 0, 512, j * 128, 128)
                B.copy(vhg[j][:, :], ps[:, :])
            prep(1)
            if not part:
                s3 = W.get("in3")
                for h in range(4):
                    ps = ps_mm()
                    fm_mm(ps[:, :N], s3, h * 128, 128, N)
                    B.act(gate[h][:, :N], ps[:, :N], AF.Silu)
                s4 = W.get("in4")
                for h in range(8):
                    ps = ps_mm()
                    for j in range(NT):
                        B.mm(ps[0:65, j * 128:(j + 1) * 128], Zr[j][:, h, :], ident_b, start=True, stop=False)
                    for kc in range(8):
                        B.mm(ps[0:64, :N], s4[:, kc, h * 64:(h + 1) * 64], hT[kc][:, :N], start=False, stop=(kc == 7))
                    B.copy(Qp[h][:, :N], ps[0:65, :N])
            s5 = W.get("in5")
            for h in range(8):
                ps = ps_mm()
                fm_mm(ps[0:64, :N], s5, h * 64, 64, N)
                ks = kst[h % 2]
                B.copy(ks[:, :N], ps[0:64, :N])
                B.dma('pool', lambda e, ks=ks, h=h: e.dma_start(out=s_k[h, :, t0:t0 + N], in_=ks[:, :N]), [ks], [s_k])
            for j in range(NT if wr_out else 0):
                ps = ps_mm()
                tm_mm(ps[:, :], s5, 0, 512, j * 128, 128)
                o = ost_next()
                B.copy(o[:, :], ps[:, :])
                B.dma('pool', lambda e, o=o, j=j: e.dma_start(out=o_fk[to + j * 128:to + (j + 1) * 128, :], in_=o[:, :]),
                      [o], [o_fk], is_out=True)
            s6 = W.get("in6")
            for j in range(NT):
                ps = ps_mm()
                tm_mm(ps[:, :], s6, 0, 512, j * 128, 128)
                if wr_out:
                    o = ost_next()
                    B.copy(o[:, :], ps[:, :])
                    B.dma('pool', lambda e, o=o, j=j: e.dma_start(out=o_fv[to + j * 128:to + (j + 1) * 128, :], in_=o[:, :]),
                          [o], [o_fv], is_out=True)
                vs = vst[j % 2]
                B.copy(vs[:, :], ps[:, :], eng='dve')
                kb = kb0 + j
                B.dma('pool', lambda e, vs=vs, kb=kb: e.dma_start(
                    out=s_v[:, :, kb, :].rearrange("h p d -> p h d"), in_=vs[:, :].rearrange("p (h d) -> p h d", d=64)),
                    [vs], [s_v])
            for h in range(4):
                st = h % 2
                nQt, nKt, Qb, csm, nKtm = nQt_s[st], nKt_s[st], Qb_s[st], csm_s[st], nKtm_s[st]
                o_ps = PS[6 + h % 2]
                for j in range(NT):
                    if not part:
                        aps = ps_mm()
                        B.mm(aps[:, 0:128], nKt[:, j * 128:(j + 1) * 128], nQt[:, j * 128:(j + 1) * 128])
                        am = attm[j % 2]
                        B.tt(am[:, :], aps[:, 0:128], mask2f, ALU.mult)
                        B.mm(o_ps[:, j * 128:(j + 1) * 128], vhg[j][:, h * 128:(h + 1) * 128], am[:, :], start=True, stop=False)
                    for hf in range(2):
                        n = 2 * j + hf
                        if not part:
                            B.mm(o_ps[:, n * 64:(n + 1) * 64], Sbf[h][:, :], Qb[:, n * 64:(n + 1) * 64], start=False, stop=True)
                        kps = ps_mm()
                        B.mm(kps[:, 0:128], nKtm[j][hf * 64:(hf + 1) * 64, :], vhg[j][hf * 64:(hf + 1) * 64, h * 128:(h + 1) * 128])
                        kt = kvt[n % 2]
                        B.ts(kt[:, :], kps[:, 0:128], csm[:, 40 + n:41 + n], None, ALU.mult)
                        B.stt(S32[h][:, :], S32[h][:, :], csm[:, 32 + n:33 + n], kt[:, :], ALU.mult, ALU.add)
                        B.copy(Sbf[h][:, :], S32[h][:, :], eng='pool')
                if not part:
                    hgrn_common(h, N, o_ps)
                if last:
                    B.dma('pool', lambda e, h=h: e.dma_start(out=o_hg[h], in_=S32[h][:, :]), [S32[h]], [o_hg], is_out=True)
                if h + 2 < 4:
                    prep(h + 2)
            if part:
                return
            for h in range(8):
                Kb = Kbuf[0]
                Vb = Vbuf[0]
                B.dma('sp', lambda e, Kb=Kb, h=h: e.dma_start(out=Kb[0:64, 0:nkb * 128], in_=s_k[h, :, 0:nkb * 128]), [s_k], [Kb])
                B.dma('sp', lambda e, Vb=Vb, h=h: e.dma_start(out=Vb[:, :].rearrange("p (k d) -> p k d", d=65)[:, 0:nkb, 0:64],
                                                          in_=s_v[h, :, 0:nkb, :]), [s_v], [Vb])
                O_ps = PS[6 + h % 2]
                for kb in range(nkb):
                    jd = kb - kb0
                    c0 = max(0, jd) * 128
                    S_ps = PS[4 + kb % 2]
                    B.mm(S_ps[:, c0:N], Kb[0:65, kb * 128:(kb + 1) * 128], Qp[h][0:65, c0:N])
                    P = PT[kb % 2]
                    B.act(P[:, c0:N], S_ps[:, c0:N], AF.Exp, bias=biasc[:, kb, h:h + 1], scale=0.125)
                    if jd >= 0:
                        B.tt(P[:, c0:c0 + 128], P[:, c0:c0 + 128], tri_le_b, ALU.mult, eng='pool')
                    B.mm(O_ps[0:65, c0:N], Vb[:, kb * 65:(kb + 1) * 65], P[:, c0:N], start=(kb == 0), stop=(kb == nkb - 1))
                B.copy(tA[0:65, :N], O_ps[0:65, :N])
                dps = ps_mm()
                B.mm(dps[0:64, :N], sel65, tA[0:65, :N])
                B.op('dve', lambda e, dps=dps: e.reciprocal(out=rden[0:64, :N], in_=dps[0:64, :N]), [dps], [rden])
                B.tt(cat_fx[h][0:64, :N], tA[0:64, :N], rden[0:64, :N], ALU.mult)
            wa = W.get("wo_a")
            wb = [W.get("wo_b0"), W.get("wo_b1")]
            for m in range(8):
                ps = ps_mm()
                for h in range(4):
                    B.mm(ps[:, :N], wa[:, h, m * 128:(m + 1) * 128], cat_hg[h][:, :N], start=(h == 0), stop=False)
                for h in range(8):
                    B.mm(ps[:, :N], wb[m // 4][0:64, h, (m % 4) * 128:(m % 4 + 1) * 128], cat_fx[h][0:64, :N], start=False,
                         stop=(h == 7))
                resid_add(m, ps, N)

        def mixer_sample():
            N = NB
            rmsnorm(PV_NM, N)
            sl = [W.get("in%d" % i) for i in range(3)]
            flfs = B.sb([NB, 8], F32, "flfs")
            ps = ps_mm()
            for kc in range(8):
                B.mm(ps[0:N, 0:8], hT[kc][:, 0:N], wff[:, kc, :], start=(kc == 0), stop=(kc == 7))
            B.tt(flfs[:, :], ps[0:N, 0:8], pv[0:N, PV_FB:PV_FB + 8], ALU.add)
            B.act(flfs[:, :], flfs[:, :], AF.Sigmoid)
            B.act(flfs[:, :], flfs[:, :], AF.Ln)
            B.dma('pool', lambda e: e.dma_start(out=o_fls, in_=flfs[:, :]), [flfs], [o_fls], is_out=True)
            for h in range(4):
                ps = ps_mm()
                fm_mm(ps[:, :N], sl[0], h * 128, 128, N)
                B.act(q32[h][:, :N], ps[:, :N], AF.Silu)
            for h in range(4):
                ps = ps_mm()
                fm_mm(ps[:, :N], sl[1], h * 128, 128, N)
                B.act(tA[:, :N], ps[:, :N], AF.Sigmoid)
                B.ts(frt[h][:, :N], tA[:, :N], lbt[:, 4 + h:5 + h], lbt[:, h:h + 1], ALU.mult, ALU.add)
            K32 = Kbuf[0][:, :].bitcast(F32)
            vs_tm = K32[0:NB, 0:512]
            ps = ps_mm()
            tm_mm(ps[0:N, :], sl[2], 0, 512, 0, N)
            B.copy(vs_tm, ps[0:N, :])
            s3 = W.get("in3")
            for h in range(4):
                ps = ps_mm()
                fm_mm(ps[:, :N], s3, h * 128, 128, N)
                B.act(gate[h][:, :N], ps[:, :N], AF.Silu)
            qs_tm = K32[0:NB, 512:1024]
            ks_tm = K32[0:NB, 1024:1536]
            vf_tm = K32[0:NB, 1536:2048]
            for i, (dstt, odr) in enumerate(((qs_tm, None), (ks_tm, o_fks), (vf_tm, o_fvs))):
                s = W.get("in%d" % (4 + i))
                ps = ps_mm()
                tm_mm(ps[0:N, :], s, 0, 512, 0, N)
                B.copy(dstt, ps[0:N, :])
                if odr is not None:
                    B.dma('pool', lambda e, dstt=dstt, odr=odr: e.dma_start(out=odr, in_=dstt), [dstt], [odr], is_out=True)
            o_ps = [PS[6], PS[7]]
            for b in range(NB):
                st = ost_next()
                B.dma('sp', lambda e, st=st, b=b: e.dma_start(out=st[:, :].rearrange("p (h v) -> p h v", v=128),
                                                          in_=st_hg[b].rearrange("h k v -> k h v")), [], [st])
                vb = ps_mm()
                B.mm(vb[:, :], ident[0:N, b:b + 1].to_broadcast([N, 128]), vs_tm)
                for h in range(4):
                    B.ts(tC[:, h * 128:(h + 1) * 128], vb[:, h * 128:(h + 1) * 128], frt[h][:, b:b + 1], -1.0, ALU.mult, ALU.mult)
                    B.tt(tC[:, h * 128:(h + 1) * 128], tC[:, h * 128:(h + 1) * 128], vb[:, h * 128:(h + 1) * 128], ALU.add)
                    B.stt(st[:, h * 128:(h + 1) * 128], st[:, h * 128:(h + 1) * 128], frt[h][:, b:b + 1],
                          tC[:, h * 128:(h + 1) * 128], ALU.mult, ALU.add)
                    B.mm(o_ps[h // 2][:, (h % 2) * 16 + b:(h % 2) * 16 + b + 1], st[:, h * 128:(h + 1) * 128], q32[h][:, b:b + 1])
                B.dma('pool', lambda e, st=st, b=b: e.dma_start(out=o_hgs[b].rearrange("h k v -> k h v"),
                                                            in_=st[:, :].rearrange("p (h v) -> p h v", v=128)),
                      [st], [o_hgs], is_out=True)
            for h in range(4):
                hgrn_common(h, N, o_ps[h // 2][:, (h % 2) * 16:(h % 2) * 16 + 16])
            ptb_i = rstd[:, 0:NB * 16].bitcast(I32)
            ptb_f = mean_t[:, 0:NB * 16]
            B.dma('sp', lambda e: e.dma_start(out=ptb_i, in_=ptab), [], [ptb_i])
            B.copy(ptb_f, ptb_i, eng='dve')
            B.ts(ptb_f, ptb_f, 128.0, iota_p, ALU.mult, ALU.add)
            B.copy(ptb_i, ptb_f, eng='dve')
            pn = B.sb([NB, 8], F32, "pn")
            pnb = [B.sb([NB, 8], F32, "pnb%d" % i) for i in range(2)]
            prod16 = ost[0][0:NB, :]
            B.tt(prod16, qs_tm, ks_tm, ALU.mult)
            B.op('dve', lambda e: e.tensor_reduce(out=pn[:, :], in_=prod16.rearrange("p (h d) -> p h d", d=64),
                                                  axis=AX.X, op=ALU.add), [prod16], [pn])
            B.act(pn[:, :], pn[:, :], AF.Exp, scale=0.125)
            lfp = [kvt[0][:, :], kvt[1][:, :]]
            bia = [tA[:, 256:384], tA[:, 384:512]]
            sfxb = [tC[:, 0:128], tC[:, 128:256]]
            sc = [tB[:, 0:128], tB[:, 128:256]]
            pp = [tB[:, 256:384], tB[:, 384:512]]
            kpg = [q32[0], q32[1], q32[2]]
            vpg = [frt[0], frt[1], frt[2]]
            V32 = Vbuf[0][:, 0:2048].bitcast(F32)
            Rn = [V32[0:8, 0:512], V32[0:8, 512:1024]]
            rd = B.sb([8, 2], F32, "rd")
            ofx = PS[5]
            bmask_t = ost[1][0:8, :]
            B.op('dve', lambda e: e.tensor_copy(out=bmask_t.rearrange("p (h d) -> p h d", d=64),
                                                in_=cst[0:8, 0:8].unsqueeze(2).to_broadcast([8, 8, 64])), [cst], [bmask_t])
            for b in range(NB):
                lf = lfp[b % 2]
                for pg in range(16):
                    col = b * 16 + pg
                    B.dma('pool', lambda e, lf=lf, pg=pg, col=col: e.indirect_dma_start(
                        out=lf[:, pg * 8:(pg + 1) * 8], out_offset=None, in_=cl,
                        in_offset=bass.IndirectOffsetOnAxis(ap=ptb_i[:, col:col + 1], axis=0)), [ptb_i], ["lfk%d_%d" % (b % 2, pg)])
                ps = ps_mm()
                lfkeys = ["lfk%d_%d" % (b % 2, pg) for pg in range(16)]
                B.mm(ps[:, 0:128], tri_gt, lf, extra_r=lfkeys)
                B.mm(ps[:, 128:256], onesf, lf, extra_r=lfkeys)
                B.mm(ps[:, 256:264], ident[0:N, b:b + 1].to_broadcast([N, 128]), flfs[:, :])
                prev = sfxb[0]
                B.copy(prev, ps[:, 128:256], eng='dve')
                for li, sh in enumerate((1, 2, 4, 8)):
                    cur = sfxb[(li + 1) % 2]
                    n_ok = (16 - sh) * 8
                    B.tt(cur[:, 0:n_ok], prev[:, 0:n_ok], prev[:, sh * 8:128], ALU.add)
                    B.copy(cur[:, n_ok:128], prev[:, n_ok:128], eng='dve')
                    prev = cur
                bi = bia[b % 2]
                B.tt(bi[:, 0:120], ps[:, 0:120], prev[:, 8:128], ALU.add)
                B.copy(bi[:, 120:128], ps[:, 120:128], eng='dve')
                B.tt(bi.rearrange("p (g h) -> p g h", h=8), bi.rearrange("p (g h) -> p g h", h=8),
                     ps[:, 256:264].unsqueeze(1).to_broadcast([128, 16, 8]), ALU.add)
                qb = ps_mm()
                B.mm(qb[:, :], ident[0:N, b:b + 1].to_broadcast([N, 128]), qs_tm)
                s_t = sc[b % 2]
                for pg in range(16):
                    col = b * 16 + pg
                    kp = kpg[pg % 3]
                    B.dma('pool', lambda e, kp=kp, col=col: e.indirect_dma_start(
                        out=kp[:, :], out_offset=None, in_=ck,
                        in_offset=bass.IndirectOffsetOnAxis(ap=ptb_i[:, col:col + 1], axis=0)), [ptb_i], [kp])
                    B.tt(kp[:, :], kp[:, :], qb[:, :], ALU.mult)
                    B.op('dve', lambda e, kp=kp, pg=pg, s_t=s_t: e.tensor_reduce(
                        out=s_t[:, pg * 8:(pg + 1) * 8], in_=kp[:, :].rearrange("p (h d) -> p h d", d=64), axis=AX.X,
                        op=ALU.add), [kp], [s_t])
                p_t = pp[b % 2]
                B.stt(s_t, s_t, 0.125, bi, ALU.mult, ALU.add)
                B.act(p_t, s_t, AF.Exp)
                pb = pnb[b % 2]
                B.ts(pb[:, :], pn[:, :], ident[0:N, b:b + 1], None, ALU.mult)
                R_ps = PS[6 + b % 2]
                d_ps = PS[4]
                for pg in range(16):
                    col = b * 16 + pg
                    vp = vpg[pg % 3]
                    B.dma('pool', lambda e, vp=vp, col=col: e.indirect_dma_start(
                        out=vp[:, :], out_offset=None, in_=cv,
                        in_offset=bass.IndirectOffsetOnAxis(ap=ptb_i[:, col:col + 1], axis=0)), [ptb_i], [vp])
                    B.mm(R_ps[0:8, :], p_t[:, pg * 8:(pg + 1) * 8], vp[:, :], start=(pg == 0), stop=False)
                B.mm(R_ps[0:8, :], pb[:, :], vf_tm, start=False, stop=True)
                for pg in range(16):
                    B.mm(d_ps[0:8, 0:1], p_t[:, pg * 8:(pg + 1) * 8], onesf[:, 0:1], start=(pg == 0), stop=False)
                B.mm(d_ps[0:8, 0:1], pb[:, :], onesf[0:N, 0:1], start=False, stop=True)
                B.op('dve', lambda e: e.reciprocal(out=rd[:, 0:1], in_=d_ps[0:8, 0:1]), [d_ps], [rd])
                rn = Rn[b % 2]
                B.stt(rn, R_ps[0:8, :], rd[:, 0:1], bmask_t, ALU.mult, ALU.mult)
                for pr in range(4):
                    B.mm(ofx[:, pr * 16 + b:pr * 16 + b + 1], rn[:, pr * 128:(pr + 1) * 128], onesf[0:8, 0:1])
            catfs = B.sb([128, 4, NB], BF16, "catfs")
            B.copy(catfs[:, :, :], ofx[:, 0:64].rearrange("p (a b) -> p a b", b=NB))
            wa = W.get("wo_a")
            wc = W.get("wo_c")
            for m in range(8):
                ps = ps_mm()
                for h in range(4):
                    B.mm(ps[:, :N], wa[:, h, m * 128:(m + 1) * 128], cat_hg[h][:, :N], start=(h == 0), stop=False)
                for h in range(4):
                    B.mm(ps[:, :N], wc[:, h, m * 128:(m + 1) * 128], catfs[:, h, :], start=False, stop=(h == 3))
                resid_add(m, ps, N)

        for ci, (t0, N_, mode) in enumerate(SCHED):
            last = (ci == len(SCHED) - 1)
            cur["ci"] = ci
            load_x(xp[t0:t0 + N_, :], N_)
            dump(0)
            mixer_prompt(t0, N_, mode, last=last)
            if mode == 'partial':
                continue
            dump(1)
            dump(5, cat_hg)
            dump(6, cat_fx)
            ffn(0, N_, last=last)
            dump(2)
            conformer(N_, last=last, halo=(mode == 'halo'))
            dump(3)
            ffn(1, N_, last=last, halo=(mode == 'halo'))
            dump(4)
            if mode == 'halo':
                continue
            rmsnorm(PV_NFIN, N_, out_f32=True)
            to = t0 - (OWN0 if nch == 8 else 0)
            store_y(o_y[to:to + N_, :], N_)
        if do_sample:
            for l in range(2):
                t = xtm[l % 2]
                for half in range(3):
                    c0 = half * 1024
                    c1 = min(FFN, c0 + 1024)
                    B.dma('sp', lambda e, t=t, l=l, c0=c0, c1=c1: e.dma_start(
                        out=t[0:32, 0:c1 - c0], in_=st_ff[l].rearrange("b r f -> (b r) f")[:, c0:c1]), [], [t])
                    for j in range(c0 // 128, c1 // 128):
                        ps = ps_mm()
                        B.tr(ps[:, 0:32], t[0:32, j * 128 - c0:(j + 1) * 128 - c0], ident[0:32, 0:32])
                        B.copy(hfs_ap(l, j), ps[:, 0:32])
            load_x(xs, NB)
            mixer_sample()
            ffn(0, NB, sample=True)
            conformer(NB, sample=True)
            ffn(1, NB, sample=True)
            rmsnorm(PV_NFIN, NB, out_f32=True)
            store_y(o_ys, NB)
        B.finish()
    return nc


_NC_CACHE = {}


def _host_consts():
    cst = np.zeros((128, 1024), np.float32)
    r = np.arange(128)
    cst[:, 0:128] = np.eye(128)
    cst[:, 128:256] = 1.0
    cst[:, 256:384] = (r[:, None] <= r[None, :])
    cst[:, 384:512] = (r[:, None] <= r[None, :]) & ((r[:, None] // 64) == (r[None, :] // 64))
    cst[:, 512:640] = (r[:, None] > r[None, :])
    cst[:, 640] = r
    cst[:, 641] = EPS
    sel = np.zeros((128, 4), np.float32)
    for i in range(120):
        sel[i, i // 30] = 1.0
    cst[:, 648:652] = sel
    cst[64, 656:720] = 1.0
    return cst


def _fm(v, nchunk):
    return np.ascontiguousarray(np.asarray(v, np.float32).reshape(nchunk, 128).T)


def kernel(x_prompt, x_sample, cache_fox_k, cache_fox_v, cache_fox_logf, page_table,
           state_hgrn, state_conv, state_ffn_conv, norm_mix, norm_ffn, norm_final,
           w_in0, fox_fb, hg_lb, hg_gnorm, w_out0, w_pw1, b_pw1, w_dw, b_dw, ln_g, ln_b,
           w_pw2, b_pw2, w_ffn_in, w_ffn_dw, b_ffn_dw, w_ffn_out):
    f = lambda a: np.ascontiguousarray(np.asarray(a, np.float32))
    if 'nc' not in _NC_CACHE:
        _NC_CACHE['nc'] = build_program()
    nc = _NC_CACHE['nc']
    pvec = np.zeros((128, 640), np.float32)
    pvec[:, 0:16] = np.concatenate([_fm(norm_mix[0], 8), _fm(norm_mix[1], 8)], 1)
    pvec[:, 16:32] = np.concatenate([_fm(norm_ffn[0], 8), _fm(norm_ffn[1], 8)], 1)
    pvec[:, 32:40] = _fm(norm_final, 8)
    pvec[:, 40:48] = np.concatenate([_fm(hg_lb[0], 4), _fm(hg_lb[1], 4)], 1)
    pvec[:, 48:52] = _fm(hg_gnorm[0], 4)
    pvec[:, 52:68] = _fm(b_pw1[0], 16)
    pvec[:, 68:76] = _fm(b_dw[0], 8)
    pvec[:, 76:84] = _fm(ln_g[0], 8)
    pvec[:, 84:92] = _fm(ln_b[0], 8)
    pvec[:, 92:100] = _fm(b_pw2[0], 8)
    pvec[:, 100:144] = np.concatenate([_fm(b_ffn_dw[0], 22), _fm(b_ffn_dw[1], 22)], 1)
    wffd = np.asarray(w_ffn_dw, np.float32).reshape(2, 3, 22, 128)
    pvec[:, 144:276] = np.ascontiguousarray(wffd.transpose(3, 0, 2, 1)).reshape(128, 132)
    wd = np.asarray(w_dw, np.float32)[0].reshape(31, 8, 128)
    pvec[:, 276:524] = np.ascontiguousarray(wd.transpose(2, 1, 0)).reshape(128, 248)
    pvec[:, 524:532] = np.broadcast_to(np.asarray(fox_fb, np.float32)[0][None, :], (128, 8))
    cst = _host_consts()
    wdw_rep = np.ascontiguousarray(np.tile(np.asarray(w_dw, np.float32)[0, 0:30], (4, 1)))
    ckf = f(cache_fox_k)[0].reshape(NPOOL * 128, 512)
    cvf = f(cache_fox_v)[0].reshape(NPOOL * 128, 512)
    clf = f(cache_fox_logf)[0].reshape(NPOOL * 128, 8)
    pt = np.asarray(page_table, np.int32)
    shared = {"ck": ckf, "cv": cvf, "cl": clf, "pvec": pvec, "cst": cst, "wdw_rep": wdw_rep,
              "w_in0": f(w_in0)[0], "w_out0": f(w_out0)[0], "w_pw1": f(w_pw1)[0], "w_pw2": f(w_pw2)[0],
              "w_ffi": f(w_ffn_in), "w_ffo": f(w_ffn_out)}
    xpf = f(x_prompt)
    xsf = f(x_sample)[:, 0, :]
    sth = f(state_hgrn)[0]
    stc = f(state_conv)[0]
    stf = f(state_ffn_conv)
    in_maps = []
    for c in range(NCORE):
        sl = slice(NB * c, NB * (c + 1))
        m = dict(shared)
        bb, half = c // 2, c % 2
        cf = np.zeros((128, 72), np.float32)
        if half == 0:
            xin = np.zeros((SEQ, D), np.float32)
            xin[SEQ // 2:] = xpf[bb, :SEQ // 2]
            cf[:, 0:15] = -30000.0
            cf[:, 32:48] = -30000.0
        else:
            xin = xpf[bb]
            cf[:, 64] = 1.0
        m["xp"] = xin
        m["cflag"] = cf
        m["xs"] = np.ascontiguousarray(xsf[sl])
        m["ptab"] = np.ascontiguousarray(np.broadcast_to(pt[sl].reshape(1, NB * 16), (128, NB * 16)))
        m["st_hg"] = np.ascontiguousarray(sth[sl])
        m["st_cv"] = np.ascontiguousarray(stc[sl])
        m["st_ff"] = np.ascontiguousarray(stf[:, sl])
        in_maps.append(m)
    res = run_bass_kernel_spmd(nc, in_maps, core_ids=list(range(NCORE)))
    R = res.results
    cat = lambda k, ax=0: np.concatenate([R[c][k] for c in range(NCORE)], axis=ax)
    stk4 = lambda k: np.stack([np.concatenate([R[2 * b][k], R[2 * b + 1][k]], axis=0) for b in range(4)], axis=0)
    odd4 = lambda k: np.stack([R[2 * b + 1][k] for b in range(4)], axis=0)
    y_prompt = stk4("o_y")
    y_sample = cat("o_ys")[:, None, :]
    fk_p = stk4("o_fk").reshape(1, 4, SEQ, 8, 64)
    fv_p = stk4("o_fv").reshape(1, 4, SEQ, 8, 64)
    fl_p = stk4("o_fl").reshape(1, 4, SEQ, 8)
    fk_s = cat("o_fks").reshape(1, 128, 1, 8, 64)
    fv_s = cat("o_fvs").reshape(1, 128, 1, 8, 64)
    fl_s = cat("o_fls").reshape(1, 128, 1, 8)
    hg_p = odd4("o_hg")[None]
    hg_s = cat("o_hgs")[None]
    cv_p = odd4("o_cv")[None]
    cv_s = cat("o_cvs")[None]
    ff_p = np.stack([R[2 * b + 1]["o_ff"] for b in range(4)], axis=1)
    ff_s = cat("o_ffs", 1)
    outs = (y_prompt, y_sample, fk_p, fv_p, fl_p, fk_s, fv_s, fl_s, hg_p, hg_s, cv_p, cv_s, ff_p, ff_s)
    return tuple(np.ascontiguousarray(o, dtype=np.float32) for o in outs)
```
